# Optimizing a Trainium2 kernel written in Bass

```python
import math
import jax
import jax.numpy as jnp
from jax import lax
import numpy as np

D_MODEL = 1024
BATCH = 16
SEQ = 256
DEPTH = 4
DEC_BATCH = 8
DEC_SEQ = 1024
PAST_LEN = 512

GRID_W = 64
CHUNK = 64
EPS = 1e-6
GLA_H = 4
GLA_DK = 64
GLA_DV = 128
GLA_QK = GLA_H * GLA_DK
GLA_W = GLA_H * GLA_DV
GLA_LR = 16
GLA_TAU = 16.0
S5_GH = 16
S5_W = 512
S5_G = S5_W // S5_GH
S5_P = 64
GDN_H = 4
GDN_DK = 128
GDN_DV = 128
GDN_W = GDN_H * GDN_DV
CONV_K = 3
MIX_W = GLA_W + S5_W + GDN_W
SPLITS = (GLA_QK, GLA_QK, GLA_W, 2 * GLA_LR, GLA_W, S5_W, S5_W, 3 * GDN_W, 2 * GDN_H, 2 * GDN_H, GDN_W)
IN_DIM = 2 * GLA_QK + 2 * GLA_W + 2 * GLA_LR + 2 * S5_W + 4 * GDN_W + 4 * GDN_H

kernel_name = 'hybrid_gla_s5_deltanet_diffusion_step'


def rmsnorm(x, g):
    xf = x.astype(jnp.float32)
    return xf * lax.rsqrt(jnp.mean(xf * xf, axis=-1, keepdims=True) + EPS) * g.astype(jnp.float32)


def l2norm(x):
    return x * lax.rsqrt(jnp.sum(x * x, axis=-1, keepdims=True) + EPS)


def to_heads(t, n):
    b, l, _ = t.shape
    return t.reshape(b, l, n, -1).transpose(0, 2, 1, 3)


def from_heads(t):
    b, n, l, d = t.shape
    return t.transpose(0, 2, 1, 3).reshape(b, l, n * d)


def flip_seq(t):
    return jnp.flip(t, axis=2)


def to_chunks(t):
    b, h, l = t.shape[:3]
    return jnp.moveaxis(t.reshape(b, h, l // CHUNK, CHUNK, *t.shape[3:]), 2, 0)


def from_chunks(t):
    n, b, h, c, d = t.shape
    return jnp.moveaxis(t, 0, 2).reshape(b, h, n * c, d)


def grid_to_cols(t):
    b, l, ch = t.shape
    return t.reshape(b, l // GRID_W, GRID_W, ch).transpose(0, 2, 1, 3).reshape(b, l, ch)


def cols_to_grid(t):
    b, l, ch = t.shape
    return t.reshape(b, GRID_W, l // GRID_W, ch).transpose(0, 2, 1, 3).reshape(b, l, ch)


def centred_conv(x, w):
    pad = CONV_K // 2
    return lax.conv_general_dilated(x, w[:, None, :].astype(x.dtype), (1,), [(pad, pad)],
                                    dimension_numbers=('NWC', 'WIO', 'NWC'),
                                    feature_group_count=x.shape[-1])


def gla_chunked(q, k, v, g, s0):
    q, k, v, g = to_chunks(q), to_chunks(k), to_chunks(v), to_chunks(g)
    b = jnp.cumsum(g, axis=-2)
    qe = q * jnp.exp(b)
    ke = k * jnp.exp(-b)
    lower = jnp.tril(jnp.ones((CHUNK, CHUNK), dtype=bool))
    scores = jnp.where(lower, jnp.einsum('nbhid,nbhjd->nbhij', qe, ke), 0.0)
    o_intra = jnp.einsum('nbhij,nbhjv->nbhiv', scores, v)
    b_last = b[..., -1:, :]
    k_dec = k * jnp.exp(b_last - b)
    decay_last = jnp.exp(b_last[..., 0, :])

    def step(s, inp):
        qe_n, kd_n, v_n, dl_n, oi_n = inp
        o = oi_n + jnp.einsum('bhid,bhdv->bhiv', qe_n, s)
        s = dl_n[..., None] * s + jnp.einsum('bhid,bhiv->bhdv', kd_n, v_n)
        return s, o

    s_fin, o = lax.scan(step, s0, (qe, k_dec, v, decay_last, o_intra))
    return from_chunks(o), s_fin


def gdn_chunked(q, k, v, g, beta, s0):
    q, k, v = to_chunks(q), to_chunks(k), to_chunks(v)
    g, beta = to_chunks(g), to_chunks(beta)
    gc = jnp.cumsum(g, axis=-1)
    lower = jnp.tril(jnp.ones((CHUNK, CHUNK), dtype=bool))
    strict = jnp.tril(jnp.ones((CHUNK, CHUNK), dtype=bool), -1)
    diff = gc[..., :, None] - gc[..., None, :]
    decay = jnp.where(lower, jnp.exp(jnp.where(lower, diff, 0.0)), 0.0)
    kb = k * beta[..., None]
    m = jnp.where(strict, jnp.einsum('nbhid,nbhjd->nbhij', kb, k) * decay, 0.0)
    eye = jnp.eye(CHUNK, dtype=m.dtype)
    t = lax.linalg.triangular_solve(m + eye, jnp.broadcast_to(eye, m.shape), left_side=True, lower=True)
    u = jnp.einsum('nbhij,nbhjv->nbhiv', t, v * beta[..., None])
    w = jnp.einsum('nbhij,nbhjd->nbhid', t, kb * jnp.exp(gc)[..., None])
    a_qk = jnp.einsum('nbhid,nbhjd->nbhij', q, k) * decay
    qg = q * jnp.exp(gc)[..., None]
    k_dec = k * jnp.exp(gc[..., -1:] - gc)[..., None]
    decay_last = jnp.exp(gc[..., -1])

    def step(s, inp):
        u_n, w_n, qg_n, a_n, kd_n, dl_n = inp
        v_new = u_n - jnp.einsum('bhid,bhdv->bhiv', w_n, s)
        o = jnp.einsum('bhid,bhdv->bhiv', qg_n, s) + jnp.einsum('bhij,bhjv->bhiv', a_n, v_new)
        s = dl_n[..., None, None] * s + jnp.einsum('bhid,bhiv->bhdv', kd_n, v_new)
        return s, o

    s_fin, o = lax.scan(step, s0, (u, w, qg, a_qk, k_dec, decay_last))
    return from_chunks(o), s_fin


def _lin_combine(e1, e2):
    a1, b1 = e1
    a2, b2 = e2
    return a1 * a2, a2 * b1 + b2


def s5_scan(u, lam_re, lam_im, log_dt, b_cplx, c_cplx, h0):
    lam = lax.complex(lam_re.astype(jnp.float32), lam_im.astype(jnp.float32))
    lam_bar = jnp.exp(lam * jnp.exp(log_dt.astype(jnp.float32))[:, None])
    b_bar = ((lam_bar - 1.0) / lam)[..., None] * b_cplx
    bu = jnp.einsum('blgh,gph->blgp', u.astype(jnp.complex64), b_bar)
    bu = bu.at[:, 0].add(lam_bar * h0)
    a = jnp.broadcast_to(lam_bar, bu.shape)
    _, xs = lax.associative_scan(_lin_combine, (a, bu), axis=1)
    y = jnp.einsum('blgp,ghp->blgh', xs, c_cplx).real
    return y, xs[:, -1]


def mixer(h, p, l, st_gla, st_s5, st_gdn, latent):
    f32 = jnp.float32
    bsz, n, _ = h.shape
    idx = [int(i) for i in np.cumsum(SPLITS)[:-1]]
    proj = jnp.einsum('bld,de->ble', h, p['w_in'][l].astype(f32))
    gq, gk, gv, glr, ggate, su, sgate, dqkv, da, db, dgate = jnp.split(proj, idx, axis=-1)

    q = to_heads(gq, GLA_H) * GLA_DK ** -0.5
    k = to_heads(gk, GLA_H)
    v = to_heads(gv, GLA_H)
    glog = jax.nn.log_sigmoid(
        jnp.einsum('blsr,srk->blsk', glr.reshape(bsz, n, 2, GLA_LR), p['gla_gate_w'][l].astype(f32))
        + p['gla_gate_b'][l].astype(f32)) / GLA_TAU
    o_f, sg_f = gla_chunked(q, k, v, to_heads(glog[:, :, 0], GLA_H), st_gla[:, 0])
    o_b, sg_b = gla_chunked(flip_seq(q), flip_seq(k), flip_seq(v),
                            flip_seq(to_heads(glog[:, :, 1], GLA_H)), st_gla[:, 1])
    o_gla = from_heads(rmsnorm(o_f + flip_seq(o_b), p['gla_norm'][l])) * jax.nn.silu(ggate)

    u = grid_to_cols(su) if latent else su
    ug = u.reshape(bsz, n, S5_G, S5_GH)
    b_cplx = lax.complex(p['s5_b_re'][l].astype(f32), p['s5_b_im'][l].astype(f32))
    c_cplx = lax.complex(p['s5_c_re'][l].astype(f32), p['s5_c_im'][l].astype(f32))
    y_f, ss_f = s5_scan(ug, p['s5_lam_re'][l, 0], p['s5_lam_im'][l, 0], p['s5_log_dt'][l, 0],
                        b_cplx, c_cplx[0], st_s5[:, 0])
    y_b, ss_b = s5_scan(jnp.flip(ug, axis=1), p['s5_lam_re'][l, 1], p['s5_lam_im'][l, 1],
                        p['s5_log_dt'][l, 1], b_cplx, c_cplx[1], st_s5[:, 1])
    y = (y_f + jnp.flip(y_b, axis=1)).reshape(bsz, n, S5_W) + p['s5_d'][l].astype(f32) * u
    if latent:
        y = cols_to_grid(y)
    y = jax.nn.gelu(y)
    y = y * jax.nn.sigmoid(y @ p['s5_glu_w'][l].astype(f32) + p['s5_glu_b'][l].astype(f32))
    o_s5 = y * jax.nn.silu(sgate)

    w_conv = p['gdn_conv'][l].astype(f32)
    if latent:
        qkv = centred_conv(dqkv.reshape(bsz * (n // GRID_W), GRID_W, 3 * GDN_W), w_conv)
        qkv = qkv.reshape(bsz, n, 3 * GDN_W)
    else:
        qkv = centred_conv(dqkv, w_conv)
    cq, ck, cv = jnp.split(jax.nn.silu(qkv), 3, axis=-1)
    q = l2norm(to_heads(cq, GDN_H)) * GDN_DK ** -0.5
    k = l2norm(to_heads(ck, GDN_H))
    v = to_heads(cv, GDN_H)
    g = -jnp.exp(p['gdn_a_log'][l].astype(f32)) * jax.nn.softplus(
        da.reshape(bsz, n, 2, GDN_H) + p['gdn_dt_bias'][l].astype(f32))
    beta = jax.nn.sigmoid(db.reshape(bsz, n, 2, GDN_H))
    g = g.transpose(2, 0, 3, 1)
    beta = beta.transpose(2, 0, 3, 1)
    o_f, sd_f = gdn_chunked(q, k, v, g[0], beta[0], st_gdn[:, 0])
    o_b, sd_b = gdn_chunked(flip_seq(q), flip_seq(k), flip_seq(v), flip_seq(g[1]), flip_seq(beta[1]),
                            st_gdn[:, 1])
    o_gdn = from_heads(rmsnorm(o_f + flip_seq(o_b), p['gdn_norm'][l])) * jax.nn.silu(dgate)

    out = jnp.concatenate([o_gla, o_s5, o_gdn], axis=-1) @ p['w_out'][l].astype(f32)
    states = (jnp.stack([sg_f, sg_b], axis=1), jnp.stack([ss_f, ss_b], axis=1),
              jnp.stack([sd_f, sd_b], axis=1))
    return out, states


def adaln(cond, p, l):
    return (jax.nn.silu(cond.astype(jnp.float32)) @ p['w_ada'][l].astype(jnp.float32)
            + p['b_ada'][l].astype(jnp.float32))


def trunk_layer(x, ada, p, l, st_gla, st_s5, st_gdn, latent):
    shift, scale, gate = jnp.split(ada, 3, axis=-1)
    h = rmsnorm(x, p['norm_pre'][l]) * (1.0 + scale) + shift
    out, states = mixer(h, p, l, st_gla, st_s5, st_gdn, latent)
    return x + gate * rmsnorm(out, p['norm_post'][l]), states


def setup_inputs(seed: int = 0) -> dict:
    key = jax.random.key(seed)
    ks = iter(jax.random.split(key, 40))
    f32 = jnp.float32

    def nrm(shape, scale):
        return jax.random.normal(next(ks), shape, f32) * scale

    def unif(shape, lo, hi):
        return jax.random.uniform(next(ks), shape, f32, lo, hi)

    lam_re = -0.5 + nrm((DEPTH, 2, S5_G, S5_P), 0.01)
    lam_im = math.pi * jnp.arange(S5_P, dtype=f32) + nrm((DEPTH, 2, S5_G, S5_P), 0.01)
    s5_log_dt = unif((DEPTH, 2, S5_G), math.log(1e-3), math.log(1e-1))
    gdn_dt = jnp.exp(unif((DEPTH, 2, GDN_H), math.log(1e-3), math.log(1e-1)))
    gdn_dt_bias = gdn_dt + jnp.log(-jnp.expm1(-gdn_dt))
    gdn_a_log = jnp.log(unif((DEPTH, 2, GDN_H), 1.0, 16.0))
    return {
        'x_prompt': nrm((BATCH, SEQ, D_MODEL), 1.0),
        'x_sample': nrm((DEC_BATCH, DEC_SEQ, D_MODEL), 1.0),
        'c': nrm((DEC_BATCH, D_MODEL), 1.0),
        'state_gla': nrm((DEC_BATCH, DEPTH, 2, GLA_H, GLA_DK, GLA_DV), 1.0),
        'state_s5_re': nrm((DEC_BATCH, DEPTH, 2, S5_G, S5_P), 0.1),
        'state_s5_im': nrm((DEC_BATCH, DEPTH, 2, S5_G, S5_P), 0.1),
        'state_gdn': nrm((DEC_BATCH, DEPTH, 2, GDN_H, GDN_DK, GDN_DV), GDN_DK ** -0.5),
        'c_ctx': nrm((D_MODEL,), 1.0),
        'norm_pre': 1.0 + nrm((DEPTH, D_MODEL), 0.02),
        'norm_post': 1.0 + nrm((DEPTH, D_MODEL), 0.02),
        'w_ada': nrm((DEPTH, D_MODEL, 3 * D_MODEL), 0.5 * D_MODEL ** -0.5),
        'b_ada': nrm((DEPTH, 3 * D_MODEL), 0.02),
        'w_in': nrm((DEPTH, D_MODEL, IN_DIM), D_MODEL ** -0.5),
        'gla_gate_w': nrm((DEPTH, 2, GLA_LR, GLA_QK), GLA_LR ** -0.5),
        'gla_gate_b': nrm((DEPTH, 2, GLA_QK), 0.1),
        'gla_norm': 1.0 + nrm((DEPTH, GLA_DV), 0.02),
        's5_lam_re': lam_re,
        's5_lam_im': lam_im,
        's5_log_dt': s5_log_dt,
        's5_b_re': nrm((DEPTH, S5_G, S5_P, S5_GH), (2 * S5_GH) ** -0.5),
        's5_b_im': nrm((DEPTH, S5_G, S5_P, S5_GH), (2 * S5_GH) ** -0.5),
        's5_c_re': nrm((DEPTH, 2, S5_G, S5_GH, S5_P), 0.5),
        's5_c_im': nrm((DEPTH, 2, S5_G, S5_GH, S5_P), 0.5),
        's5_d': nrm((DEPTH, S5_W), 1.0),
        's5_glu_w': nrm((DEPTH, S5_W, S5_W), S5_W ** -0.5),
        's5_glu_b': nrm((DEPTH, S5_W), 0.02),
        'gdn_conv': nrm((DEPTH, CONV_K, 3 * GDN_W), CONV_K ** -0.5),
        'gdn_a_log': gdn_a_log,
        'gdn_dt_bias': gdn_dt_bias,
        'gdn_norm': 1.0 + nrm((DEPTH, GDN_DV), 0.02),
        'w_out': nrm((DEPTH, MIX_W, D_MODEL), MIX_W ** -0.5),
    }


def reference(x_prompt, x_sample, c, state_gla, state_s5_re, state_s5_im, state_gdn, c_ctx,
              norm_pre, norm_post, w_ada, b_ada, w_in, gla_gate_w, gla_gate_b, gla_norm,
              s5_lam_re, s5_lam_im, s5_log_dt, s5_b_re, s5_b_im, s5_c_re, s5_c_im, s5_d,
              s5_glu_w, s5_glu_b, gdn_conv, gdn_a_log, gdn_dt_bias, gdn_norm, w_out):
    f32 = jnp.float32
    p = dict(norm_pre=norm_pre, norm_post=norm_post, w_ada=w_ada, b_ada=b_ada, w_in=w_in,
             gla_gate_w=gla_gate_w, gla_gate_b=gla_gate_b, gla_norm=gla_norm,
             s5_lam_re=s5_lam_re, s5_lam_im=s5_lam_im, s5_log_dt=s5_log_dt,
             s5_b_re=s5_b_re, s5_b_im=s5_b_im, s5_c_re=s5_c_re, s5_c_im=s5_c_im, s5_d=s5_d,
             s5_glu_w=s5_glu_w, s5_glu_b=s5_glu_b, gdn_conv=gdn_conv, gdn_a_log=gdn_a_log,
             gdn_dt_bias=gdn_dt_bias, gdn_norm=gdn_norm, w_out=w_out)

    bp = x_prompt.shape[0]
    z_gla = jnp.zeros((bp, 2, GLA_H, GLA_DK, GLA_DV), f32)
    z_s5 = jnp.zeros((bp, 2, S5_G, S5_P), jnp.complex64)
    z_gdn = jnp.zeros((bp, 2, GDN_H, GDN_DK, GDN_DV), f32)
    xp = x_prompt.astype(f32)
    gla_states, s5_states, gdn_states = [], [], []
    for l in range(DEPTH):
        ada = adaln(c_ctx, p, l)[None, None, :]
        xp, (sg, ss, sd) = trunk_layer(xp, ada, p, l, z_gla, z_s5, z_gdn, False)
        gla_states.append(sg)
        s5_states.append(ss)
        gdn_states.append(sd)
    y_prompt = xp.astype(x_prompt.dtype)
    s5_all = jnp.stack(s5_states, axis=1)
    new_state_gla = jnp.stack(gla_states, axis=1).astype(x_prompt.dtype)
    new_state_s5_re = s5_all.real.astype(x_prompt.dtype)
    new_state_s5_im = s5_all.imag.astype(x_prompt.dtype)
    new_state_gdn = jnp.stack(gdn_states, axis=1).astype(x_prompt.dtype)

    s5_cache = lax.complex(state_s5_re.astype(f32), state_s5_im.astype(f32))
    xs = x_sample.astype(f32)
    for l in range(DEPTH):
        ada = adaln(c, p, l)[:, None, :]
        xs, _ = trunk_layer(xs, ada, p, l, state_gla[:, l].astype(f32), s5_cache[:, l],
                            state_gdn[:, l].astype(f32), True)
    y_sample = xs.astype(x_sample.dtype)
    return (y_prompt, y_sample, new_state_gla, new_state_s5_re, new_state_s5_im, new_state_gdn)
```

```python
import contextlib
import math
import numpy as np
import concourse.bass as bass
import concourse.mybir as mybir
from concourse.bass_utils import run_bass_kernel_spmd

F32 = mybir.dt.float32
BF16 = mybir.dt.bfloat16
I32 = mybir.dt.int32
AF = mybir.ActivationFunctionType
ALU = mybir.AluOpType

DEPTH = 4
D = 1024
T = 1536
NT = 12
IN_DIM = 4656
EPS = 1e-6
SEQS = [(0, 256), (256, 256), (512, 1024)]
NDS = 8
NEGBIG = -30000.0
DEBUG = {}
EMBED_WAIT = True

_CL = {}


def _build_consts():
    cols = []
    off = 0

    def add(name, arr):
        nonlocal off
        arr = np.asarray(arr, np.float32).reshape(128, -1)
        _CL[name] = (off, arr.shape[1])
        off += arr.shape[1]
        cols.append(arr)

    j = np.arange(128)[:, None]
    i = np.arange(128)[None, :]
    add('ident', (j == i))
    add('ones', np.ones((128, 128)))
    add('U', (j <= i))
    add('SU', (j < i))
    add('L', (j >= i))
    add('SL', (j > i))
    add('UN16', (j <= i) / -16.0)
    add('LN16', (j >= i) / -16.0)
    add('SLN16', (j > i) / -16.0)
    add('SUN16', (j < i) / -16.0)
    add('NEGf', NEGBIG * (i < j))
    add('NEGb', NEGBIG * (i > j))
    kt = np.zeros((128, 4, 8), np.float32)
    s = np.arange(8)
    kt[:64, 0] = 7 - s
    kt[64:, 0] = s
    kt[:64, 1] = s + 1
    kt[64:, 1] = 8 - s
    kt[:64, 2] = -s
    kt[64:, 2] = s
    kt[:64, 3] = s
    kt[64:, 3] = -s
    add('KTAB', kt)
    sp = (np.arange(128) // 16)[:, None]
    tf = (np.arange(128) // 16)[None, :]
    add('TMF', (sp <= tf))
    add('TMB', (sp >= tf))
    global NCONST_P
    NCONST_P = off
    bd = lambda bs: (j // bs == i // bs)
    add('M16', bd(16))
    add('OFF16', bd(32) & ~bd(16))
    add('OFF32', bd(64) & ~bd(32))
    add('OFF64', ~bd(64))
    selw = np.zeros((128, 8, 240), np.float32)
    for g in range(8):
        for h in range(16):
            selw[g * 16 + h, g, 112 + h] = 1.0
    add('SELW', selw)
    return np.concatenate(cols, axis=1)


NCONST_P = 0
CONSTS = _build_consts()
NCONST = CONSTS.shape[1]


class KB:
    def __init__(self):
        self.nc = bass.Bass("TRN2", target_bir_lowering=False)
        nc = self.nc
        self.eng = {'pe': nc.tensor, 'act': nc.scalar, 'dve': nc.vector, 'pool': nc.gpsimd,
                    'sp': nc.sync}
        self.semobj = {}
        for e in ['pe', 'act', 'dve', 'pool']:
            self.semobj[e] = nc.alloc_semaphore(name=f"s_{e}")
        self.cnt = {e: 0 for e in ['pe', 'act', 'dve', 'pool']}
        self.dcnt = {}
        self.dnext = {'sp': 0, 'pool': 0}
        for q in ['sp', 'pool']:
            for i in range(NDS):
                sid = f"d_{q}{i}"
                self.semobj[sid] = nc.alloc_semaphore(name=sid)
                self.dcnt[sid] = 0
        self.waited = {e: {} for e in self.eng}
        self.lw = {}
        self.rd = {}
        self.nalloc = 0
        self.psn = 0
        self.banks = None

    def sb(self, es, shape, dt=F32, name=None):
        self.nalloc += 1
        return es.enter_context(self.nc.sbuf_tensor(f"{name or 't'}_{self.nalloc}", list(shape), dt))

    def init_psum(self, es):
        self.banks = [es.enter_context(self.nc.psum_tensor(f"bank{i}", [128, 512], F32))
                      for i in range(8)]

    def ps(self):
        i = self.psn
        self.psn = (self.psn + 1) % 7
        return self.banks[i], ('ps', i)

    def _need(self, e, deps):
        need = {}
        for d in deps:
            if d is None:
                continue
            sid, val = d
            if val <= 0:
                continue
            if e == 'pe' and sid == 'pe':
                continue
            if val > need.get(sid, 0):
                need[sid] = val
        return [(sid, val) for sid, val in need.items() if self.waited[e].get(sid, 0) < val]

    def _wait(self, e, deps, keep_last=False):
        need = self._need(e, deps)
        last = None
        if keep_last and need:
            last = need.pop()
        for sid, val in need:
            self.eng[e].wait_ge(self.semobj[sid], val)
            self.waited[e][sid] = val
        return last

    def _deps(self, r, w):
        deps = []
        for k in r:
            deps.append(self.lw.get(k))
        for k in w:
            deps.append(self.lw.get(k))
            deps.extend(self.rd.get(k, {}).items())
        return deps

    def _mark(self, tag, r, w):
        for k in w:
            self.lw[k] = tag
            self.rd[k] = {}
        for k in r:
            d = self.rd.setdefault(k, {})
            if tag[1] > d.get(tag[0], 0):
                d[tag[0]] = tag[1]

    def op(self, e, fn, r=(), w=(), inc=True):
        psr = [x for x in r if isinstance(x, tuple) and x[0] == 'ps']
        if psr:
            r = [x for x in r if not (isinstance(x, tuple) and x[0] == 'ps')]
            w = list(w) + psr
        embed = EMBED_WAIT and e in ('act', 'dve', 'pool')
        last = self._wait(e, self._deps(r, w), keep_last=embed)
        inst = fn(self.eng[e])
        if last is not None:
            inst._wait_ge(self.semobj[last[0]], last[1])
            self.waited[e][last[0]] = last[1]
        if inc:
            self.cnt[e] += 1
            inst.then_inc(self.semobj[e], 1)
            tag = (e, self.cnt[e])
        else:
            tag = (e, self.cnt[e] + 1)
        self._mark(tag, r, w)

    def dma(self, q, out, in_, r=(), w=(), **kw):
        slot = self.dnext[q]
        self.dnext[q] = (slot + 1) % NDS
        sid = f"d_{q}{slot}"
        deps = self._deps(r, w)
        deps.append((sid, self.dcnt[sid]))
        self._wait(q, deps)
        inst = self.eng[q].dma_start(out=out, in_=in_, **kw)
        self.dcnt[sid] += 16
        inst.then_inc(self.semobj[sid], 16)
        self._mark((sid, self.dcnt[sid]), r, w)

    def barrier(self):
        allv = [(e, self.cnt[e]) for e in self.cnt] + [(sid, v) for sid, v in self.dcnt.items()]
        for e in self.eng:
            self._wait(e, allv)
        self.lw = {}
        self.rd = {}

    def mm(self, out, lhsT, rhs, start, stop, r, w, inc=None):
        self.op('pe', lambda E: E.matmul(out, lhsT, rhs, start=start, stop=stop), r, w,
                inc=(stop if inc is None else inc))

    def tr(self, out, in_, ident, r, w, inc=True):
        self.op('pe', lambda E: E.transpose(out, in_, ident), r, w, inc=inc)

    def act(self, out, in_, func, r, w, bias=None, scale=None, e='act'):
        kw = {}
        if bias is not None:
            kw['bias'] = bias
        if scale is not None:
            kw['scale'] = scale
        self.op('act', lambda E: E.activation(out, in_, func, **kw), r, w)

    def tt(self, out, a, b, op, r, w, e='dve'):
        self.op(e, lambda E: E.tensor_tensor(out, a, b, op), r, w)

    def ts(self, out, a, s1, s2, op0, op1, r, w, e='dve'):
        if op1 is None:
            self.op(e, lambda E: E.tensor_scalar(out, a, s1, None, op0), r, w)
        else:
            self.op(e, lambda E: E.tensor_scalar(out, a, s1, s2, op0, op1), r, w)

    def stt(self, out, a, sc, b, op0, op1, r, w, e='dve'):
        self.op(e, lambda E: E.scalar_tensor_tensor(out, a, sc, b, op0, op1), r, w)

    def cp(self, out, in_, r, w, e='dve'):
        if e == 'act':
            self.op('act', lambda E: E.copy(out, in_), r, w)
        else:
            self.op(e, lambda E: E.tensor_copy(out, in_), r, w)

    def recip(self, out, in_, r, w):
        self.op('dve', lambda E: E.reciprocal(out, in_), r, w)

    def memset(self, ap, v, w, e='dve'):
        self.op(e, lambda E: E.memset(ap, v), (), w)


def build(dbg=None):
    dbg = dbg or {}
    k = KB()
    nc = k.nc

    def din(name, shape):
        return nc.dram_tensor(name, list(shape), F32, kind="ExternalInput").ap()

    def dout(name, shape):
        return nc.dram_tensor(name, list(shape), F32, kind="ExternalOutput").ap()

    xin = din('xin', [T, D])
    cond = din('cond', [2, D])
    st_gla = din('st_gla', [DEPTH, 2, 4, 64, 128])
    st_s5re = din('st_s5re', [DEPTH, 2, 32, 64])
    st_s5im = din('st_s5im', [DEPTH, 2, 32, 64])
    st_gdn = din('st_gdn', [DEPTH, 2, 4, 128, 128])
    norm_pre = din('norm_pre', [DEPTH, D])
    norm_post = din('norm_post', [DEPTH, D])
    w_ada = din('w_ada', [DEPTH, D, 3 * D])
    b_ada = din('b_ada', [DEPTH, 3 * D])
    w_in = din('w_in', [DEPTH, D, IN_DIM])
    gla_gate_w = din('gla_gate_w', [DEPTH, 2, 16, 256])
    gla_gate_b = din('gla_gate_b', [DEPTH, 2, 256])
    gla_norm = din('gla_norm', [DEPTH, 128])
    s5_lam_re = din('s5_lam_re', [DEPTH, 2, 32, 64])
    s5_lam_im = din('s5_lam_im', [DEPTH, 2, 32, 64])
    s5_log_dt = din('s5_log_dt', [DEPTH, 2, 32])
    s5_b_re = din('s5_b_re', [DEPTH, 32, 64, 16])
    s5_b_im = din('s5_b_im', [DEPTH, 32, 64, 16])
    s5_c_re = din('s5_c_re', [DEPTH, 2, 32, 16, 64])
    s5_c_im = din('s5_c_im', [DEPTH, 2, 32, 16, 64])
    s5_d = din('s5_d', [DEPTH, 512])
    s5_glu_w = din('s5_glu_w', [DEPTH, 512, 512])
    s5_glu_b = din('s5_glu_b', [DEPTH, 512])
    gdn_conv = din('gdn_conv', [DEPTH, 3, 1536])
    gdn_a_log = din('gdn_a_log', [DEPTH, 2, 4])
    gdn_dt_bias = din('gdn_dt_bias', [DEPTH, 2, 4])
    gdn_norm = din('gdn_norm', [DEPTH, 128])
    w_out = din('w_out', [DEPTH, 1536, D])
    consts = din('consts', [128, NCONST])

    y_out = dout('y', [T, D])
    ns_gla = dout('ns_gla', [2, DEPTH, 2, 4, 64, 128])
    ns_s5re = dout('ns_s5re', [2, DEPTH, 2, 32, 64])
    ns_s5im = dout('ns_s5im', [2, DEPTH, 2, 32, 64])
    ns_gdn = dout('ns_gdn', [2, DEPTH, 2, 4, 128, 128])
    gsc = nc.dram_tensor('gsc', [DEPTH, 2, 96, 128], F32, kind="Internal").ap()
    dbg_out = {}
    for name, shape in dbg.items():
        dbg_out[name] = dout('dbg_' + name, shape)

    es = contextlib.ExitStack()
    with es:
        k.init_psum(es)
        CS = k.sb(es, [128, NCONST_P], F32, 'consts_sb')

        def C(name):
            o, w = _CL[name]
            return CS[:, o:o + w]

        k.dma('sp', CS[:], consts[:, 0:NCONST_P], (), ['C'])
        identb = k.sb(es, [128, 128], BF16, 'identb')
        k.cp(identb[:], C('ident'), ['C'], ['identb'])
        onesb = k.sb(es, [128, 128], BF16, 'onesb')
        k.cp(onesb[:], C('ones'), ['C'], ['onesb'])
        Ub = k.sb(es, [128, 128], BF16, 'Ub')
        k.cp(Ub[:], C('U'), ['C'], ['Ub'])
        Lb = k.sb(es, [128, 128], BF16, 'Lb')
        k.cp(Lb[:], C('L'), ['C'], ['Lb'])

        xT = k.sb(es, [128, 8, T], F32, 'xT')
        catT = k.sb(es, [128, 12, T], BF16, 'catT')
        npre = k.sb(es, [128, DEPTH, 8], F32, 'npre')
        npost = k.sb(es, [128, DEPTH, 8], F32, 'npost')
        bada = k.sb(es, [128, DEPTH, 24], F32, 'bada')
        glan = k.sb(es, [128, DEPTH], F32, 'glan')
        gdnn = k.sb(es, [128, DEPTH], F32, 'gdnn')
        s5dT = k.sb(es, [128, DEPTH, 4], F32, 's5dT')
        glub = k.sb(es, [128, DEPTH, 4], F32, 'glub')
        convw = k.sb(es, [128, DEPTH, 3, 12], F32, 'convw')
        scond = k.sb(es, [128, 2, 8], F32, 'scond')
        stg = k.sb(es, [128, 128], F32, 'stg')

        def load_cols(dst, src_rows, R, key):
            k.dma('sp', stg[0:R, :], src_rows, (), ['stg'])
            pb, kp = k.ps()
            k.tr(pb[:, 0:R], stg[0:R, :], C('ident')[0:R, 0:R], ['stg', 'C'], [kp])
            k.cp(dst, pb[:, 0:R], [kp], [key])

        load_cols(npre[:].rearrange("p l c -> p (l c)"), norm_pre.rearrange("l (c p) -> (l c) p", p=128), 32, 'npre')
        load_cols(npost[:].rearrange("p l c -> p (l c)"), norm_post.rearrange("l (c p) -> (l c) p", p=128), 32, 'npost')
        load_cols(bada[:].rearrange("p l c -> p (l c)"), b_ada.rearrange("l (c p) -> (l c) p", p=128), 96, 'bada')
        load_cols(glan[:], gla_norm, 4, 'glan')
        load_cols(gdnn[:], gdn_norm, 4, 'gdnn')
        load_cols(s5dT[:].rearrange("p l c -> p (l c)"), s5_d.rearrange("l (c p) -> (l c) p", p=128), 16, 's5dT')
        load_cols(glub[:].rearrange("p l c -> p (l c)"), s5_glu_b.rearrange("l (c p) -> (l c) p", p=128), 16, 'glub')
        for l in range(DEPTH):
            load_cols(convw[:, l].rearrange("p k c -> p (k c)"), gdn_conv[l].rearrange("k (c p) -> (k c) p", p=128), 36, 'convw')
        load_cols(scond[:].rearrange("p j c -> p (j c)"), cond.rearrange("j (c p) -> (j c) p", p=128), 16, 'scond')
        k.act(scond[:], scond[:], AF.Silu, ['scond'], ['scond'])

        adaA = k.sb(es, [128, 2, 24, 2], F32, 'adaA')
        gmodA = k.sb(es, [128, 2, 8, 2], F32, 'gmodA')
        gpostA = k.sb(es, [128, 2, 8, 2], F32, 'gpostA')

        def ada_gen(l, wab):
            par = l % 2
            wv = w_ada[l].rearrange("(c p) f -> p c f", p=128)
            bk7, ky7 = k.banks[7], ('ps', 7)
            k.dma('sp', wab[0][:], wv[:, :, 0:128], (), [('wabp', 0)])
            yield
            for fb in range(24):
                if fb + 1 < 24:
                    k.dma('sp', wab[(fb + 1) % 2][:], wv[:, :, (fb + 1) * 128:(fb + 2) * 128], (),
                          [('wabp', (fb + 1) % 2)])
                for c in range(8):
                    k.mm(bk7[:, 0:2], wab[fb % 2][:, c, :], scond[:, :, c], c == 0, c == 7,
                         [('wabp', fb % 2), 'scond'], [ky7])
                yield
                k.ts(adaA[:, par, fb, :], bk7[:, 0:2], bada[:, l, fb:fb + 1], None, ALU.add, None,
                     [ky7, 'bada'], [('adaA', par)])
                yield
            ka = ('adaA', par)
            k.ts(gmodA[:, par], adaA[:, par, 8:16, :], 1.0, None, ALU.add, None, [ka], [ka])
            k.tt(gmodA[:, par], gmodA[:, par], npre[:, l, :].unsqueeze(2).broadcast_to([128, 8, 2]),
                 ALU.mult, [ka, 'npre'], [ka])
            k.tt(gpostA[:, par], adaA[:, par, 16:24, :],
                 npost[:, l, :].unsqueeze(2).broadcast_to([128, 8, 2]), ALU.mult, [ka, 'npost'], [ka])
            yield

        with contextlib.ExitStack() as e0:
            wab0 = [k.sb(e0, [128, 8, 128], F32, f'wab0_{i}') for i in range(2)]
            for _ in ada_gen(0, wab0):
                pass
            k.barrier()

        with contextlib.ExitStack() as e0:
            xtm = [k.sb(e0, [128, D], F32, f'xtm{i}') for i in range(2)]
            for t in range(NT):
                xb = xtm[t % 2]
                kx = ('xtm', t % 2)
                k.dma('sp', xb[:], xin[t * 128:(t + 1) * 128, :], (), [kx])
                for half in range(2):
                    pb, kp = k.ps()
                    for c4 in range(4):
                        c = half * 4 + c4
                        k.tr(pb[:, c4 * 128:(c4 + 1) * 128], xb[:, c * 128:(c + 1) * 128],
                             C('ident'), [kx, 'C'], [kp])
                    k.cp(xT[:, half * 4:half * 4 + 4, t * 128:(t + 1) * 128],
                         pb[:].rearrange("p (c t) -> p c t", c=4), [kp], [('xT', t // 4)],
                         e='act' if half else 'dve')
            k.barrier()

        for l in range(DEBUG.get('depth', DEPTH)):
            layer(k, es, l, locals())

        with contextlib.ExitStack() as e0:
            ytm = [k.sb(e0, [128, D], F32, f'ytm{i}') for i in range(2)]
            for t in range(NT):
                yb = ytm[t % 2]
                ky = ('ytm', t % 2)
                for half in range(2):
                    pb, kp = k.ps()
                    for c4 in range(4):
                        c = half * 4 + c4
                        k.tr(pb[:, c4 * 128:(c4 + 1) * 128], xT[:, c, t * 128:(t + 1) * 128],
                             C('ident'), [('xT', t // 4), 'C'], [kp])
                    k.cp(yb[:, half * 512:(half + 1) * 512], pb[:], [kp], [ky],
                         e='act' if half else 'dve')
                k.dma('sp', y_out[t * 128:(t + 1) * 128, :], yb[:], [ky], [('yout', t)])
            k.barrier()
    return nc


def layer(k, es, l, G):
    nc = k.nc
    C = G['C']
    xT = G['xT']
    catT = G['catT']
    dbg_out = G['dbg_out']
    identb = G['identb']

    def dump(name, ap, r):
        if name in dbg_out and l == DEBUG.get('layer', 0):
            k.dma('sp', dbg_out[name], ap, r, [('dbg', name)])

    with contextlib.ExitStack() as el:
        par_ = l % 2
        adaT = G['adaA'][:, par_]
        gmod = G['gmodA'][:, par_]
        gpost = G['gpostA'][:, par_]

        if DEBUG.get('stop', 9) < 2:
            return
        hT = k.sb(el, [128, 8, T], BF16, 'hT')
        with contextlib.ExitStack() as eh:
            WH = {}

            def alloc_w(stk, ncols):
                WH['W'] = k.sb(stk, [128, 8, ncols], BF16, 'W')
                WH['wstg'] = [k.sb(stk, [128, 8, 128], F32, f'wstg{i}') for i in range(2)]
            with contextlib.ExitStack() as e1:
                sq = k.sb(e1, [128, 8, 512], BF16, 'sq')
                rstd = k.sb(e1, [128, 512], F32, 'rstd')
                tmps = [k.sb(e1, [128, 512], F32, f'tmpB{i}') for i in range(2)]
                for b in range(3):
                    j = 0 if b == 0 else 1
                    bs = slice(b * 512, (b + 1) * 512)
                    k.act(sq[:], xT[:, :, bs], AF.Square, [('xT', b)], ['sq'])
                    pb, kp = k.ps()
                    for c in range(8):
                        k.mm(pb[:], G['onesb'][:], sq[:, c, :], c == 0, c == 7, ['sq', 'onesb'], [kp])
                    k.act(rstd[:], pb[:], AF.Ln, [kp], ['rstd'], scale=1.0 / D, bias=EPS)
                    k.act(rstd[:], rstd[:], AF.Exp, ['rstd'], ['rstd'], scale=-0.5)
                    for c in range(8):
                        tmp = tmps[c % 2]
                        k.tt(tmp[:], xT[:, c, bs], rstd[:], ALU.mult, [('xT', b), 'rstd'],
                             [('tmpB', c % 2)])
                        k.act(hT[:, c, bs], tmp[:], AF.Identity, [('tmpB', c % 2), 'gmod', 'adaT'],
                              [('hT', b)], bias=adaT[:, c, j:j + 1], scale=gmod[:, c, j:j + 1])
                k.barrier()
            dump('hT', hT[:, :, 0:128], [('hT', 0)])

            wctr = [0]

            def load_w(col0, ncols, dst0=0):
                wsrc = G['w_in'][l].rearrange("(c p) e -> p c e", p=128)
                for o in range(0, ncols, 128):
                    n = min(128, ncols - o)
                    i = wctr[0] % 2
                    wctr[0] += 1
                    k.dma('sp', WH['wstg'][i][:, :, 0:n], wsrc[:, :, col0 + o:col0 + o + n], (),
                          [('wstg', i)])
                    k.cp(WH['W'][:, :, dst0 + o:dst0 + o + n], WH['wstg'][i][:, :, 0:n],
                         [('wstg', i)], ['W'], e='act' if i else 'dve')

            def proj_fm(wc0, ncols, b):
                pb, kp = k.ps()
                for c in range(8):
                    k.mm(pb[0:ncols, :], WH['W'][:, c, wc0:wc0 + ncols], hT[:, c, b * 512:(b + 1) * 512],
                         c == 0, c == 7, ['W', ('hT', b)], [kp])
                return pb, kp

            def proj_tm(wc0, ncols, t):
                pb, kp = k.ps()
                for c in range(8):
                    k.mm(pb[:, 0:ncols], hT[:, c, t * 128:(t + 1) * 128], WH['W'][:, c, wc0:wc0 + ncols],
                         c == 0, c == 7, ['W', ('hT', t // 4)], [kp])
                return pb, kp

            def headnorm(e1, oT, koT, nh, normw, gate_wc0, cat0):
                NS_ = 3
                HS = [dict(sqh=k.sb(e1, [128, 512], BF16, f'sqh{i}'), rs=k.sb(e1, [128, 512], F32, f'rsh{i}'),
                           gs=k.sb(e1, [128, 512], F32, f'gsh{i}')) for i in range(NS_)]

                def item(sl, h, b):
                    sqh, rs, gs = HS[sl]['sqh'], HS[sl]['rs'], HS[sl]['gs']
                    K_ = lambda nm: (nm, sl)
                    bkA, kyA = k.banks[2 * sl], ('ps', 2 * sl)
                    bkB, kyB = k.banks[2 * sl + 1], ('ps', 2 * sl + 1)
                    bs = slice(b * 512, (b + 1) * 512)
                    k.act(sqh[:], oT[:, h, bs], AF.Square, [koT], [K_('sqh')])
                    for c in range(8):
                        k.mm(bkB[:, :], WH['W'][:, c, gate_wc0 + h * 128:gate_wc0 + (h + 1) * 128],
                             hT[:, c, bs], c == 0, c == 7, ['W', ('hT', b)], [kyB])
                    yield
                    k.mm(bkA[:], G['onesb'][:], sqh[:], True, True, [K_('sqh'), 'onesb'], [kyA])
                    k.act(gs[:], bkB[:], AF.Silu, [kyB], [K_('gsh')])
                    yield
                    k.act(rs[:], bkA[:], AF.Ln, [kyA], [K_('rsh')], scale=1.0 / 128, bias=EPS)
                    k.act(rs[:], rs[:], AF.Exp, [K_('rsh')], [K_('rsh')], scale=-0.5)
                    yield
                    k.stt(rs[:], rs[:], normw, gs[:], ALU.mult, ALU.mult,
                          [K_('rsh'), K_('gsh'), 'glan', 'gdnn'], [K_('rsh')])
                    k.tt(catT[:, cat0 + h, bs], oT[:, h, bs], rs[:], ALU.mult, [koT, K_('rsh')],
                         [('catT', cat0 + h)])
                    yield

                items = [(h, b) for h in range(nh) for b in range(3)]
                free = list(range(NS_))
                running = []
                while items or running:
                    while items and free:
                        sl = free.pop(0)
                        h, b = items.pop(0)
                        running.append((sl, item(sl, h, b)))
                    for (sl, g_) in list(running):
                        try:
                            next(g_)
                        except StopIteration:
                            running.remove((sl, g_))
                            free.append(sl)

            if DEBUG.get('stop', 9) < 3:
                return
            if DEBUG.get('gla', True):
                gla_phase(k, l, G, locals())
            else:
                k.memset(catT[:, 0:4, :], 0.0, [('catT', i) for i in range(4)])
            k.barrier()
            if DEBUG.get('gdn', True):
                from_gdn = gdn_phase(k, l, G, locals())
            else:
                k.memset(catT[:, 8:12, :], 0.0, [('catT', 8 + i) for i in range(4)])
            k.barrier()
            if DEBUG.get('s5', True):
                s5_part1(k, l, G, locals())
            k.barrier()
        if DEBUG.get('s5', True):
            s5_part2(k, l, G, locals())
        else:
            k.memset(catT[:, 4:8, :], 0.0, [('catT', 4 + i) for i in range(4)])
        k.barrier()
        dump('catT', catT[:, :, 0:128], [('catT', i) for i in range(12)])

        if DEBUG.get('nog'):
            return
        with contextlib.ExitStack() as e1:
            Wo = k.sb(e1, [128, 12, D], BF16, 'Wo')
            wos = [k.sb(e1, [128, 2, D], F32, f'wos{i}') for i in range(2)]
            wosrc = G['w_out'][l].rearrange("(m p) f -> p m f", p=128)
            for m2 in range(6):
                k.dma('sp', wos[m2 % 2][:], wosrc[:, m2 * 2:m2 * 2 + 2, :], (), [('wos', m2 % 2)])
                k.cp(Wo[:, m2 * 2:m2 * 2 + 2, :], wos[m2 % 2][:], [('wos', m2 % 2)], ['Wo'], e='act' if m2 % 2 else 'dve')
            oTs = [k.sb(e1, [128, 8, 512], F32, f'outT{i}') for i in range(2)]
            sq = k.sb(e1, [128, 8, 512], BF16, 'sqG')
            rstd = k.sb(e1, [128, 512], F32, 'rstdG')
            tmp = k.sb(e1, [128, 512], F32, 'tmpG')
            for b in range(3):
                j = 0 if b == 0 else 1
                bs = slice(b * 512, (b + 1) * 512)
                oT = oTs[b % 2]
                kO = ('outT', b % 2)
                for ft in range(8):
                    pb, kp = k.ps()
                    for m in range(12):
                        k.mm(pb[:], Wo[:, m, ft * 128:(ft + 1) * 128], catT[:, m, bs], m == 0,
                             m == 11, ['Wo'] + [('catT', i) for i in range(12)], [kp])
                    k.cp(oT[:, ft, :], pb[:], [kp], [kO])
                    k.act(sq[:, ft, :], pb[:], AF.Square, [kp], ['sqG'])
                pb, kp = k.ps()
                for c in range(8):
                    k.mm(pb[:], G['onesb'][:], sq[:, c, :], c == 0, c == 7, ['sqG', 'onesb'], [kp])
                k.act(rstd[:], pb[:], AF.Ln, [kp], ['rstdG'], scale=1.0 / D, bias=EPS)
                k.act(rstd[:], rstd[:], AF.Exp, ['rstdG'], ['rstdG'], scale=-0.5)
                for c in range(8):
                    k.tt(tmp[:], oT[:, c, :], rstd[:], ALU.mult, [kO, 'rstdG'], ['tmpG'])
                    k.stt(xT[:, c, bs], tmp[:], gpost[:, c, j:j + 1], xT[:, c, bs], ALU.mult,
                          ALU.add, ['tmpG', 'gpost', ('xT', b)], [('xT', b)])
            k.barrier()


def gla_phase(k, l, G, L):
    C = G['C']
    catT = G['catT']
    hT = L['hT']
    proj_fm = L['proj_fm']
    dump = L['dump']
    identb = G['identb']
    with contextlib.ExitStack() as e0:
      L['alloc_w'](e0, 544)
      W = L['WH']['W']
      L['load_w'](0, 512)
      oT = k.sb(e0, [128, 4, T], F32, 'oTgla')
      with contextlib.ExitStack() as e1:
        qkT = k.sb(e1, [128, 4, T], BF16, 'qkT')
        gwp = k.sb(e1, [32, 512], F32, 'gwp')
        gbr = k.sb(e1, [1, 512], F32, 'gbr')
        onesr = k.sb(e1, [1, 128], F32, 'onesr')
        k.memset(gwp[:], 0.0, ['gwp'])
        k.memset(onesr[:], 1.0, ['onesr'])
        k.dma('sp', gwp[0:16, 0:256], G['gla_gate_w'][l, 0], (), ['gwp'])
        k.dma('sp', gwp[16:32, 256:512], G['gla_gate_w'][l, 1], (), ['gwp'])
        k.dma('sp', gbr[:], G['gla_gate_b'][l].rearrange("(o a) b -> o (a b)", o=1), (), ['gbr'])
        for b in range(3):
            bs = slice(b * 512, (b + 1) * 512)
            for i in range(4):
                pb, kp = proj_fm(i * 128, 128, b)
                if i < 2:
                    k.act(qkT[:, i, bs], pb[:], AF.Identity, [kp], [("qkT", i)], scale=0.125)
                else:
                    k.cp(qkT[:, i, bs], pb[:], [kp], [('qkT', i)])
        k.barrier()
        L['load_w'](512, 544)
        CH = []
        for ci in range(2):
            sbf = lambda shp, dt, nm: k.sb(e1, shp, dt, f'{nm}{ci}')
            CH.append(dict(
                v_tm=sbf([128, 512], BF16, 'v_tm'), k_tm=sbf([128, 256], F32, 'k_tm'),
                sp=sbf([128, 512], F32, 'sp'), glrT=sbf([32, 128], F32, 'glrT'),
                Ep=sbf([128, 2, 128], F32, 'Ep'), Em=sbf([128, 2, 128], F32, 'Em'),
                qe=sbf([128, 2, 128], BF16, 'qe'), ke=sbf([128, 2, 128], BF16, 'ke'),
                keh=sbf([128, 4, 128], BF16, 'keh'), qeh=sbf([128, 4, 128], BF16, 'qeh'),
                kdc=sbf([128, 256], F32, 'kdc'), kdec=sbf([128, 256], BF16, 'kdec'),
                scT=sbf([128, 4, 128], BF16, 'scT'),
                S=[sbf([128, 128], F32, f'S{i}_') for i in range(2)],
                Sb=[sbf([128, 128], BF16, f'Sb{i}_') for i in range(2)], psn=[0]))
        k.memset(oT[:].rearrange("p a b -> p (a b)"), 0.0, ['oTgla'])

        def gla_tile(t, d, ci):
            B = CH[ci]
            K_ = lambda nm: (nm, ci)
            v_tm, k_tm, sp, glrT, Ep, Em = B['v_tm'], B['k_tm'], B['sp'], B['glrT'], B['Ep'], B['Em']
            qe, ke, keh, qeh, kdc, kdec, scT, S, Sb = (B['qe'], B['ke'], B['keh'], B['qeh'], B['kdc'],
                                                       B['kdec'], B['scT'], B['S'], B['Sb'])

            def ps():
                i = 4 * ci + B['psn'][0] % 4
                B['psn'][0] += 1
                return k.banks[i], ('ps', i)

            ts_ = slice(t * 128, (t + 1) * 128)
            kh = ('hT', t // 4)
            pbA, kpA = ps()
            pbk = pbA[:, 0:128].bitcast(BF16)
            for dt_ in range(2):
                k.tr(pbk[:, dt_ * 128:(dt_ + 1) * 128], qkT[:, 2 + dt_, ts_], identb[:],
                     [('qkT', 2 + dt_), 'identb'], [kpA], inc=(dt_ == 1))
            pbV, kpV = ps()
            for c in range(8):
                k.mm(pbV[:, 0:512], hT[:, c, ts_], W[:, c, 0:512], c == 0, c == 7, ['W', kh], [kpV])
            pbG, kpG = ps()
            for c in range(8):
                k.mm(pbG[0:32, 0:128], W[:, c, 512:544], hT[:, c, ts_], c == 0, c == 7, ['W', kh], [kpG])
            yield
            k.cp(k_tm[:], pbk[:, 0:256], [kpA], [K_('k_tm')], e='act')
            k.cp(v_tm[:], pbV[:], [kpV], [K_('v_tm')])
            k.cp(glrT[:], pbG[0:32, 0:128], [kpG], [K_('glrT')])
            yield
            pbZ, kpZ = ps()
            k.mm(pbZ[:], glrT[:], gwp[:], True, False, [K_('glrT'), 'gwp'], [kpZ])
            k.mm(pbZ[:], onesr[:], gbr[:], False, True, ['onesr', 'gbr'], [kpZ])
            yield
            k.act(sp[:], pbZ[:], AF.Exp, [kpZ], [K_('sp')], scale=-1.0)
            k.act(sp[:], sp[:], AF.Ln, [K_('sp')], [K_('sp')], bias=1.0)
            yield
            cum = C('UN16') if d == 0 else C('LN16')
            cum2 = C('SLN16') if d == 0 else C('SUN16')
            mask = G['Ub'] if d == 0 else G['Lb']
            pb, kp = ps()
            for dt_ in range(2):
                k.mm(pb[:, dt_ * 128:(dt_ + 1) * 128],
                     sp[:, d * 256 + dt_ * 128:d * 256 + (dt_ + 1) * 128], cum, True, True,
                     [K_('sp'), 'C'], [kp], inc=(dt_ == 1))
            pb2, kp2 = ps()
            k.mm(pb2[:, 0:256], cum2, sp[:, d * 256:(d + 1) * 256], True, True, [K_('sp'), 'C'], [kp2])
            yield
            k.act(Ep[:], pb[:, 0:256].rearrange("p (a b) -> p a b", a=2), AF.Exp, [kp], [K_('Ep')])
            k.act(Em[:], pb[:, 0:256].rearrange("p (a b) -> p a b", a=2), AF.Exp, [kp], [K_('Em')],
                  scale=-1.0)
            k.act(kdc[:], pb2[:, 0:256], AF.Exp, [kp2], [K_('kdc')])
            yield
            k.tt(qe[:], qkT[:, 0:2, ts_], Ep[:], ALU.mult, [('qkT', 0), ('qkT', 1), K_('Ep')], [K_('qe')])
            k.tt(ke[:], qkT[:, 2:4, ts_], Em[:], ALU.mult, [('qkT', 2), ('qkT', 3), K_('Em')], [K_('ke')])
            k.tt(kdec[:], kdc[:], k_tm[:], ALU.mult, [K_('kdc'), K_('k_tm')], [K_('kdec')])
            for hh in range(2):
                hm = C('U')[:, 63:64] if hh == 0 else C('SL')[:, 63:64]
                k.ts(keh[:, hh::2, :], ke[:], hm, None, ALU.mult, None, [K_('ke'), 'C'], [K_('keh')])
                k.ts(qeh[:, hh::2, :], qe[:], hm, None, ALU.mult, None, [K_('qe'), 'C'], [K_('qeh')])
            yield
            pb3, kp3 = ps()
            for h in range(4):
                k.mm(pb3[:, h * 128:(h + 1) * 128], keh[:, h, :], qe[:, h // 2, :], True, True,
                     [K_('keh'), K_('qe')], [kp3], inc=(h == 3))
            yield
            k.tt(scT[:], pb3[:].rearrange("p (h i) -> p h i", h=4),
                 mask[:].unsqueeze(1).broadcast_to([128, 4, 128]), ALU.mult, [kp3, 'Ub', 'Lb'],
                 [K_('scT')])
            yield
            pb4, kp4 = ps()
            for h in range(4):
                k.mm(pb4[:, h * 128:(h + 1) * 128], v_tm[:, h * 128:(h + 1) * 128], scT[:, h, :],
                     True, False, [K_('v_tm'), K_('scT')], [kp4])
                k.mm(pb4[:, h * 128:(h + 1) * 128], Sb[h // 2][:, :], qeh[:, h, :], False, True,
                     [K_('Sb%d' % (h // 2)), K_('qeh')], [kp4], inc=(h == 3))
            pb5 = []
            for pr in range(2):
                p5, k5 = ps()
                for hh in range(2):
                    h = pr * 2 + hh
                    k.mm(p5[hh * 64:(hh + 1) * 64, 0:128], kdec[:, h * 64:(h + 1) * 64],
                         v_tm[:, h * 128:(h + 1) * 128], True, True, [K_('kdec'), K_('v_tm')], [k5],
                         inc=(hh == 1))
                pb5.append((p5, k5))
            yield
            k.tt(oT[:, :, ts_], oT[:, :, ts_], pb4[:].rearrange("p (h i) -> p h i", h=4), ALU.add,
                 [kp4, 'oTgla'], ['oTgla'])
            dlcol = 127 if d == 0 else 0
            for pr in range(2):
                p5, k5 = pb5[pr]
                k.stt(S[pr][:], S[pr][:], Ep[:, pr, dlcol:dlcol + 1], p5[:, 0:128], ALU.mult, ALU.add,
                      [K_('S%d' % pr), K_('Ep'), k5], [K_('S%d' % pr)])
                k.cp(Sb[pr][:], S[pr][:], [K_('S%d' % pr)], [K_('Sb%d' % pr)], e='act')
            yield

        def chain(ci, d):
            B = CH[ci]
            for si, (t0, ln) in enumerate(SEQS):
                tiles = list(range(t0 // 128, (t0 + ln) // 128))
                for pr in range(2):
                    if si < 2:
                        k.memset(B['S'][pr][:], 0.0, [('S%d' % pr, ci)])
                    else:
                        k.dma('sp', B['S'][pr][:],
                              G['st_gla'][l, d, pr * 2:pr * 2 + 2].rearrange("h a b -> (h a) b"),
                              (), [('S%d' % pr, ci)])
                    k.cp(B['Sb'][pr][:], B['S'][pr][:], [('S%d' % pr, ci)], [('Sb%d' % pr, ci)], e='act')
                yield
                for t in (tiles if d == 0 else tiles[::-1]):
                    yield from gla_tile(t, d, ci)
                if si < 2:
                    for pr in range(2):
                        k.dma('sp',
                              G['ns_gla'][si, l, d, pr * 2:pr * 2 + 2].rearrange("h a b -> (h a) b"),
                              B['S'][pr][:], [('S%d' % pr, ci)], [('nsgla', si, l, d, pr)])

        alive = [chain(0, 0), chain(1, 1)]
        while alive:
            for g_ in list(alive):
                try:
                    next(g_)
                except StopIteration:
                    alive.remove(g_)
        k.barrier()
      dump('oT_gla', oT[:, :, 0:128], ['oTgla'])
      L['load_w'](1056, 512)
      with contextlib.ExitStack() as e3:
        L['headnorm'](e3, oT, 'oTgla', 4, G['glan'][:, l:l + 1], 0, 0)
        k.barrier()


def gdn_phase(k, l, G, L):
    C = G['C']
    catT = G['catT']
    proj_fm = L['proj_fm']
    proj_tm = L['proj_tm']
    dump = L['dump']
    identb = G['identb']
    onesb = G['onesb']
    convw = G['convw']
    QKV0, GA0, DG0 = 2592, 4128, 4144
    with contextlib.ExitStack() as e0:
        dtb = k.sb(e0, [128, 8], F32, 'dtb')
        nea = k.sb(e0, [128, 8], F32, 'nea')
        tmp8 = k.sb(e0, [128, 8], F32, 'tmp8')
        gall = k.sb(e0, [128, NT, 8], F32, 'gall')
        ball = k.sb(e0, [128, NT, 8], F32, 'ball')
        gcs = k.sb(e0, [128, NT, 8], F32, 'gcs')
        ngc = k.sb(e0, [128, NT, 8], F32, 'ngc')
        cbs = k.sb(e0, [128, NT, 8], F32, 'cbs')
        kdcs = k.sb(e0, [128, NT, 8], F32, 'kdcs')
        MK = k.sb(e0, [128, 4, 128], F32, 'MK')
        k.dma('sp', MK[:].rearrange("p a b -> p (a b)"), G['consts'][:, NCONST_P:NCONST_P + 512], (),
              ['MK'])
        k.dma('sp', dtb[:], G['gdn_dt_bias'][l].rearrange("a b -> (a b)").partition_broadcast(128),
              (), ['dtb'])
        k.dma('sp', nea[:], G['gdn_a_log'][l].rearrange("a b -> (a b)").partition_broadcast(128),
              (), ['nea'])
        k.act(nea[:], nea[:], AF.Exp, ['nea'], ['nea'])
        k.ts(nea[:], nea[:], -1.0, None, ALU.mult, None, ['nea'], ['nea'])
        with contextlib.ExitStack() as ew:
            L['alloc_w'](ew, 16)
            L['load_w'](GA0, 16, 0)
            Wg = L['WH']['W']
            hT_ = L['hT']
            pb, kp = k.ps()
            for t in range(NT):
                for c in range(8):
                    k.mm(pb[:, t * 16:(t + 1) * 16], hT_[:, c, t * 128:(t + 1) * 128], Wg[:, c, 0:16],
                         c == 0, c == 7, ['W', ('hT', t // 4)], [kp])
            pv = pb[:, 0:192].rearrange("p (t c) -> p t c", c=16)
            tmpA = k.sb(ew, [128, NT, 8], F32, 'tmpA')
            bc8 = lambda ap: ap.unsqueeze(1).broadcast_to([128, NT, 8])
            k.tt(tmpA[:], pv[:, :, 0:8], bc8(dtb[:]), ALU.add, [kp, 'dtb'], ['tmpA'])
            k.act(tmpA[:], tmpA[:], AF.Exp, ['tmpA'], ['tmpA'])
            k.act(tmpA[:], tmpA[:], AF.Ln, ['tmpA'], ['tmpA'], bias=1.0)
            k.tt(gall[:], tmpA[:], bc8(nea[:]), ALU.mult, ['tmpA', 'nea'], ['gall'])
            k.act(ball[:], pv[:, :, 8:16], AF.Sigmoid, [kp], ['ball'])
            pb2, kp2 = k.ps()
            for qi, (cm, lo) in enumerate(((C('U'), 0), (C('L'), 4), (C('SL'), 0), (C('SU'), 4))):
                k.mm(pb2[:, qi * 48:(qi + 1) * 48], cm, gall[:, :, lo:lo + 4], True, True, ['C', 'gall'],
                     [kp2])
            v4 = lambda ap: ap.rearrange("p (t c) -> p t c", c=4)
            k.cp(gcs[:, :, 0:4], v4(pb2[:, 0:48]), [kp2], ['gcs'])
            k.cp(gcs[:, :, 4:8], v4(pb2[:, 48:96]), [kp2], ['gcs'])
            k.act(kdcs[:, :, 0:4], v4(pb2[:, 96:144]), AF.Exp, [kp2], ['kdcs'])
            k.act(kdcs[:, :, 4:8], v4(pb2[:, 144:192]), AF.Exp, [kp2], ['kdcs'])
            k.ts(ngc[:], gcs[:], -1.0, None, ALU.mult, None, ['gcs'], ['ngc'])
            k.act(tmpA[:], gcs[:], AF.Exp, ['gcs'], ['tmpA'])
            k.tt(cbs[:], tmpA[:], ball[:], ALU.mult, ['tmpA', 'ball'], ['cbs'])
            gsT = k.sb(ew, [96, 128], F32, 'gsT')
            for qi, (src_, key_) in enumerate(((gcs, 'gcs'), (ball, 'ball'))):
                pb, kp = k.ps()
                k.tr(pb[0:96, 0:128], src_[:].rearrange("p a b -> p (a b)"), C('ident'), [key_, 'C'],
                     [kp])
                k.cp(gsT[:], pb[0:96, 0:128], [kp], ['gsT'])
                k.dma('sp', G['gsc'][l, qi], gsT[:], ['gsT'], ['gsc'])
            k.barrier()

        for hp in range(2):
            with contextlib.ExitStack() as e1:
                qkvT = k.sb(e1, [128, 2, 3, T], BF16, 'qkvT')
                oT = k.sb(e1, [128, 2, T], F32, 'oTgdn')
                with contextlib.ExitStack() as e3:
                    L['alloc_w'](e3, 512)
                    Wp = L['WH']['W']
                    hT_ = L['hT']
                    NPS = 3
                    PS_ = []
                    for i in range(NPS):
                        PS_.append(dict(raw=k.sb(e3, [128, 512], F32, f'raw{i}'),
                                        acc=k.sb(e3, [128, 512], F32, f'acc{i}'),
                                        sqb=k.sb(e3, [128, 512], BF16, f'sqb{i}'),
                                        rsb=k.sb(e3, [128, 512], F32, f'rsb{i}')))

                    def prep_item(sl, hh, w3, b):
                        h = hp * 2 + hh
                        P_ = PS_[sl]
                        raw, acc, sqb, rsb = P_['raw'], P_['acc'], P_['sqb'], P_['rsb']
                        K_ = lambda nm: (nm, sl)
                        bkA, kyA = k.banks[2 * sl], ('ps', 2 * sl)
                        bkB, kyB = k.banks[2 * sl + 1], ('ps', 2 * sl + 1)
                        bs = slice(b * 512, (b + 1) * 512)
                        for c in range(8):
                            k.mm(bkA[:, :], Wp[:, c, w3 * 128:(w3 + 1) * 128], hT_[:, c, bs], c == 0, c == 7,
                                 ['W', ('hT', b)], [kyA])
                        yield
                        if w3 == 3:
                            k.act(catT[:, 8 + h, bs], bkA[:], AF.Silu, [kyA], [('catT', 8 + h)])
                            yield
                            return
                        ct = w3 * 4 + h
                        k.cp(raw[:], bkA[:], [kyA], [K_('raw')], e='act')
                        yield
                        nseg, Ls = (2, 256) if b == 0 else (8, 64)
                        xv = raw[:].rearrange("p (s l) -> p s l", s=nseg)
                        av = acc[:].rearrange("p (s l) -> p s l", s=nseg)
                        k.ts(acc[:], raw[:], convw[:, l, 1, ct:ct + 1], None, ALU.mult, None,
                             [K_('raw'), 'convw'], [K_('acc')])
                        k.stt(av[:, :, 1:Ls], xv[:, :, 0:Ls - 1], convw[:, l, 0, ct:ct + 1], av[:, :, 1:Ls],
                              ALU.mult, ALU.add, [K_('raw'), 'convw', K_('acc')], [K_('acc')])
                        k.stt(av[:, :, 0:Ls - 1], xv[:, :, 1:Ls], convw[:, l, 2, ct:ct + 1],
                              av[:, :, 0:Ls - 1], ALU.mult, ALU.add, [K_('raw'), 'convw', K_('acc')],
                              [K_('acc')])
                        yield
                        if w3 == 2:
                            k.act(qkvT[:, hh, 2, bs], acc[:], AF.Silu, [K_('acc')], ['qkvT'])
                            yield
                            return
                        k.act(acc[:], acc[:], AF.Silu, [K_('acc')], [K_('acc')])
                        k.act(sqb[:], acc[:], AF.Square, [K_('acc')], [K_('sqb')])
                        yield
                        k.mm(bkB[:], onesb[:], sqb[:], True, True, [K_('sqb'), 'onesb'], [kyB])
                        yield
                        k.act(rsb[:], bkB[:], AF.Ln, [kyB], [K_('rsb')], bias=EPS)
                        k.act(rsb[:], rsb[:], AF.Exp, [K_('rsb')], [K_('rsb')], scale=-0.5)
                        yield
                        sc = 128 ** -0.5 if w3 == 0 else 1.0
                        k.stt(qkvT[:, hh, w3, bs], acc[:], sc, rsb[:], ALU.mult, ALU.mult,
                              [K_('acc'), K_('rsb')], ['qkvT'])
                        yield

                    for hh in range(2):
                        h = hp * 2 + hh
                        L['load_w'](QKV0 + h * 128, 128, 0)
                        L['load_w'](QKV0 + 512 + h * 128, 128, 128)
                        L['load_w'](QKV0 + 1024 + h * 128, 128, 256)
                        L['load_w'](DG0 + h * 128, 128, 384)
                        items = [(w3, b) for w3 in range(4) for b in range(3)]
                        free = list(range(NPS))
                        running = []
                        while items or running:
                            while items and free:
                                sl = free.pop(0)
                                w3, b = items.pop(0)
                                running.append((sl, prep_item(sl, hh, w3, b)))
                            for (sl, g_) in list(running):
                                try:
                                    next(g_)
                                except StopIteration:
                                    running.remove((sl, g_))
                                    free.append(sl)
                        k.barrier()
                with contextlib.ExitStack() as e2:
                    kq = ['qkvT']
                    NSLOT = 6
                    SL = []
                    for sl in range(NSLOT):
                        bf = lambda nm, n=128: k.sb(e2, [128, n], BF16, f'{nm}{sl}')
                        f3 = lambda nm, n=128: k.sb(e2, [128, n], F32, f'{nm}{sl}')
                        dgA = f3('dgA', 256)
                        decA = f3('decA', 128)
                        rowsA = f3('rowsA', 256)
                        rowsB = f3('rowsB', 256)
                        dgb = dgA[:].bitcast(BF16)
                        decb = decA[:].bitcast(BF16)
                        SL.append(dict(
                            kv_tm=bf('kv_tm', 256), vb=bf('vb'), kbg=bf('kbg'), kd=bf('kd'),
                            dg=dgA, erow=f3('erow'), decT=decA, BS=f3('BS'),
                            t1=f3('t1'), APt=bf('APt', 256), TP=bf('TP', 384),
                            PT=[bf('PTa'), bf('PTb')], PT0=bf('PT0f'),
                            TiT=dgb[:, 0:128], BTm=dgb[:, 128:256], Yb=dgb[:, 256:384],
                            wTn=dgb[:, 384:512], vnew=decb[:, 0:128], qg=decb[:, 128:256],
                            St=f3('St'), Sb=bf('Sbg'), rows=[rowsA, rowsB]))
                    k.memset(oT[:].rearrange("p a b -> p (a b)"), 0.0, ['oTgdn'])

                    def load_rows(sl, t, d, hh, par):
                        dh = d * 4 + hp * 2 + hh
                        rb = SL[sl]['rows'][par]
                        k.dma('pool', rb[:, 0:128], G['gsc'][l, 0, t * 8 + dh, :].partition_broadcast(128),
                              ['gsc'], [('rows', sl, par)])
                        k.dma('pool', rb[:, 128:256], G['gsc'][l, 1, t * 8 + dh, :].partition_broadcast(128),
                              ['gsc'], [('rows', sl, par)])

                    def gdn_tile(t, d, sl, hh, par, nxt):
                        h = hp * 2 + hh
                        B = SL[sl]
                        K_ = lambda nm: (nm, sl)
                        kv_tm, vb, kbg, kd, dg = B['kv_tm'], B['vb'], B['kbg'], B['kd'], B['dg']
                        erow, decT, BS, t1 = B['erow'], B['decT'], B['BS'], B['t1']
                        APt = B['APt']
                        AT, P0 = APt[:, 0:128], APt[:, 128:256]
                        TP = B['TP']
                        Ti = TP[:, 128:256]
                        P = [TP[:, 0:128], TP[:, 256:384]]
                        PT, PT0, TiT, BTm, Yb = B['PT'], B['PT0'], B['TiT'], B['BTm'], B['Yb']
                        wTn, vnew, qg, St, Sb = B['wTn'], B['vnew'], B['qg'], B['St'], B['Sb']
                        kDG, kDE = K_('dg'), K_('decT')
                        ts_ = slice(t * 128, (t + 1) * 128)
                        dh = d * 4 + h
                        NEG = C('NEGf') if d == 0 else C('NEGb')
                        strict = C('SU') if d == 0 else C('SL')
                        bk, ky = k.banks[sl], ('ps', sl)
                        qT_, kT_, vT_ = qkvT[:, hh, 0, ts_], qkvT[:, hh, 1, ts_], qkvT[:, hh, 2, ts_]
                        pbk = bk[:, 0:128].bitcast(BF16)
                        k.tr(pbk[:, 0:128], kT_, identb[:], kq + ['identb'], [ky], inc=False)
                        k.tr(pbk[:, 128:256], vT_, identb[:], kq + ['identb'], [ky], inc=False)
                        k.mm(bk[:, 128:384], kT_, qkvT[:, hh, 0:2, ts_], True, True, kq, [ky])
                        grow, brow = B['rows'][par][:, 0:128], B['rows'][par][:, 128:256]
                        kRW = ('rows', sl, par)
                        if nxt is not None:
                            load_rows(sl, nxt[0], nxt[1], hh, 1 - par)
                        yield
                        k.cp(kv_tm[:], pbk[:, 0:256], [ky], [K_('kv_tm')], e='act')
                        k.act(erow[:], grow, AF.Exp, [kRW], [K_('erow')])
                        k.tt(t1[:], grow, NEG, ALU.add, [kRW, 'C'], [K_('t1')])
                        k.stt(BS[:], brow, -1.0, strict, ALU.mult, ALU.mult, [kRW, 'C'], [K_('BS')])
                        yield
                        k.act(decT[:], t1[:], AF.Exp, [K_('t1'), 'ngc'], [kDE], bias=ngc[:, t, dh:dh + 1])
                        k.act(vb[:], kv_tm[:, 128:256], AF.Identity, [K_('kv_tm'), 'ball'], [K_('vb')],
                              scale=ball[:, t, dh:dh + 1])
                        k.act(kbg[:], kv_tm[:, 0:128], AF.Identity, [K_('kv_tm'), 'cbs'], [K_('kbg')],
                              scale=cbs[:, t, dh:dh + 1])
                        k.act(kd[:], kv_tm[:, 0:128], AF.Identity, [K_('kv_tm'), 'kdcs'], [K_('kd')],
                              scale=kdcs[:, t, dh:dh + 1])
                        yield
                        k.tt(APt[:].rearrange("p (a b) -> p a b", a=2),
                             bk[:, 128:384].rearrange("p (a b) -> p a b", a=2),
                             decT[:].unsqueeze(1).broadcast_to([128, 2, 128]), ALU.mult, [ky, kDE],
                             [K_('AT'), K_('P0')])
                        k.tt(P0, P0, BS[:], ALU.mult, [K_('P0'), K_('BS')], [K_('P0')])
                        yield
                        ptb = bk[:, 384:448].bitcast(BF16)
                        k.tr(ptb[:, 0:128], P0[:], identb[:], [K_('P0'), 'identb'], [ky])
                        k.tt(P[0][:], P0[:], MK[:, 0, :], ALU.mult, [K_('P0'), 'MK'], [K_('Pa')])
                        k.tt(Ti[:], P[0][:], identb[:], ALU.add, [K_('Pa'), 'identb'], [K_('Ti')])
                        yield
                        k.cp(PT0[:], ptb[:, 0:128], [ky], [K_('PT0')], e='act')
                        yield
                        k.tt(PT[0][:], PT0[:], MK[:, 0, :], ALU.mult, [K_('PT0'), 'MK'], [K_('PTa')])
                        yield
                        pk = ['Pa', 'Pb']
                        ptk = ['PTa', 'PTb']
                        kTP = [K_('Pa'), K_('Ti'), K_('Pb')]
                        for s_ in range(4):
                            c_, n_ = s_ % 2, (s_ + 1) % 2
                            cTi = (256, 384) if c_ == 0 else (128, 256)
                            cPP = (128, 256) if c_ == 0 else (256, 384)
                            if 1 <= s_ < 3:
                                rhs2 = TP[:, 0:256] if c_ == 0 else TP[:, 128:384]
                                k.mm(bk[:, 128:384], PT[c_][:], rhs2, True, True, [K_(ptk[c_])] + kTP, [ky],
                                     inc=False)
                            elif s_ == 0:
                                k.mm(bk[:, cPP[0]:cPP[1]], PT[c_][:], P[c_], True, True,
                                     [K_(ptk[c_]), K_(pk[c_])], [ky], inc=False)
                            else:
                                k.mm(bk[:, cTi[0]:cTi[1]], PT[c_][:], Ti, True, True,
                                     [K_(ptk[c_]), K_('Ti')], [ky])
                            if s_ < 3:
                                k.mm(bk[:, 0:128], P[c_], PT[c_][:], True, True,
                                     [K_(ptk[c_]), K_(pk[c_])], [ky])
                            yield
                            if s_ < 3:
                                k.cp(P[n_], bk[:, cPP[0]:cPP[1]], [ky], [K_(pk[n_])], e='act')
                            if s_ >= 1:
                                k.tt(Ti, Ti, bk[:, cTi[0]:cTi[1]], ALU.add, [K_('Ti'), ky], [K_('Ti')])
                            if s_ < 3:
                                k.cp(PT[n_][:], bk[:, 0:128], [ky], [K_(ptk[n_])], e='act')
                            yield
                        for lv in range(3):
                            ptb2 = bk[:, 0:64].bitcast(BF16)
                            k.tr(ptb2[:, 0:128], Ti[:], identb[:], [K_('Ti'), 'identb'], [ky])
                            k.tt(BTm, PT0[:], MK[:, 1 + lv, :], ALU.mult, [K_('PT0'), 'MK'], [kDG])
                            k.mm(bk[:, 128:256], BTm, Ti[:], True, True, [kDG, K_('Ti')], [ky])
                            yield
                            k.cp(TiT, ptb2[:, 0:128], [ky], [kDG], e='act')
                            k.cp(Yb, bk[:, 128:256], [ky], [kDG], e='act')
                            yield
                            k.mm(bk[:, 256:384], TiT, Yb, True, True, [kDG], [ky])
                            yield
                            k.tt(Ti[:], Ti[:], bk[:, 256:384], ALU.add, [K_('Ti'), ky], [K_('Ti')])
                            yield
                        k.mm(bk[:, 0:128], kbg[:], Ti[:], True, True, [K_('kbg'), K_('Ti')], [ky])
                        k.mm(bk[:, 128:256], Ti[:], vb[:], True, False, [K_('Ti'), K_('vb')], [ky], inc=True)
                        k.tt(qg, qT_, erow[:], ALU.mult, kq + [K_('erow')], [kDE])
                        yield
                        k.act(wTn, bk[:, 0:128], AF.Identity, [ky], [kDG], scale=-1.0)
                        yield
                        k.mm(bk[:, 128:256], wTn, Sb[:], False, True, [kDG, K_('Sbg')], [ky])
                        k.mm(bk[:, 256:384], Sb[:], qg, True, False, [K_('Sbg'), kDE], [ky], inc=True)
                        yield
                        k.cp(vnew, bk[:, 128:256], [ky], [kDE])
                        yield
                        k.mm(bk[:, 256:384], vnew, AT[:], False, True, [kDE, K_('AT')], [ky])
                        k.mm(bk[:, 384:512], kd[:], vnew, True, True, [K_('kd'), kDE], [ky])
                        yield
                        k.tt(oT[:, hh, ts_], oT[:, hh, ts_], bk[:, 256:384], ALU.add, [ky, 'oTgdn'],
                             ['oTgdn'])
                        dlc = 127 if d == 0 else 0
                        k.stt(St[:], St[:], erow[:, dlc:dlc + 1], bk[:, 384:512], ALU.mult, ALU.add,
                              [K_('St'), K_('erow'), ky], [K_('St')])
                        k.cp(Sb[:], St[:], [K_('St')], [K_('Sbg')], e='act')
                        yield

                    def chain(sl, hh, items):
                        h = hp * 2 + hh
                        St, Sb = SL[sl]['St'], SL[sl]['Sb']
                        steps = []
                        for (si, d) in items:
                            t0, ln = SEQS[si]
                            tiles = list(range(t0 // 128, (t0 + ln) // 128))
                            for t in (tiles if d == 0 else tiles[::-1]):
                                steps.append((t, d))
                        load_rows(sl, steps[0][0], steps[0][1], hh, 0)
                        idx = 0
                        for (si, d) in items:
                            t0, ln = SEQS[si]
                            ntl = ln // 128
                            if si < 2:
                                k.memset(St[:], 0.0, [('St', sl)])
                            else:
                                k.dma('sp', St[:], G['st_gdn'][l, d, h], (), [('St', sl)])
                            k.cp(Sb[:], St[:], [('St', sl)], [('Sbg', sl)], e='act')
                            yield
                            for _ in range(ntl):
                                t, d_ = steps[idx]
                                nxt = steps[idx + 1] if idx + 1 < len(steps) else None
                                yield from gdn_tile(t, d_, sl, hh, idx % 2, nxt)
                                idx += 1
                            if si < 2:
                                k.dma('sp', G['ns_gdn'][si, l, d, h], St[:], [('St', sl)],
                                      [('nsgdn', si, l, d, h)])

                    plans = [[(2, 0)], [(2, 1)], [(0, 0), (0, 1), (1, 0), (1, 1)]]
                    gens = []
                    for hh in range(2):
                        for pi, pl in enumerate(plans):
                            gens.append(chain(hh * 3 + pi, hh, pl))
                    alive = list(gens)
                    while alive:
                        for g_ in list(alive):
                            try:
                                next(g_)
                            except StopIteration:
                                alive.remove(g_)
                    k.barrier()
                if hp == 0:
                    dump('oT_gdn', oT[:, 0, 0:128], ['oTgdn'])
                with contextlib.ExitStack() as e4:
                    NS_ = 3
                    HS = [dict(sqh=k.sb(e4, [128, 512], BF16, f'sqh{i}'), rs=k.sb(e4, [128, 512], F32, f'rsh{i}'))
                          for i in range(NS_)]

                    def hn_item(sl, hh, b):
                        h = hp * 2 + hh
                        sqh, rs = HS[sl]['sqh'], HS[sl]['rs']
                        K_ = lambda nm: (nm, sl)
                        bkA, kyA = k.banks[sl], ('ps', sl)
                        bs = slice(b * 512, (b + 1) * 512)
                        k.act(sqh[:], oT[:, hh, bs], AF.Square, ['oTgdn'], [K_('sqh')])
                        yield
                        k.mm(bkA[:], onesb[:], sqh[:], True, True, [K_('sqh'), 'onesb'], [kyA])
                        yield
                        k.act(rs[:], bkA[:], AF.Ln, [kyA], [K_('rsh')], scale=1.0 / 128, bias=EPS)
                        k.act(rs[:], rs[:], AF.Exp, [K_('rsh')], [K_('rsh')], scale=-0.5)
                        yield
                        k.stt(rs[:], rs[:], G['gdnn'][:, l:l + 1], catT[:, 8 + h, bs], ALU.mult, ALU.mult,
                              [K_('rsh'), 'gdnn', ('catT', 8 + h)], [K_('rsh')])
                        k.tt(catT[:, 8 + h, bs], oT[:, hh, bs], rs[:], ALU.mult, ['oTgdn', K_('rsh')],
                             [('catT', 8 + h)])
                        yield

                    items = [(hh, b) for hh in range(2) for b in range(3)]
                    free = list(range(NS_))
                    running = []
                    while items or running:
                        while items and free:
                            sl = free.pop(0)
                            hh, b = items.pop(0)
                            running.append((sl, hn_item(sl, hh, b)))
                        for (sl, g_) in list(running):
                            try:
                                next(g_)
                            except StopIteration:
                                running.remove((sl, g_))
                                free.append(sl)
                    k.barrier()


def s5_part1(k, l, G, L):
    hT = L['hT']
    proj_fm = L['proj_fm']
    with contextlib.ExitStack() as e1:
        L['alloc_w'](e1, 1024)
        L['load_w'](1568, 1024)
        tmpb = k.sb(e1, [128, 8, 512], BF16, 'tmpb')
        for b in range(3):
            bs = slice(b * 512, (b + 1) * 512)
            for i in range(8):
                pb, kp = proj_fm(i * 128, 128, b)
                if i < 4:
                    k.cp(tmpb[:, i, :], pb[:], [kp], ['tmpb'])
                else:
                    k.act(tmpb[:, i, :], pb[:], AF.Silu, [kp], ['tmpb'])
            k.cp(hT[:, :, bs], tmpb[:], ['tmpb'], [('hT', b)])
        k.barrier()


def s5_part2(k, l, G, L):
    C = G['C']
    catT = G['catT']
    hT = L['hT']
    dump = L['dump']
    identb = G['identb']
    stg = G['stg']
    PI = math.pi
    with contextlib.ExitStack() as e0:
        SELWb = k.sb(e0, [128, 8, 240], BF16, 'SELWb')
        with contextlib.ExitStack() as et:
            selt = k.sb(et, [128, 1920], F32, 'selt')
            o_, w_ = _CL['SELW']
            k.dma('sp', selt[:], G['consts'][:, o_:o_ + w_], (), ['selt'])
            k.cp(SELWb[:].rearrange("p a b -> p (a b)"), selt[:], ['selt'], ['SELWb'])
            k.barrier()
        AKre = k.sb(e0, [128, 4, 32, 8], F32, 'AKre')
        AKim = k.sb(e0, [128, 4, 32, 8], F32, 'AKim')
        bbre = k.sb(e0, [128, 32, 16], F32, 'bbre')
        bbim = k.sb(e0, [128, 32, 16], F32, 'bbim')
        Cre = k.sb(e0, [128, 32, 16], F32, 'Cre')
        Cim = k.sb(e0, [128, 32, 16], F32, 'Cim')
        LPA = k.sb(e0, [128, 4, 32, 2], F32, 'LPA')
        LPB = k.sb(e0, [128, 4, 32, 2], F32, 'LPB')
        h0 = k.sb(e0, [128, 32, 2], F32, 'h0')
        Fin = k.sb(e0, [128, 2, 2, 32], F32, 'Fin')
        hm0 = C('U')[:, 63:64]
        hm1 = C('SL')[:, 63:64]

        def load_T(dst, src_g_dn, key):
            k.dma('sp', stg[0:32, :].rearrange("g (d n) -> g d n", d=2),
                  src_g_dn.rearrange("d g n -> g d n"), (), ['stg'])
            pb, kp = k.ps()
            k.tr(pb[:, 0:32], stg[0:32, :], C('ident')[0:32, 0:32], ['stg', 'C'], [kp])
            k.cp(dst, pb[:, 0:32], [kp], [key])

        with contextlib.ExitStack() as e1:
            sm = lambda nm: k.sb(e1, [128, 32], F32, nm)
            lamre, lamim, dtt, mre, ang = sm('lamre'), sm('lamim'), sm('dtt'), sm('mre'), sm('ang')
            a1re, a1im, den, qre, qim, t_a, t_b = (sm('a1re'), sm('a1im'), sm('den'), sm('qre'),
                                                   sm('qim'), sm('t_a'), sm('t_b'))
            load_T(lamre[:], G['s5_lam_re'][l], 'lamre')
            load_T(lamim[:], G['s5_lam_im'][l], 'lamim')
            load_T(h0[:, :, 0], G['st_s5re'][l], 'h0')
            load_T(h0[:, :, 1], G['st_s5im'][l], 'h0')
            for d in range(2):
                k.dma('sp', dtt[d * 64:(d + 1) * 64, :], G['s5_log_dt'][l, d].partition_broadcast(64),
                      (), ['dtt'])
            k.act(dtt[:], dtt[:], AF.Exp, ['dtt'], ['dtt'])
            k.tt(mre[:], lamre[:], dtt[:], ALU.mult, ['lamre', 'dtt'], ['mre'])
            k.tt(ang[:], lamim[:], dtt[:], ALU.mult, ['lamim', 'dtt'], ['ang'])
            big = lambda nm, dt=F32: k.sb(e1, [128, 4, 32, 8], dt, nm)
            EA, EM, RR, NI = big('EA'), big('EM'), big('RR'), big('NI', I32)
            o_, w_ = _CL['KTAB']
            KT = G['CS'][:, o_:o_ + w_].rearrange("p (q s) -> p q s", q=4)
            ktb = KT.unsqueeze(2).broadcast_to([128, 4, 32, 8])
            for q in range(4):
                k.tt(EA[:, q], KT[:, q].unsqueeze(1).broadcast_to([128, 32, 8]),
                     ang[:].unsqueeze(2).broadcast_to([128, 32, 8]), ALU.mult, ['C', 'ang'], ['EA'])
                k.tt(EM[:, q], KT[:, q].unsqueeze(1).broadcast_to([128, 32, 8]),
                     mre[:].unsqueeze(2).broadcast_to([128, 32, 8]), ALU.mult, ['C', 'mre'], ['EM'])
            fl = lambda t: t[:].rearrange("p a b c -> p (a b c)")
            k.ts(fl(RR), fl(EA), 1.0 / (2 * PI), None, ALU.mult, None, ['EA'], ['RR'])
            k.cp(fl(NI), fl(RR), ['RR'], ['NI'])
            k.cp(fl(RR), fl(NI), ['NI'], ['RR'])
            k.stt(fl(EA), fl(RR), -2 * PI, fl(EA), ALU.mult, ALU.add, ['RR', 'EA'], ['EA'])
            k.ts(fl(EA), fl(EA), 3.14159, -3.14159, ALU.min, ALU.max, ['EA'], ['EA'])
            k.act(fl(EM), fl(EM), AF.Exp, ['EM'], ['EM'])
            k.act(fl(RR), fl(EA), AF.Sin, ['EA'], ['RR'])
            k.tt(fl(AKim), fl(RR), fl(EM), ALU.mult, ['RR', 'EM'], ['AK'])
            k.act(fl(EA), fl(EA), AF.Abs, ['EA'], ['EA'])
            k.act(fl(RR), fl(EA), AF.Sin, ['EA'], ['RR'], scale=-1.0, bias=PI / 2)
            k.tt(fl(AKre), fl(RR), fl(EM), ALU.mult, ['RR', 'EM'], ['AK'])
            for (r0, s1, s8) in ((0, 0, 7), (64, 7, 0)):
                rs_ = slice(r0, r0 + 64)
                k.cp(a1re[rs_, :], AKre[rs_, 1, :, s1], ['AK'], ['a1re'])
                k.cp(a1im[rs_, :], AKim[rs_, 1, :, s1], ['AK'], ['a1im'])
                k.cp(LPA[rs_, 0, :, 0], AKre[rs_, 1, :, s8], ['AK'], ['LPA'])
                k.cp(LPB[rs_, 0, :, 1], AKim[rs_, 1, :, s8], ['AK'], ['LPB'])
            for q_ in range(4):
                if q_ > 0:
                    k.tt(t_a[:], LPA[:, q_ - 1, :, 0], LPA[:, q_ - 1, :, 0], ALU.mult, ['LPA'], ['t_a'])
                    k.tt(t_b[:], LPB[:, q_ - 1, :, 1], LPB[:, q_ - 1, :, 1], ALU.mult, ['LPB'], ['t_b'])
                    k.tt(LPA[:, q_, :, 0], t_a[:], t_b[:], ALU.subtract, ['t_a', 't_b'], ['LPA'])
                    k.tt(t_a[:], LPA[:, q_ - 1, :, 0], LPB[:, q_ - 1, :, 1], ALU.mult, ['LPA', 'LPB'], ['t_a'])
                    k.ts(LPB[:, q_, :, 1], t_a[:], 2.0, None, ALU.mult, None, ['t_a'], ['LPB'])
                k.cp(LPA[:, q_, :, 1], LPA[:, q_, :, 0], ['LPA'], ['LPA'])
                k.ts(LPB[:, q_, :, 0], LPB[:, q_, :, 1], -1.0, None, ALU.mult, None, ['LPB'], ['LPB'])
            k.ts(a1re[:], a1re[:], -1.0, None, ALU.add, None, ['a1re'], ['a1re'])
            k.tt(den[:], lamre[:], lamre[:], ALU.mult, ['lamre'], ['den'])
            k.tt(t_a[:], lamim[:], lamim[:], ALU.mult, ['lamim'], ['t_a'])
            k.tt(den[:], den[:], t_a[:], ALU.add, ['den', 't_a'], ['den'])
            k.recip(den[:], den[:], ['den'], ['den'])
            k.tt(t_a[:], a1re[:], lamre[:], ALU.mult, ['a1re', 'lamre'], ['t_a'])
            k.tt(t_b[:], a1im[:], lamim[:], ALU.mult, ['a1im', 'lamim'], ['t_b'])
            k.tt(t_a[:], t_a[:], t_b[:], ALU.add, ['t_a', 't_b'], ['t_a'])
            k.tt(qre[:], t_a[:], den[:], ALU.mult, ['t_a', 'den'], ['qre'])
            k.tt(t_a[:], a1im[:], lamre[:], ALU.mult, ['a1im', 'lamre'], ['t_a'])
            k.tt(t_b[:], a1re[:], lamim[:], ALU.mult, ['a1re', 'lamim'], ['t_b'])
            k.tt(t_a[:], t_a[:], t_b[:], ALU.subtract, ['t_a', 't_b'], ['t_a'])
            k.tt(qim[:], t_a[:], den[:], ALU.mult, ['t_a', 'den'], ['qim'])
            Bre = k.sb(e1, [128, 32, 16], F32, 'Bre')
            Bim = k.sb(e1, [128, 32, 16], F32, 'Bim')
            tb1 = k.sb(e1, [128, 32, 16], F32, 'tb1')
            for d in range(2):
                k.dma('sp', Bre[d * 64:(d + 1) * 64], G['s5_b_re'][l].rearrange("g n h -> n g h"), (),
                      ['Bre'])
                k.dma('sp', Bim[d * 64:(d + 1) * 64], G['s5_b_im'][l].rearrange("g n h -> n g h"), (),
                      ['Bim'])
            qrb = qre[:].unsqueeze(2).broadcast_to([128, 32, 16])
            qib = qim[:].unsqueeze(2).broadcast_to([128, 32, 16])
            k.tt(bbre[:], Bre[:], qrb, ALU.mult, ['Bre', 'qre'], ['bbre'])
            k.tt(tb1[:], Bim[:], qib, ALU.mult, ['Bim', 'qim'], ['tb1'])
            k.tt(bbre[:], bbre[:], tb1[:], ALU.subtract, ['bbre', 'tb1'], ['bbre'])
            k.tt(bbim[:], Bim[:], qrb, ALU.mult, ['Bim', 'qre'], ['bbim'])
            k.tt(tb1[:], Bre[:], qib, ALU.mult, ['Bre', 'qim'], ['tb1'])
            k.tt(bbim[:], bbim[:], tb1[:], ALU.add, ['bbim', 'tb1'], ['bbim'])
            cst = k.sb(e1, [128, 128], F32, 'cst')
            for (src, dst, key) in ((G['s5_c_re'], Cre, 'Cre'), (G['s5_c_im'], Cim, 'Cim')):
                for cc in range(4):
                    k.dma('sp', cst[:].rearrange("a (d n) -> a d n", d=2),
                          src[l].rearrange("d g h n -> (g h) d n")[cc * 128:(cc + 1) * 128], (),
                          ['cst'])
                    pb, kp = k.ps()
                    k.tr(pb[:, 0:128], cst[:], C('ident'), ['cst', 'C'], [kp])
                    k.cp(dst[:, cc * 8:(cc + 1) * 8, :].rearrange("p g h -> p (g h)"), pb[:, 0:128],
                         [kp], [key])
            k.barrier()

        agen = None
        if l + 1 < DEPTH:
            wabp = [k.sb(e0, [128, 8, 128], F32, f'wabp{i}') for i in range(2)]
            agen = G['ada_gen'](l + 1, wabp)

        def tick():
            nonlocal agen
            if agen is not None:
                try:
                    next(agen)
                except StopIteration:
                    agen = None

        NSL = 198
        QB = [0, 34, 68]
        for ct in range(4):
            g0 = ct * 8
            gs_ = slice(g0, g0 + 8)
            with contextlib.ExitStack() as e1:
                wt = lambda nm: k.sb(e1, [128, 8, 128], BF16, nm)
                W1re, W1im, Wcre, Wcim, Wtp = wt('W1re'), wt('W1im'), wt('Wcre'), wt('Wcim'), wt('Wtp')
                with contextlib.ExitStack() as e2:
                    V = [k.sb(e2, [128, 8, 8, 16], F32, f'V{i}') for i in range(6)]
                    f2 = lambda t: t[:].rearrange("p a b c -> p a (b c)")

                    def cmul(outre, outim, q, X_re, X_im, xdim, neg_im=False, tmp=None):
                        are = AKre[:, q, gs_, :].unsqueeze(3).broadcast_to([128, 8, 8, 16])
                        aim = AKim[:, q, gs_, :].unsqueeze(3).broadcast_to([128, 8, 8, 16])
                        xre = X_re[:, gs_, :].unsqueeze(2).broadcast_to([128, 8, 8, 16])
                        xim = X_im[:, gs_, :].unsqueeze(2).broadcast_to([128, 8, 8, 16])
                        rk = ['AK', 'bbre', 'bbim', 'Cre', 'Cim']
                        k.tt(outre, are, xre, ALU.mult, rk, ['Vo'])
                        k.tt(tmp[:], aim, xim, ALU.mult, rk, ['Vt'])
                        k.tt(outre, outre, tmp[:], ALU.subtract, ['Vo', 'Vt'], ['Vo'])
                        k.tt(outim, are, xim, ALU.mult, rk, ['Vo2'])
                        k.tt(tmp[:], aim, xre, ALU.mult, rk, ['Vt'])
                        if neg_im:
                            k.stt(outim, outim, -1.0, tmp[:], ALU.mult, ALU.subtract,
                                  ['Vo2', 'Vt'], ['Vo2'])
                        else:
                            k.tt(outim, outim, tmp[:], ALU.add, ['Vo2', 'Vt'], ['Vo2'])

                    cmul(V[0][:], V[1][:], 0, bbre, bbim, 16, tmp=V[5])
                    for ri, (Vs, Wd) in enumerate(((V[0], W1re), (V[1], W1im))):
                        for half in range(2):
                            pb, kp = k.ps()
                            for gi in range(4):
                                g = half * 4 + gi
                                k.tr(pb[:, gi * 128:(gi + 1) * 128],
                                     Vs[:, g].rearrange("p s h -> p (s h)"), C('ident'),
                                     ['Vo', 'Vo2', 'C'], [kp])
                            k.cp(Wd[:, half * 4:half * 4 + 4, :].rearrange("p g m -> p (g m)"), pb[:],
                                 [kp], ['W1'], e='act' if half else 'dve')
                    k.barrier()
                    cmul(Wcre[:].rearrange("p g (s h) -> p g s h", s=8),
                         Wcim[:].rearrange("p g (s h) -> p g s h", s=8), 1, Cre, Cim, 16,
                         neg_im=True, tmp=V[5])
                    k.barrier()
                    cmul(V[0][:], V[1][:], 2, bbre, bbim, 16, tmp=V[5])
                    k.barrier()
                    cmul(V[2][:], V[3][:], 3, Cre, Cim, 16, neg_im=True, tmp=V[5])
                    k.barrier()
                    tF = k.sb(e2, [128, 512], F32, 'tF')
                    tB = k.sb(e2, [128, 512], F32, 'tB')
                    for half in range(2):
                        gsl = slice(half * 4, half * 4 + 4)
                        pF, kF = k.ps()
                        pB, kB = k.ps()
                        for (hm, pp, kk_) in ((hm0, pF, kF), (hm1, pB, kB)):
                            k.ts(f2(V[4]), f2(V[0]), hm, None, ALU.mult, None, ['Vo', 'Vo2', 'C'], ['V4'])
                            k.ts(f2(V[5]), f2(V[1]), hm, None, ALU.mult, None, ['Vo', 'Vo2', 'C'], ['V5'])
                            for gi in range(4):
                                g = half * 4 + gi
                                k.mm(pp[:, gi * 128:(gi + 1) * 128],
                                     V[4][:, g].rearrange("p s h -> p (s h)"),
                                     V[2][:, g].rearrange("p s h -> p (s h)"), True, False,
                                     ['V4', 'Vo', 'Vo2'], [kk_])
                                k.mm(pp[:, gi * 128:(gi + 1) * 128],
                                     V[5][:, g].rearrange("p s h -> p (s h)"),
                                     V[3][:, g].rearrange("p s h -> p (s h)"), False, True,
                                     ['V5', 'Vo', 'Vo2'], [kk_])
                        k.tt(tF[:].rearrange("p (g m) -> p g m", g=4),
                             pF[:].rearrange("p (g m) -> p g m", g=4),
                             C('TMF').unsqueeze(1).broadcast_to([128, 4, 128]), ALU.mult, [kF, 'C'],
                             ['tF'])
                        k.tt(tB[:].rearrange("p (g m) -> p g m", g=4),
                             pB[:].rearrange("p (g m) -> p g m", g=4),
                             C('TMB').unsqueeze(1).broadcast_to([128, 4, 128]), ALU.mult, [kB, 'C'],
                             ['tB'])
                        k.tt(Wtp[:, gsl, :].rearrange("p g m -> p (g m)"), tF[:], tB[:], ALU.add,
                             ['tF', 'tB'], ['Wtp'])
                    k.barrier()

                with contextlib.ExitStack() as e3:
                    Ublk = k.sb(e3, [128, 8, 192], BF16, 'Ublk')
                    ZP = k.sb(e3, [128, 16, 41, 2], F32, 'ZP')
                    ZS = k.sb(e3, [128, 8, 137, 2], F32, 'ZS')
                    T1 = k.sb(e3, [128, 16, 32, 2], F32, 'T1')
                    T2 = k.sb(e3, [128, 16, 32, 2], F32, 'T2')
                    LAP = k.sb(e3, [128, 4, 16, 2], F32, 'LAP')
                    LBP = k.sb(e3, [128, 4, 16, 2], F32, 'LBP')
                    Xbre = k.sb(e3, [128, 8, 192], BF16, 'Xbre')
                    Xbim = k.sb(e3, [128, 8, 192], BF16, 'Xbim')
                    Ysb = k.sb(e3, [128, 8, 192], BF16, 'Ysb')
                    ytmp = k.sb(e3, [128, T], F32, 'ytmp')
                    ga = k.sb(e3, [128, 512], F32, 'ga')
                    gb_ = k.sb(e3, [128, 512], F32, 'gb_')
                    suP = hT[:, ct, 0:512].rearrange("p (b s) -> p b s", s=8)
                    suS = hT[:, ct, 512:T].rearrange("p (rh s c) -> p c rh s", rh=2, s=8, c=64)
                    for gi in range(8):
                        pb, kp = k.ps()
                        for s_ in range(8):
                            k.mm(pb[:, 0:64], SELWb[:, gi, 112 - 16 * s_:240 - 16 * s_], suP[:, :, s_],
                                 s_ == 0, s_ == 7, ['SELWb', ('hT', 0)], [kp])
                        for s_ in range(8):
                            k.mm(pb[:, 64:192], SELWb[:, gi, 112 - 16 * s_:240 - 16 * s_],
                                 suS[:, :, :, s_], s_ == 0, s_ == 7, ['SELWb', ('hT', 1), ('hT', 2)],
                                 [kp])
                        k.cp(Ublk[:, gi, :], pb[:, 0:192], [kp], ['Ublk'], e='act')
                    k.memset(ZP[:].rearrange("p a b c -> p (a b c)"), 0.0, ['Zb'])
                    k.memset(ZS[:].rearrange("p a b c -> p (a b c)"), 0.0, ['Zb'])
                    for kk_ in range(4):
                        for sq_ in range(2):
                            k.cp(LAP[:, kk_, sq_ * 8:(sq_ + 1) * 8, :], LPA[:, kk_, gs_, :], ['LPA'], ['LAP'])
                            k.cp(LBP[:, kk_, sq_ * 8:(sq_ + 1) * 8, :], LPB[:, kk_, gs_, :], ['LPB'], ['LAP'])
                    for gp in range(4):
                        pR, kR = k.ps()
                        pI, kI = k.ps()
                        for gi in range(2):
                            g = gp * 2 + gi
                            k.mm(pR[:, gi * 192:(gi + 1) * 192], W1re[:, g, :], Ublk[:, g, :], True, True,
                                 ['W1', 'Ublk'], [kR])
                            k.mm(pI[:, gi * 192:(gi + 1) * 192], W1im[:, g, :], Ublk[:, g, :], True, True,
                                 ['W1', 'Ublk'], [kI])
                        for ri, (pp, kk_) in enumerate(((pR, kR), (pI, kI))):
                            src = pp[:, 0:384].rearrange("p (g c) -> p g c", g=2)
                            eng = 'act' if ri else 'dve'
                            g2 = slice(gp * 2, gp * 2 + 2)
                            ZPg = ZP[:].rearrange("p (s g) q r -> p g s q r", s=2)
                            k.cp(ZPg[0:64, g2, :, 9:41, ri],
                                 src[0:64, :, 0:64].rearrange("p g (s b) -> p g s b", s=2), [kk_], ['Zb'],
                                 e=eng)
                            k.cp(ZS[0:64, g2, 9:137, ri], src[0:64, :, 64:192], [kk_], ['Zb'], e=eng)
                            k.cp(ZPg[64:128, g2, :, 9:41, ri],
                                 src[64:128, :, 0:64].rearrange("p g (s b) -> p g s b", s=2)[:, :, :, ::-1],
                                 [kk_], ['Zb'], e=eng)
                            k.cp(ZS[64:128, g2, 9:137, ri], src[64:128, :, 64:192][:, :, ::-1], [kk_],
                                 ['Zb'], e=eng)
                    k.cp(ZS[:, :, 8, :], h0[:, gs_, :], ['h0', 'Zb'], ['Zb'])
                    CH = 32

                    def cadd(Zdst, Zsrc, la, lb, nsg, n):
                        tick()
                        t1 = T1[:].rearrange("p a b c -> p (a b c)")[:, 0:nsg * n * 2].rearrange(
                            "p (a b c) -> p a b c", a=nsg, b=n)
                        t2 = T2[:].rearrange("p a b c -> p (a b c)")[:, 0:nsg * n * 2].rearrange(
                            "p (a b c) -> p a b c", a=nsg, b=n)
                        k.tt(t1, Zsrc, la.unsqueeze(2).broadcast_to([128, nsg, n, 2]), ALU.mult,
                             ['Zb', 'LAP', 'LPA'], ['T1'])
                        k.tt(t2, Zsrc[:, :, :, ::-1], lb.unsqueeze(2).broadcast_to([128, nsg, n, 2]),
                             ALU.mult, ['Zb', 'LAP', 'LPB'], ['T2'])
                        k.tt(t1, t1, t2, ALU.add, ['T1', 'T2'], ['T1'])
                        k.tt(Zdst, Zdst, t1, ALU.add, ['Zb', 'T1'], ['Zb'])

                    for (Zr, nsg, Ltot, nel, LAx, LBx, CHx) in (
                            (ZP, 16, 41, 33, lambda q: LAP[:, q], lambda q: LBP[:, q], 32),
                            (ZS, 8, 137, 129, lambda q: LPA[:, q, gs_, :], lambda q: LPB[:, q, gs_, :], 64)):
                        for q_, dsh in enumerate((1, 2, 4)):
                            starts = list(range(dsh, Ltot, CHx))
                            for lo in reversed(starts):
                                hi = min(lo + CHx, Ltot)
                                cadd(Zr[:, :, lo:hi, :], Zr[:, :, lo - dsh:hi - dsh, :], LAx(q_), LBx(q_),
                                     nsg, hi - lo)
                        k.memset(Zr[:, :, 0:8, :], 0.0, ['Zb'])
                        nst = (nel + 7) // 8 if DEBUG.get('s5_steps', 1) else 0
                        for j in range(nst):
                            s0 = 8 + 8 * j
                            n = min(8, 8 + nel - s0)
                            cadd(Zr[:, :, s0:s0 + n, :], Zr[:, :, s0 - 8:s0 - 8 + n, :], LAx(3), LBx(3),
                                 nsg, n)
                    for si in range(2):
                        for ri in range(2):
                            k.cp(Fin[:, si, ri, gs_], ZP[:, si * 8:(si + 1) * 8, 40, ri], ['Zb'], ['Fin'])
                    ZPv = ZP[:].rearrange("p (s g) q r -> p g s q r", s=2)
                    for ri, Xd in enumerate((Xbre, Xbim)):
                        eng = 'act'
                        k.cp(Xd[0:64, :, 0:64].rearrange("p g (s b) -> p g s b", s=2),
                             ZPv[0:64, :, :, 8:40, ri], ['Zb'], ['Xb'], e=eng)
                        k.cp(Xd[0:64, :, 64:192], ZS[0:64, :, 8:136, ri], ['Zb'], ['Xb'], e=eng)
                        k.cp(Xd[64:128, :, 0:64].rearrange("p g (s b) -> p g s b", s=2),
                             ZPv[64:128, :, :, 8:40, ri][:, :, :, ::-1], ['Zb'], ['Xb'], e=eng)
                        k.cp(Xd[64:128, :, 64:192], ZS[64:128, :, 8:136, ri][:, :, ::-1], ['Zb'], ['Xb'],
                             e=eng)
                    for gi in range(8):
                        pb, kp = k.ps()
                        k.mm(pb[:, 0:192], Wtp[:, gi, :], Ublk[:, gi, :], True, False, ['Wtp', 'Ublk'], [kp])
                        k.mm(pb[:, 0:192], Wcre[:, gi, :], Xbre[:, gi, :], False, False, ['Vo', 'Xb'], [kp])
                        k.mm(pb[:, 0:192], Wcim[:, gi, :], Xbim[:, gi, :], False, True, ['Vo2', 'Xb'], [kp])
                        k.cp(Ysb[:, gi, :], pb[:, 0:192], [kp], ['Ysb'], e='act')
                    yP = ytmp[:, 0:512].rearrange("p (b s) -> p b s", s=8)
                    yS = ytmp[:, 512:T].rearrange("p (rh s c) -> p c rh s", rh=2, s=8, c=64)
                    dcol = G['s5dT'][:, l, ct:ct + 1]
                    for tau in range(8):
                        pb, kp = k.ps()
                        for gi in range(8):
                            k.mm(pb[:, 0:192], SELWb[:, tau, 112 - 16 * gi:240 - 16 * gi], Ysb[:, gi, :],
                                 gi == 0, gi == 7, ['SELWb', 'Ysb'], [kp])
                        k.stt(yP[:, :, tau], suP[:, :, tau], dcol, pb[:, 0:64], ALU.mult, ALU.add,
                              [('hT', 0), 's5dT', kp], ['ytmp'])
                        k.stt(yS[:, :, :, tau], suS[:, :, :, tau], dcol,
                              pb[:, 64:192].rearrange("p (c rh) -> p c rh", rh=2), ALU.mult, ALU.add,
                              [('hT', 1), ('hT', 2), 's5dT', kp], ['ytmp'])
                    if ct == 0:
                        dump('s5_y', ytmp[:, 0:128], ['ytmp'])
                        dump('s5_ys', ytmp[:, 512:640], ['ytmp'])
                    for b in range(3):
                        bs = slice(b * 512, (b + 1) * 512)
                        k.act(ga[:], ytmp[:, bs], AF.Square, ['ytmp'], ['ga'])
                        k.act(ga[:], ga[:], AF.Identity, ['ga'], ['ga'], scale=0.044715, bias=1.0)
                        k.tt(ga[:], ga[:], ytmp[:, bs], ALU.mult, ['ga', 'ytmp'], ['ga'])
                        k.act(gb_[:], ga[:], AF.Sigmoid, ['ga'], ['gb_'], scale=1.5957691216057308)
                        k.tt(catT[:, 4 + ct, bs], gb_[:], ytmp[:, bs], ALU.mult, ['gb_', 'ytmp'],
                             [('catT', 4 + ct)])
                    k.barrier()

        while agen is not None:
            tick()
        for si in range(2):
            for ri, dst in enumerate((G['ns_s5re'], G['ns_s5im'])):
                pb, kp = k.ps()
                k.tr(pb[0:32, 0:128], Fin[:, si, ri, :], C('ident'), ['Fin', 'C'], [kp])
                k.cp(stg[0:32, :], pb[0:32, 0:128], [kp], ['stg'])
                k.dma('sp', dst[si, l].rearrange("d g n -> g d n"),
                      stg[0:32, :].rearrange("g (d n) -> g d n", d=2), ['stg'], [('nss5', si, ri)])
        with contextlib.ExitStack() as e1:
            gw = k.sb(e1, [128, 4, 512], BF16, 'gluw')
            gws = k.sb(e1, [128, 4, 512], F32, 'gluws')
            k.dma('sp', gws[:], G['s5_glu_w'][l].rearrange("(c p) f -> p c f", p=128), (), ['gluws'])
            k.cp(gw[:], gws[:], ['gluws'], ['gluw'])
            sig = k.sb(e1, [128, 4, 512], F32, 'sig')
            kc = [('catT', 4 + i) for i in range(4)]
            for b in range(3):
                bs = slice(b * 512, (b + 1) * 512)
                for ft in range(4):
                    pb, kp = k.ps()
                    for c4 in range(4):
                        k.mm(pb[:], gw[:, c4, ft * 128:(ft + 1) * 128], catT[:, 4 + c4, bs], c4 == 0,
                             c4 == 3, ['gluw'] + kc, [kp])
                    k.act(sig[:, ft, :], pb[:], AF.Sigmoid, [kp, 'glub'], ['sig'],
                          bias=G['glub'][:, l, ft:ft + 1])
                k.tt(sig[:], sig[:], hT[:, 4:8, bs], ALU.mult, ['sig', ('hT', b)], ['sig'])
                k.tt(catT[:, 4:8, bs], catT[:, 4:8, bs], sig[:], ALU.mult, ['sig'] + kc, kc)
            k.barrier()


_NC_CACHE = {}


def _get_nc(dbg=None):
    key = tuple(sorted((dbg or {}).items(), key=lambda kv: kv[0])) if dbg else ()
    key = str(key) + str(sorted(DEBUG.items()))
    if key not in _NC_CACHE:
        _NC_CACHE[key] = build(dbg)
    return _NC_CACHE[key]


def kernel(x_prompt, x_sample, c, state_gla, state_s5_re, state_s5_im, state_gdn, c_ctx,
           norm_pre, norm_post, w_ada, b_ada, w_in, gla_gate_w, gla_gate_b, gla_norm,
           s5_lam_re, s5_lam_im, s5_log_dt, s5_b_re, s5_b_im, s5_c_re, s5_c_im, s5_d,
           s5_glu_w, s5_glu_b, gdn_conv, gdn_a_log, gdn_dt_bias, gdn_norm, w_out, _dbg=None):
    f = lambda a: np.ascontiguousarray(np.asarray(a, dtype=np.float32))
    x_prompt, x_sample, c, c_ctx = f(x_prompt), f(x_sample), f(c), f(c_ctx)
    shared = dict(norm_pre=f(norm_pre), norm_post=f(norm_post), w_ada=f(w_ada), b_ada=f(b_ada),
                  w_in=f(w_in), gla_gate_w=f(gla_gate_w), gla_gate_b=f(gla_gate_b),
                  gla_norm=f(gla_norm), s5_lam_re=f(s5_lam_re), s5_lam_im=f(s5_lam_im),
                  s5_log_dt=f(s5_log_dt), s5_b_re=f(s5_b_re), s5_b_im=f(s5_b_im),
                  s5_c_re=f(s5_c_re), s5_c_im=f(s5_c_im), s5_d=f(s5_d), s5_glu_w=f(s5_glu_w),
                  s5_glu_b=f(s5_glu_b), gdn_conv=f(gdn_conv), gdn_a_log=f(gdn_a_log),
                  gdn_dt_bias=f(gdn_dt_bias), gdn_norm=f(gdn_norm), w_out=f(w_out),
                  consts=CONSTS)
    state_gla, state_s5_re, state_s5_im, state_gdn = f(state_gla), f(state_s5_re), f(state_s5_im), f(state_gdn)
    in_maps = []
    for i in range(8):
        m = dict(shared)
        m['xin'] = np.ascontiguousarray(np.concatenate(
            [x_prompt[2 * i], x_prompt[2 * i + 1], x_sample[i]], axis=0))
        m['cond'] = np.ascontiguousarray(np.stack([c_ctx, c[i]], axis=0))
        m['st_gla'] = state_gla[i]
        m['st_s5re'] = state_s5_re[i]
        m['st_s5im'] = state_s5_im[i]
        m['st_gdn'] = state_gdn[i]
        in_maps.append(m)
    nc = _get_nc(_dbg)
    res = run_bass_kernel_spmd(nc, in_maps, core_ids=list(range(8)))
    R = res.results
    y_prompt = np.stack([R[i // 2]['y'][(i % 2) * 256:(i % 2 + 1) * 256] for i in range(16)], 0)
    y_sample = np.stack([R[i]['y'][512:] for i in range(8)], 0)
    cat = lambda n: np.concatenate([R[i][n] for i in range(8)], axis=0)
    outs = (y_prompt.astype(np.float32), y_sample.astype(np.float32), cat('ns_gla'),
            cat('ns_s5re'), cat('ns_s5im'), cat('ns_gdn'))
    if _dbg:
        return outs, {n: [R[i]['dbg_' + n] for i in range(8)] for n in _dbg}
    return outs
```

```python
import contextlib
import math
import numpy as np
import concourse.bass as bass
import concourse.mybir as mybir
from concourse.bass_utils import run_bass_kernel_spmd

F32 = mybir.dt.float32
BF16 = mybir.dt.bfloat16
I32 = mybir.dt.int32
AF = mybir.ActivationFunctionType
ALU = mybir.AluOpType

DEPTH = 4
D = 1024
T = 1536
NT = 12
IN_DIM = 4656
EPS = 1e-6
SEQS = [(0, 256), (256, 256), (512, 1024)]
NDS = 8
NEGBIG = -30000.0
DEBUG = {}
EMBED_WAIT = True

_CL = {}


def _build_consts():
    cols = []
    off = 0

    def add(name, arr):
        nonlocal off
        arr = np.asarray(arr, np.float32).reshape(128, -1)
        _CL[name] = (off, arr.shape[1])
        off += arr.shape[1]
        cols.append(arr)

    j = np.arange(128)[:, None]
    i = np.arange(128)[None, :]
    add('ident', (j == i))
    add('ones', np.ones((128, 128)))
    add('U', (j <= i))
    add('SU', (j < i))
    add('L', (j >= i))
    add('SL', (j > i))
    add('UN16', (j <= i) / -16.0)
    add('LN16', (j >= i) / -16.0)
    add('SLN16', (j > i) / -16.0)
    add('SUN16', (j < i) / -16.0)
    add('NEGf', NEGBIG * (i < j))
    add('NEGb', NEGBIG * (i > j))
    kt = np.zeros((128, 4, 8), np.float32)
    s = np.arange(8)
    kt[:64, 0] = 7 - s
    kt[64:, 0] = s
    kt[:64, 1] = s + 1
    kt[64:, 1] = 8 - s
    kt[:64, 2] = -s
    kt[64:, 2] = s
    kt[:64, 3] = s
    kt[64:, 3] = -s
    add('KTAB', kt)
    sp = (np.arange(128) // 16)[:, None]
    tf = (np.arange(128) // 16)[None, :]
    add('TMF', (sp <= tf))
    add('TMB', (sp >= tf))
    global NCONST_P
    NCONST_P = off
    bd = lambda bs: (j // bs == i // bs)
    add('M16', bd(16))
    add('OFF16', bd(32) & ~bd(16))
    add('OFF32', bd(64) & ~bd(32))
    add('OFF64', ~bd(64))
    selw = np.zeros((128, 8, 240), np.float32)
    for g in range(8):
        for h in range(16):
            selw[g * 16 + h, g, 112 + h] = 1.0
    add('SELW', selw)
    return np.concatenate(cols, axis=1)


NCONST_P = 0
CONSTS = _build_consts()
NCONST = CONSTS.shape[1]


class KB:
    def __init__(self):
        self.nc = bass.Bass("TRN2", target_bir_lowering=False)
        nc = self.nc
        self.eng = {'pe': nc.tensor, 'act': nc.scalar, 'dve': nc.vector, 'pool': nc.gpsimd,
                    'sp': nc.sync}
        self.semobj = {}
        for e in ['pe', 'act', 'dve', 'pool']:
            self.semobj[e] = nc.alloc_semaphore(name=f"s_{e}")
        self.cnt = {e: 0 for e in ['pe', 'act', 'dve', 'pool']}
        self.dcnt = {}
        self.dnext = {'sp': 0, 'pool': 0}
        for q in ['sp', 'pool']:
            for i in range(NDS):
                sid = f"d_{q}{i}"
                self.semobj[sid] = nc.alloc_semaphore(name=sid)
                self.dcnt[sid] = 0
        self.waited = {e: {} for e in self.eng}
        self.lw = {}
        self.rd = {}
        self.nalloc = 0
        self.psn = 0
        self.banks = None

    def sb(self, es, shape, dt=F32, name=None):
        self.nalloc += 1
        return es.enter_context(self.nc.sbuf_tensor(f"{name or 't'}_{self.nalloc}", list(shape), dt))

    def init_psum(self, es):
        self.banks = [es.enter_context(self.nc.psum_tensor(f"bank{i}", [128, 512], F32))
                      for i in range(8)]

    def ps(self):
        i = self.psn
        self.psn = (self.psn + 1) % 7
        return self.banks[i], ('ps', i)

    def _need(self, e, deps):
        need = {}
        for d in deps:
            if d is None:
                continue
            sid, val = d
            if val <= 0:
                continue
            if e == 'pe' and sid == 'pe':
                continue
            if val > need.get(sid, 0):
                need[sid] = val
        return [(sid, val) for sid, val in need.items() if self.waited[e].get(sid, 0) < val]

    def _wait(self, e, deps, keep_last=False):
        need = self._need(e, deps)
        last = None
        if keep_last and need:
            last = need.pop()
        for sid, val in need:
            self.eng[e].wait_ge(self.semobj[sid], val)
            self.waited[e][sid] = val
        return last

    def _deps(self, r, w):
        deps = []
        for k in r:
            deps.append(self.lw.get(k))
        for k in w:
            deps.append(self.lw.get(k))
            deps.extend(self.rd.get(k, {}).items())
        return deps

    def _mark(self, tag, r, w):
        for k in w:
            self.lw[k] = tag
            self.rd[k] = {}
        for k in r:
            d = self.rd.setdefault(k, {})
            if tag[1] > d.get(tag[0], 0):
                d[tag[0]] = tag[1]

    def op(self, e, fn, r=(), w=(), inc=True):
        psr = [x for x in r if isinstance(x, tuple) and x[0] == 'ps']
        if psr:
            r = [x for x in r if not (isinstance(x, tuple) and x[0] == 'ps')]
            w = list(w) + psr
        embed = EMBED_WAIT and e in ('act', 'dve', 'pool')
        last = self._wait(e, self._deps(r, w), keep_last=embed)
        inst = fn(self.eng[e])
        if last is not None:
            inst._wait_ge(self.semobj[last[0]], last[1])
            self.waited[e][last[0]] = last[1]
        if inc:
            self.cnt[e] += 1
            inst.then_inc(self.semobj[e], 1)
            tag = (e, self.cnt[e])
        else:
            tag = (e, self.cnt[e] + 1)
        self._mark(tag, r, w)

    def dma(self, q, out, in_, r=(), w=(), **kw):
        slot = self.dnext[q]
        self.dnext[q] = (slot + 1) % NDS
        sid = f"d_{q}{slot}"
        deps = self._deps(r, w)
        deps.append((sid, self.dcnt[sid]))
        self._wait(q, deps)
        inst = self.eng[q].dma_start(out=out, in_=in_, **kw)
        self.dcnt[sid] += 16
        inst.then_inc(self.semobj[sid], 16)
        self._mark((sid, self.dcnt[sid]), r, w)

    def barrier(self):
        allv = [(e, self.cnt[e]) for e in self.cnt] + [(sid, v) for sid, v in self.dcnt.items()]
        for e in self.eng:
            self._wait(e, allv)
        self.lw = {}
        self.rd = {}

    def mm(self, out, lhsT, rhs, start, stop, r, w, inc=None):
        self.op('pe', lambda E: E.matmul(out, lhsT, rhs, start=start, stop=stop), r, w,
                inc=(stop if inc is None else inc))

    def tr(self, out, in_, ident, r, w, inc=True):
        self.op('pe', lambda E: E.transpose(out, in_, ident), r, w, inc=inc)

    def act(self, out, in_, func, r, w, bias=None, scale=None, e='act'):
        kw = {}
        if bias is not None:
            kw['bias'] = bias
        if scale is not None:
            kw['scale'] = scale
        self.op('act', lambda E: E.activation(out, in_, func, **kw), r, w)

    def tt(self, out, a, b, op, r, w, e='dve'):
        self.op(e, lambda E: E.tensor_tensor(out, a, b, op), r, w)

    def ts(self, out, a, s1, s2, op0, op1, r, w, e='dve'):
        if op1 is None:
            self.op(e, lambda E: E.tensor_scalar(out, a, s1, None, op0), r, w)
        else:
            self.op(e, lambda E: E.tensor_scalar(out, a, s1, s2, op0, op1), r, w)

    def stt(self, out, a, sc, b, op0, op1, r, w, e='dve'):
        self.op(e, lambda E: E.scalar_tensor_tensor(out, a, sc, b, op0, op1), r, w)

    def cp(self, out, in_, r, w, e='dve'):
        if e == 'act':
            self.op('act', lambda E: E.copy(out, in_), r, w)
        else:
            self.op(e, lambda E: E.tensor_copy(out, in_), r, w)

    def recip(self, out, in_, r, w):
        self.op('dve', lambda E: E.reciprocal(out, in_), r, w)

    def memset(self, ap, v, w, e='dve'):
        self.op(e, lambda E: E.memset(ap, v), (), w)


def build(dbg=None):
    dbg = dbg or {}
    k = KB()
    nc = k.nc

    def din(name, shape):
        return nc.dram_tensor(name, list(shape), F32, kind="ExternalInput").ap()

    def dout(name, shape):
        return nc.dram_tensor(name, list(shape), F32, kind="ExternalOutput").ap()

    xin = din('xin', [T, D])
    cond = din('cond', [2, D])
    st_gla = din('st_gla', [DEPTH, 2, 4, 64, 128])
    st_s5re = din('st_s5re', [DEPTH, 2, 32, 64])
    st_s5im = din('st_s5im', [DEPTH, 2, 32, 64])
    st_gdn = din('st_gdn', [DEPTH, 2, 4, 128, 128])
    norm_pre = din('norm_pre', [DEPTH, D])
    norm_post = din('norm_post', [DEPTH, D])
    w_ada = din('w_ada', [DEPTH, D, 3 * D])
    b_ada = din('b_ada', [DEPTH, 3 * D])
    w_in = din('w_in', [DEPTH, D, IN_DIM])
    gla_gate_w = din('gla_gate_w', [DEPTH, 2, 16, 256])
    gla_gate_b = din('gla_gate_b', [DEPTH, 2, 256])
    gla_norm = din('gla_norm', [DEPTH, 128])
    s5_lam_re = din('s5_lam_re', [DEPTH, 2, 32, 64])
    s5_lam_im = din('s5_lam_im', [DEPTH, 2, 32, 64])
    s5_log_dt = din('s5_log_dt', [DEPTH, 2, 32])
    s5_b_re = din('s5_b_re', [DEPTH, 32, 64, 16])
    s5_b_im = din('s5_b_im', [DEPTH, 32, 64, 16])
    s5_c_re = din('s5_c_re', [DEPTH, 2, 32, 16, 64])
    s5_c_im = din('s5_c_im', [DEPTH, 2, 32, 16, 64])
    s5_d = din('s5_d', [DEPTH, 512])
    s5_glu_w = din('s5_glu_w', [DEPTH, 512, 512])
    s5_glu_b = din('s5_glu_b', [DEPTH, 512])
    gdn_conv = din('gdn_conv', [DEPTH, 3, 1536])
    gdn_a_log = din('gdn_a_log', [DEPTH, 2, 4])
    gdn_dt_bias = din('gdn_dt_bias', [DEPTH, 2, 4])
    gdn_norm = din('gdn_norm', [DEPTH, 128])
    w_out = din('w_out', [DEPTH, 1536, D])
    consts = din('consts', [128, NCONST])

    y_out = dout('y', [T, D])
    ns_gla = dout('ns_gla', [2, DEPTH, 2, 4, 64, 128])
    ns_s5re = dout('ns_s5re', [2, DEPTH, 2, 32, 64])
    ns_s5im = dout('ns_s5im', [2, DEPTH, 2, 32, 64])
    ns_gdn = dout('ns_gdn', [2, DEPTH, 2, 4, 128, 128])
    gsc = nc.dram_tensor('gsc', [DEPTH, 2, 96, 128], F32, kind="Internal").ap()
    dbg_out = {}
    for name, shape in dbg.items():
        dbg_out[name] = dout('dbg_' + name, shape)

    es = contextlib.ExitStack()
    with es:
        k.init_psum(es)
        CS = k.sb(es, [128, NCONST_P], F32, 'consts_sb')

        def C(name):
            o, w = _CL[name]
            return CS[:, o:o + w]

        k.dma('sp', CS[:], consts[:, 0:NCONST_P], (), ['C'])
        identb = k.sb(es, [128, 128], BF16, 'identb')
        k.cp(identb[:], C('ident'), ['C'], ['identb'])
        onesb = k.sb(es, [128, 128], BF16, 'onesb')
        k.cp(onesb[:], C('ones'), ['C'], ['onesb'])
        Ub = k.sb(es, [128, 128], BF16, 'Ub')
        k.cp(Ub[:], C('U'), ['C'], ['Ub'])
        Lb = k.sb(es, [128, 128], BF16, 'Lb')
        k.cp(Lb[:], C('L'), ['C'], ['Lb'])

        xT = k.sb(es, [128, 8, T], F32, 'xT')
        catT = k.sb(es, [128, 12, T], BF16, 'catT')
        npre = k.sb(es, [128, DEPTH, 8], F32, 'npre')
        npost = k.sb(es, [128, DEPTH, 8], F32, 'npost')
        bada = k.sb(es, [128, DEPTH, 24], F32, 'bada')
        glan = k.sb(es, [128, DEPTH], F32, 'glan')
        gdnn = k.sb(es, [128, DEPTH], F32, 'gdnn')
        s5dT = k.sb(es, [128, DEPTH, 4], F32, 's5dT')
        glub = k.sb(es, [128, DEPTH, 4], F32, 'glub')
        convw = k.sb(es, [128, DEPTH, 3, 12], F32, 'convw')
        scond = k.sb(es, [128, 2, 8], F32, 'scond')
        stg = k.sb(es, [128, 128], F32, 'stg')

        def load_cols(dst, src_rows, R, key):
            k.dma('sp', stg[0:R, :], src_rows, (), ['stg'])
            pb, kp = k.ps()
            k.tr(pb[:, 0:R], stg[0:R, :], C('ident')[0:R, 0:R], ['stg', 'C'], [kp])
            k.cp(dst, pb[:, 0:R], [kp], [key])

        load_cols(npre[:].rearrange("p l c -> p (l c)"), norm_pre.rearrange("l (c p) -> (l c) p", p=128), 32, 'npre')
        load_cols(npost[:].rearrange("p l c -> p (l c)"), norm_post.rearrange("l (c p) -> (l c) p", p=128), 32, 'npost')
        load_cols(bada[:].rearrange("p l c -> p (l c)"), b_ada.rearrange("l (c p) -> (l c) p", p=128), 96, 'bada')
        load_cols(glan[:], gla_norm, 4, 'glan')
        load_cols(gdnn[:], gdn_norm, 4, 'gdnn')
        load_cols(s5dT[:].rearrange("p l c -> p (l c)"), s5_d.rearrange("l (c p) -> (l c) p", p=128), 16, 's5dT')
        load_cols(glub[:].rearrange("p l c -> p (l c)"), s5_glu_b.rearrange("l (c p) -> (l c) p", p=128), 16, 'glub')
        for l in range(DEPTH):
            load_cols(convw[:, l].rearrange("p k c -> p (k c)"), gdn_conv[l].rearrange("k (c p) -> (k c) p", p=128), 36, 'convw')
        load_cols(scond[:].rearrange("p j c -> p (j c)"), cond.rearrange("j (c p) -> (j c) p", p=128), 16, 'scond')
        k.act(scond[:], scond[:], AF.Silu, ['scond'], ['scond'])

        adaA = k.sb(es, [128, 2, 24, 2], F32, 'adaA')
        gmodA = k.sb(es, [128, 2, 8, 2], F32, 'gmodA')
        gpostA = k.sb(es, [128, 2, 8, 2], F32, 'gpostA')

        def ada_gen(l, wab):
            par = l % 2
            wv = w_ada[l].rearrange("(c p) f -> p c f", p=128)
            bk7, ky7 = k.banks[7], ('ps', 7)
            k.dma('sp', wab[0][:], wv[:, :, 0:128], (), [('wabp', 0)])
            yield
            for fb in range(24):
                if fb + 1 < 24:
                    k.dma('sp', wab[(fb + 1) % 2][:], wv[:, :, (fb + 1) * 128:(fb + 2) * 128], (),
                          [('wabp', (fb + 1) % 2)])
                for c in range(8):
                    k.mm(bk7[:, 0:2], wab[fb % 2][:, c, :], scond[:, :, c], c == 0, c == 7,
                         [('wabp', fb % 2), 'scond'], [ky7])
                yield
                k.ts(adaA[:, par, fb, :], bk7[:, 0:2], bada[:, l, fb:fb + 1], None, ALU.add, None,
                     [ky7, 'bada'], [('adaA', par)])
                yield
            ka = ('adaA', par)
            k.ts(gmodA[:, par], adaA[:, par, 8:16, :], 1.0, None, ALU.add, None, [ka], [ka])
            k.tt(gmodA[:, par], gmodA[:, par], npre[:, l, :].unsqueeze(2).broadcast_to([128, 8, 2]),
                 ALU.mult, [ka, 'npre'], [ka])
            k.tt(gpostA[:, par], adaA[:, par, 16:24, :],
                 npost[:, l, :].unsqueeze(2).broadcast_to([128, 8, 2]), ALU.mult, [ka, 'npost'], [ka])
            yield

        with contextlib.ExitStack() as e0:
            wab0 = [k.sb(e0, [128, 8, 128], F32, f'wab0_{i}') for i in range(2)]
            for _ in ada_gen(0, wab0):
                pass
            k.barrier()

        with contextlib.ExitStack() as e0:
            xtm = [k.sb(e0, [128, D], F32, f'xtm{i}') for i in range(2)]
            for t in range(NT):
                xb = xtm[t % 2]
                kx = ('xtm', t % 2)
                k.dma('sp', xb[:], xin[t * 128:(t + 1) * 128, :], (), [kx])
                for half in range(2):
                    pb, kp = k.ps()
                    for c4 in range(4):
                        c = half * 4 + c4
                        k.tr(pb[:, c4 * 128:(c4 + 1) * 128], xb[:, c * 128:(c + 1) * 128],
                             C('ident'), [kx, 'C'], [kp])
                    k.cp(xT[:, half * 4:half * 4 + 4, t * 128:(t + 1) * 128],
                         pb[:].rearrange("p (c t) -> p c t", c=4), [kp], [('xT', t // 4)],
                         e='act' if half else 'dve')
            k.barrier()

        for l in range(DEBUG.get('depth', DEPTH)):
            layer(k, es, l, locals())

        with contextlib.ExitStack() as e0:
            ytm = [k.sb(e0, [128, D], F32, f'ytm{i}') for i in range(2)]
            for t in range(NT):
                yb = ytm[t % 2]
                ky = ('ytm', t % 2)
                for half in range(2):
                    pb, kp = k.ps()
                    for c4 in range(4):
                        c = half * 4 + c4
                        k.tr(pb[:, c4 * 128:(c4 + 1) * 128], xT[:, c, t * 128:(t + 1) * 128],
                             C('ident'), [('xT', t // 4), 'C'], [kp])
                    k.cp(yb[:, half * 512:(half + 1) * 512], pb[:], [kp], [ky],
                         e='act' if half else 'dve')
                k.dma('sp', y_out[t * 128:(t + 1) * 128, :], yb[:], [ky], [('yout', t)])
            k.barrier()
    return nc


def layer(k, es, l, G):
    nc = k.nc
    C = G['C']
    xT = G['xT']
    catT = G['catT']
    dbg_out = G['dbg_out']
    identb = G['identb']

    def dump(name, ap, r):
        if name in dbg_out and l == DEBUG.get('layer', 0):
            k.dma('sp', dbg_out[name], ap, r, [('dbg', name)])

    with contextlib.ExitStack() as el:
        par_ = l % 2
        adaT = G['adaA'][:, par_]
        gmod = G['gmodA'][:, par_]
        gpost = G['gpostA'][:, par_]

        if DEBUG.get('stop', 9) < 2:
            return
        hT = k.sb(el, [128, 8, T], BF16, 'hT')
        with contextlib.ExitStack() as eh:
            WH = {}

            def alloc_w(stk, ncols):
                WH['W'] = k.sb(stk, [128, 8, ncols], BF16, 'W')
                WH['wstg'] = [k.sb(stk, [128, 8, 128], F32, f'wstg{i}') for i in range(2)]
            with contextlib.ExitStack() as e1:
                sq = k.sb(e1, [128, 8, 512], BF16, 'sq')
                rstd = k.sb(e1, [128, 512], F32, 'rstd')
                tmps = [k.sb(e1, [128, 512], F32, f'tmpB{i}') for i in range(2)]
                for b in range(3):
                    j = 0 if b == 0 else 1
                    bs = slice(b * 512, (b + 1) * 512)
                    k.act(sq[:], xT[:, :, bs], AF.Square, [('xT', b)], ['sq'])
                    pb, kp = k.ps()
                    for c in range(8):
                        k.mm(pb[:], G['onesb'][:], sq[:, c, :], c == 0, c == 7, ['sq', 'onesb'], [kp])
                    k.act(rstd[:], pb[:], AF.Ln, [kp], ['rstd'], scale=1.0 / D, bias=EPS)
                    k.act(rstd[:], rstd[:], AF.Exp, ['rstd'], ['rstd'], scale=-0.5)
                    for c in range(8):
                        tmp = tmps[c % 2]
                        k.tt(tmp[:], xT[:, c, bs], rstd[:], ALU.mult, [('xT', b), 'rstd'],
                             [('tmpB', c % 2)])
                        k.act(hT[:, c, bs], tmp[:], AF.Identity, [('tmpB', c % 2), 'gmod', 'adaT'],
                              [('hT', b)], bias=adaT[:, c, j:j + 1], scale=gmod[:, c, j:j + 1])
                k.barrier()
            dump('hT', hT[:, :, 0:128], [('hT', 0)])

            wctr = [0]

            def load_w(col0, ncols, dst0=0):
                wsrc = G['w_in'][l].rearrange("(c p) e -> p c e", p=128)
                for o in range(0, ncols, 128):
                    n = min(128, ncols - o)
                    i = wctr[0] % 2
                    wctr[0] += 1
                    k.dma('sp', WH['wstg'][i][:, :, 0:n], wsrc[:, :, col0 + o:col0 + o + n], (),
                          [('wstg', i)])
                    k.cp(WH['W'][:, :, dst0 + o:dst0 + o + n], WH['wstg'][i][:, :, 0:n],
                         [('wstg', i)], ['W'], e='act' if i else 'dve')

            def proj_fm(wc0, ncols, b):
                pb, kp = k.ps()
                for c in range(8):
                    k.mm(pb[0:ncols, :], WH['W'][:, c, wc0:wc0 + ncols], hT[:, c, b * 512:(b + 1) * 512],
                         c == 0, c == 7, ['W', ('hT', b)], [kp])
                return pb, kp

            def proj_tm(wc0, ncols, t):
                pb, kp = k.ps()
                for c in range(8):
                    k.mm(pb[:, 0:ncols], hT[:, c, t * 128:(t + 1) * 128], WH['W'][:, c, wc0:wc0 + ncols],
                         c == 0, c == 7, ['W', ('hT', t // 4)], [kp])
                return pb, kp

            def headnorm(e1, oT, koT, nh, normw, gate_wc0, cat0):
                NS_ = 3
                HS = [dict(sqh=k.sb(e1, [128, 512], BF16, f'sqh{i}'), rs=k.sb(e1, [128, 512], F32, f'rsh{i}'),
                           gs=k.sb(e1, [128, 512], F32, f'gsh{i}')) for i in range(NS_)]

                def item(sl, h, b):
                    sqh, rs, gs = HS[sl]['sqh'], HS[sl]['rs'], HS[sl]['gs']
                    K_ = lambda nm: (nm, sl)
                    bkA, kyA = k.banks[2 * sl], ('ps', 2 * sl)
                    bkB, kyB = k.banks[2 * sl + 1], ('ps', 2 * sl + 1)
                    bs = slice(b * 512, (b + 1) * 512)
                    k.act(sqh[:], oT[:, h, bs], AF.Square, [koT], [K_('sqh')])
                    for c in range(8):
                        k.mm(bkB[:, :], WH['W'][:, c, gate_wc0 + h * 128:gate_wc0 + (h + 1) * 128],
                             hT[:, c, bs], c == 0, c == 7, ['W', ('hT', b)], [kyB])
                    yield
                    k.mm(bkA[:], G['onesb'][:], sqh[:], True, True, [K_('sqh'), 'onesb'], [kyA])
                    k.act(gs[:], bkB[:], AF.Silu, [kyB], [K_('gsh')])
                    yield
                    k.act(rs[:], bkA[:], AF.Ln, [kyA], [K_('rsh')], scale=1.0 / 128, bias=EPS)
                    k.act(rs[:], rs[:], AF.Exp, [K_('rsh')], [K_('rsh')], scale=-0.5)
                    yield
                    k.stt(rs[:], rs[:], normw, gs[:], ALU.mult, ALU.mult,
                          [K_('rsh'), K_('gsh'), 'glan', 'gdnn'], [K_('rsh')])
                    k.tt(catT[:, cat0 + h, bs], oT[:, h, bs], rs[:], ALU.mult, [koT, K_('rsh')],
                         [('catT', cat0 + h)])
                    yield

                items = [(h, b) for h in range(nh) for b in range(3)]
                free = list(range(NS_))
                running = []
                while items or running:
                    while items and free:
                        sl = free.pop(0)
                        h, b = items.pop(0)
                        running.append((sl, item(sl, h, b)))
                    for (sl, g_) in list(running):
                        try:
                            next(g_)
                        except StopIteration:
                            running.remove((sl, g_))
                            free.append(sl)

            if DEBUG.get('stop', 9) < 3:
                return
            if DEBUG.get('gla', True):
                gla_phase(k, l, G, locals())
            else:
                k.memset(catT[:, 0:4, :], 0.0, [('catT', i) for i in range(4)])
            k.barrier()
            if DEBUG.get('gdn', True):
                from_gdn = gdn_phase(k, l, G, locals())
            else:
                k.memset(catT[:, 8:12, :], 0.0, [('catT', 8 + i) for i in range(4)])
            k.barrier()
            if DEBUG.get('s5', True):
                s5_part1(k, l, G, locals())
            k.barrier()
        if DEBUG.get('s5', True):
            s5_part2(k, l, G, locals())
        else:
            k.memset(catT[:, 4:8, :], 0.0, [('catT', 4 + i) for i in range(4)])
        k.barrier()
        dump('catT', catT[:, :, 0:128], [('catT', i) for i in range(12)])

        if DEBUG.get('nog'):
            return
        with contextlib.ExitStack() as e1:
            Wo = k.sb(e1, [128, 12, D], BF16, 'Wo')
            wos = [k.sb(e1, [128, 2, D], F32, f'wos{i}') for i in range(2)]
            wosrc = G['w_out'][l].rearrange("(m p) f -> p m f", p=128)
            for m2 in range(6):
                k.dma('sp', wos[m2 % 2][:], wosrc[:, m2 * 2:m2 * 2 + 2, :], (), [('wos', m2 % 2)])
                k.cp(Wo[:, m2 * 2:m2 * 2 + 2, :], wos[m2 % 2][:], [('wos', m2 % 2)], ['Wo'], e='act' if m2 % 2 else 'dve')
            oTs = [k.sb(e1, [128, 8, 512], F32, f'outT{i}') for i in range(2)]
            sq = k.sb(e1, [128, 8, 512], BF16, 'sqG')
            rstd = k.sb(e1, [128, 512], F32, 'rstdG')
            tmp = k.sb(e1, [128, 512], F32, 'tmpG')
            for b in range(3):
                j = 0 if b == 0 else 1
                bs = slice(b * 512, (b + 1) * 512)
                oT = oTs[b % 2]
                kO = ('outT', b % 2)
                for ft in range(8):
                    pb, kp = k.ps()
                    for m in range(12):
                        k.mm(pb[:], Wo[:, m, ft * 128:(ft + 1) * 128], catT[:, m, bs], m == 0,
                             m == 11, ['Wo'] + [('catT', i) for i in range(12)], [kp])
                    k.cp(oT[:, ft, :], pb[:], [kp], [kO])
                    k.act(sq[:, ft, :], pb[:], AF.Square, [kp], ['sqG'])
                pb, kp = k.ps()
                for c in range(8):
                    k.mm(pb[:], G['onesb'][:], sq[:, c, :], c == 0, c == 7, ['sqG', 'onesb'], [kp])
                k.act(rstd[:], pb[:], AF.Ln, [kp], ['rstdG'], scale=1.0 / D, bias=EPS)
                k.act(rstd[:], rstd[:], AF.Exp, ['rstdG'], ['rstdG'], scale=-0.5)
                for c in range(8):
                    k.tt(tmp[:], oT[:, c, :], rstd[:], ALU.mult, [kO, 'rstdG'], ['tmpG'])
                    k.stt(xT[:, c, bs], tmp[:], gpost[:, c, j:j + 1], xT[:, c, bs], ALU.mult,
                          ALU.add, ['tmpG', 'gpost', ('xT', b)], [('xT', b)])
            k.barrier()


def gla_phase(k, l, G, L):
    C = G['C']
    catT = G['catT']
    hT = L['hT']
    proj_fm = L['proj_fm']
    dump = L['dump']
    identb = G['identb']
    with contextlib.ExitStack() as e0:
      L['alloc_w'](e0, 544)
      W = L['WH']['W']
      L['load_w'](0, 512)
      oT = k.sb(e0, [128, 4, T], F32, 'oTgla')
      with contextlib.ExitStack() as e1:
        qkT = k.sb(e1, [128, 4, T], BF16, 'qkT')
        gwp = k.sb(e1, [32, 512], F32, 'gwp')
        gbr = k.sb(e1, [1, 512], F32, 'gbr')
        onesr = k.sb(e1, [1, 128], F32, 'onesr')
        k.memset(gwp[:], 0.0, ['gwp'])
        k.memset(onesr[:], 1.0, ['onesr'])
        k.dma('sp', gwp[0:16, 0:256], G['gla_gate_w'][l, 0], (), ['gwp'])
        k.dma('sp', gwp[16:32, 256:512], G['gla_gate_w'][l, 1], (), ['gwp'])
        k.dma('sp', gbr[:], G['gla_gate_b'][l].rearrange("(o a) b -> o (a b)", o=1), (), ['gbr'])
        for b in range(3):
            bs = slice(b * 512, (b + 1) * 512)
            for i in range(4):
                pb, kp = proj_fm(i * 128, 128, b)
                if i < 2:
                    k.act(qkT[:, i, bs], pb[:], AF.Identity, [kp], [("qkT", i)], scale=0.125)
                else:
                    k.cp(qkT[:, i, bs], pb[:], [kp], [('qkT', i)])
        k.barrier()
        L['load_w'](512, 544)
        CH = []
        for ci in range(2):
            sbf = lambda shp, dt, nm: k.sb(e1, shp, dt, f'{nm}{ci}')
            CH.append(dict(
                v_tm=sbf([128, 512], BF16, 'v_tm'), k_tm=sbf([128, 256], F32, 'k_tm'),
                sp=sbf([128, 512], F32, 'sp'), glrT=sbf([32, 128], F32, 'glrT'),
                Ep=sbf([128, 2, 128], F32, 'Ep'), Em=sbf([128, 2, 128], F32, 'Em'),
                qe=sbf([128, 2, 128], BF16, 'qe'), ke=sbf([128, 2, 128], BF16, 'ke'),
                keh=sbf([128, 4, 128], BF16, 'keh'), qeh=sbf([128, 4, 128], BF16, 'qeh'),
                kdc=sbf([128, 256], F32, 'kdc'), kdec=sbf([128, 256], BF16, 'kdec'),
                scT=sbf([128, 4, 128], BF16, 'scT'),
                S=[sbf([128, 128], F32, f'S{i}_') for i in range(2)],
                Sb=[sbf([128, 128], BF16, f'Sb{i}_') for i in range(2)], psn=[0]))
        k.memset(oT[:].rearrange("p a b -> p (a b)"), 0.0, ['oTgla'])

        def gla_tile(t, d, ci):
            B = CH[ci]
            K_ = lambda nm: (nm, ci)
            v_tm, k_tm, sp, glrT, Ep, Em = B['v_tm'], B['k_tm'], B['sp'], B['glrT'], B['Ep'], B['Em']
            qe, ke, keh, qeh, kdc, kdec, scT, S, Sb = (B['qe'], B['ke'], B['keh'], B['qeh'], B['kdc'],
                                                       B['kdec'], B['scT'], B['S'], B['Sb'])

            def ps():
                i = 4 * ci + B['psn'][0] % 4
                B['psn'][0] += 1
                return k.banks[i], ('ps', i)

            ts_ = slice(t * 128, (t + 1) * 128)
            kh = ('hT', t // 4)
            pbA, kpA = ps()
            pbk = pbA[:, 0:128].bitcast(BF16)
            for dt_ in range(2):
                k.tr(pbk[:, dt_ * 128:(dt_ + 1) * 128], qkT[:, 2 + dt_, ts_], identb[:],
                     [('qkT', 2 + dt_), 'identb'], [kpA], inc=(dt_ == 1))
            pbV, kpV = ps()
            for c in range(8):
                k.mm(pbV[:, 0:512], hT[:, c, ts_], W[:, c, 0:512], c == 0, c == 7, ['W', kh], [kpV])
            pbG, kpG = ps()
            for c in range(8):
                k.mm(pbG[0:32, 0:128], W[:, c, 512:544], hT[:, c, ts_], c == 0, c == 7, ['W', kh], [kpG])
            yield
            k.cp(k_tm[:], pbk[:, 0:256], [kpA], [K_('k_tm')], e='act')
            k.cp(v_tm[:], pbV[:], [kpV], [K_('v_tm')])
            k.cp(glrT[:], pbG[0:32, 0:128], [kpG], [K_('glrT')])
            yield
            pbZ, kpZ = ps()
            k.mm(pbZ[:], glrT[:], gwp[:], True, False, [K_('glrT'), 'gwp'], [kpZ])
            k.mm(pbZ[:], onesr[:], gbr[:], False, True, ['onesr', 'gbr'], [kpZ])
            yield
            k.act(sp[:], pbZ[:], AF.Exp, [kpZ], [K_('sp')], scale=-1.0)
            k.act(sp[:], sp[:], AF.Ln, [K_('sp')], [K_('sp')], bias=1.0)
            yield
            cum = C('UN16') if d == 0 else C('LN16')
            cum2 = C('SLN16') if d == 0 else C('SUN16')
            mask = G['Ub'] if d == 0 else G['Lb']
            pb, kp = ps()
            for dt_ in range(2):
                k.mm(pb[:, dt_ * 128:(dt_ + 1) * 128],
                     sp[:, d * 256 + dt_ * 128:d * 256 + (dt_ + 1) * 128], cum, True, True,
                     [K_('sp'), 'C'], [kp], inc=(dt_ == 1))
            pb2, kp2 = ps()
            k.mm(pb2[:, 0:256], cum2, sp[:, d * 256:(d + 1) * 256], True, True, [K_('sp'), 'C'], [kp2])
            yield
            k.act(Ep[:], pb[:, 0:256].rearrange("p (a b) -> p a b", a=2), AF.Exp, [kp], [K_('Ep')])
            k.act(Em[:], pb[:, 0:256].rearrange("p (a b) -> p a b", a=2), AF.Exp, [kp], [K_('Em')],
                  scale=-1.0)
            k.act(kdc[:], pb2[:, 0:256], AF.Exp, [kp2], [K_('kdc')])
            yield
            k.tt(qe[:], qkT[:, 0:2, ts_], Ep[:], ALU.mult, [('qkT', 0), ('qkT', 1), K_('Ep')], [K_('qe')])
            k.tt(ke[:], qkT[:, 2:4, ts_], Em[:], ALU.mult, [('qkT', 2), ('qkT', 3), K_('Em')], [K_('ke')])
            k.tt(kdec[:], kdc[:], k_tm[:], ALU.mult, [K_('kdc'), K_('k_tm')], [K_('kdec')])
            for hh in range(2):
                hm = C('U')[:, 63:64] if hh == 0 else C('SL')[:, 63:64]
                k.ts(keh[:, hh::2, :], ke[:], hm, None, ALU.mult, None, [K_('ke'), 'C'], [K_('keh')])
                k.ts(qeh[:, hh::2, :], qe[:], hm, None, ALU.mult, None, [K_('qe'), 'C'], [K_('qeh')])
            yield
            pb3, kp3 = ps()
            for h in range(4):
                k.mm(pb3[:, h * 128:(h + 1) * 128], keh[:, h, :], qe[:, h // 2, :], True, True,
                     [K_('keh'), K_('qe')], [kp3], inc=(h == 3))
            yield
            k.tt(scT[:], pb3[:].rearrange("p (h i) -> p h i", h=4),
                 mask[:].unsqueeze(1).broadcast_to([128, 4, 128]), ALU.mult, [kp3, 'Ub', 'Lb'],
                 [K_('scT')])
            yield
            pb4, kp4 = ps()
            for h in range(4):
                k.mm(pb4[:, h * 128:(h + 1) * 128], v_tm[:, h * 128:(h + 1) * 128], scT[:, h, :],
                     True, False, [K_('v_tm'), K_('scT')], [kp4])
                k.mm(pb4[:, h * 128:(h + 1) * 128], Sb[h // 2][:, :], qeh[:, h, :], False, True,
                     [K_('Sb%d' % (h // 2)), K_('qeh')], [kp4], inc=(h == 3))
            pb5 = []
            for pr in range(2):
                p5, k5 = ps()
                for hh in range(2):
                    h = pr * 2 + hh
                    k.mm(p5[hh * 64:(hh + 1) * 64, 0:128], kdec[:, h * 64:(h + 1) * 64],
                         v_tm[:, h * 128:(h + 1) * 128], True, True, [K_('kdec'), K_('v_tm')], [k5],
                         inc=(hh == 1))
                pb5.append((p5, k5))
            yield
            k.tt(oT[:, :, ts_], oT[:, :, ts_], pb4[:].rearrange("p (h i) -> p h i", h=4), ALU.add,
                 [kp4, 'oTgla'], ['oTgla'])
            dlcol = 127 if d == 0 else 0
            for pr in range(2):
                p5, k5 = pb5[pr]
                k.stt(S[pr][:], S[pr][:], Ep[:, pr, dlcol:dlcol + 1], p5[:, 0:128], ALU.mult, ALU.add,
                      [K_('S%d' % pr), K_('Ep'), k5], [K_('S%d' % pr)])
                k.cp(Sb[pr][:], S[pr][:], [K_('S%d' % pr)], [K_('Sb%d' % pr)], e='act')
            yield

        def chain(ci, d):
            B = CH[ci]
            for si, (t0, ln) in enumerate(SEQS):
                tiles = list(range(t0 // 128, (t0 + ln) // 128))
                for pr in range(2):
                    if si < 2:
                        k.memset(B['S'][pr][:], 0.0, [('S%d' % pr, ci)])
                    else:
                        k.dma('sp', B['S'][pr][:],
                              G['st_gla'][l, d, pr * 2:pr * 2 + 2].rearrange("h a b -> (h a) b"),
                              (), [('S%d' % pr, ci)])
                    k.cp(B['Sb'][pr][:], B['S'][pr][:], [('S%d' % pr, ci)], [('Sb%d' % pr, ci)], e='act')
                yield
                for t in (tiles if d == 0 else tiles[::-1]):
                    yield from gla_tile(t, d, ci)
                if si < 2:
                    for pr in range(2):
                        k.dma('sp',
                              G['ns_gla'][si, l, d, pr * 2:pr * 2 + 2].rearrange("h a b -> (h a) b"),
                              B['S'][pr][:], [('S%d' % pr, ci)], [('nsgla', si, l, d, pr)])

        alive = [chain(0, 0), chain(1, 1)]
        while alive:
            for g_ in list(alive):
                try:
                    next(g_)
                except StopIteration:
                    alive.remove(g_)
        k.barrier()
      dump('oT_gla', oT[:, :, 0:128], ['oTgla'])
      L['load_w'](1056, 512)
      with contextlib.ExitStack() as e3:
        L['headnorm'](e3, oT, 'oTgla', 4, G['glan'][:, l:l + 1], 0, 0)
        k.barrier()


def gdn_phase(k, l, G, L):
    C = G['C']
    catT = G['catT']
    proj_fm = L['proj_fm']
    proj_tm = L['proj_tm']
    dump = L['dump']
    identb = G['identb']
    onesb = G['onesb']
    convw = G['convw']
    QKV0, GA0, DG0 = 2592, 4128, 4144
    with contextlib.ExitStack() as e0:
        dtb = k.sb(e0, [128, 8], F32, 'dtb')
        nea = k.sb(e0, [128, 8], F32, 'nea')
        tmp8 = k.sb(e0, [128, 8], F32, 'tmp8')
        gall = k.sb(e0, [128, NT, 8], F32, 'gall')
        ball = k.sb(e0, [128, NT, 8], F32, 'ball')
        gcs = k.sb(e0, [128, NT, 8], F32, 'gcs')
        ngc = k.sb(e0, [128, NT, 8], F32, 'ngc')
        cbs = k.sb(e0, [128, NT, 8], F32, 'cbs')
        kdcs = k.sb(e0, [128, NT, 8], F32, 'kdcs')
        MK = k.sb(e0, [128, 4, 128], F32, 'MK')
        k.dma('sp', MK[:].rearrange("p a b -> p (a b)"), G['consts'][:, NCONST_P:NCONST_P + 512], (),
              ['MK'])
        k.dma('sp', dtb[:], G['gdn_dt_bias'][l].rearrange("a b -> (a b)").partition_broadcast(128),
              (), ['dtb'])
        k.dma('sp', nea[:], G['gdn_a_log'][l].rearrange("a b -> (a b)").partition_broadcast(128),
              (), ['nea'])
        k.act(nea[:], nea[:], AF.Exp, ['nea'], ['nea'])
        k.ts(nea[:], nea[:], -1.0, None, ALU.mult, None, ['nea'], ['nea'])
        with contextlib.ExitStack() as ew:
            L['alloc_w'](ew, 16)
            L['load_w'](GA0, 16, 0)
            Wg = L['WH']['W']
            hT_ = L['hT']
            pb, kp = k.ps()
            for t in range(NT):
                for c in range(8):
                    k.mm(pb[:, t * 16:(t + 1) * 16], hT_[:, c, t * 128:(t + 1) * 128], Wg[:, c, 0:16],
                         c == 0, c == 7, ['W', ('hT', t // 4)], [kp])
            pv = pb[:, 0:192].rearrange("p (t c) -> p t c", c=16)
            tmpA = k.sb(ew, [128, NT, 8], F32, 'tmpA')
            bc8 = lambda ap: ap.unsqueeze(1).broadcast_to([128, NT, 8])
            k.tt(tmpA[:], pv[:, :, 0:8], bc8(dtb[:]), ALU.add, [kp, 'dtb'], ['tmpA'])
            k.act(tmpA[:], tmpA[:], AF.Exp, ['tmpA'], ['tmpA'])
            k.act(tmpA[:], tmpA[:], AF.Ln, ['tmpA'], ['tmpA'], bias=1.0)
            k.tt(gall[:], tmpA[:], bc8(nea[:]), ALU.mult, ['tmpA', 'nea'], ['gall'])
            k.act(ball[:], pv[:, :, 8:16], AF.Sigmoid, [kp], ['ball'])
            pb2, kp2 = k.ps()
            for qi, (cm, lo) in enumerate(((C('U'), 0), (C('L'), 4), (C('SL'), 0), (C('SU'), 4))):
                k.mm(pb2[:, qi * 48:(qi + 1) * 48], cm, gall[:, :, lo:lo + 4], True, True, ['C', 'gall'],
                     [kp2])
            v4 = lambda ap: ap.rearrange("p (t c) -> p t c", c=4)
            k.cp(gcs[:, :, 0:4], v4(pb2[:, 0:48]), [kp2], ['gcs'])
            k.cp(gcs[:, :, 4:8], v4(pb2[:, 48:96]), [kp2], ['gcs'])
            k.act(kdcs[:, :, 0:4], v4(pb2[:, 96:144]), AF.Exp, [kp2], ['kdcs'])
            k.act(kdcs[:, :, 4:8], v4(pb2[:, 144:192]), AF.Exp, [kp2], ['kdcs'])
            k.ts(ngc[:], gcs[:], -1.0, None, ALU.mult, None, ['gcs'], ['ngc'])
            k.act(tmpA[:], gcs[:], AF.Exp, ['gcs'], ['tmpA'])
            k.tt(cbs[:], tmpA[:], ball[:], ALU.mult, ['tmpA', 'ball'], ['cbs'])
            gsT = k.sb(ew, [96, 128], F32, 'gsT')
            for qi, (src_, key_) in enumerate(((gcs, 'gcs'), (ball, 'ball'))):
                pb, kp = k.ps()
                k.tr(pb[0:96, 0:128], src_[:].rearrange("p a b -> p (a b)"), C('ident'), [key_, 'C'],
                     [kp])
                k.cp(gsT[:], pb[0:96, 0:128], [kp], ['gsT'])
                k.dma('sp', G['gsc'][l, qi], gsT[:], ['gsT'], ['gsc'])
            k.barrier()

        for hp in range(2):
            with contextlib.ExitStack() as e1:
                qkvT = k.sb(e1, [128, 2, 3, T], BF16, 'qkvT')
                oT = k.sb(e1, [128, 2, T], F32, 'oTgdn')
                with contextlib.ExitStack() as e3:
                    L['alloc_w'](e3, 512)
                    Wp = L['WH']['W']
                    hT_ = L['hT']
                    NPS = 3
                    PS_ = []
                    for i in range(NPS):
                        PS_.append(dict(raw=k.sb(e3, [128, 512], F32, f'raw{i}'),
                                        acc=k.sb(e3, [128, 512], F32, f'acc{i}'),
                                        sqb=k.sb(e3, [128, 512], BF16, f'sqb{i}'),
                                        rsb=k.sb(e3, [128, 512], F32, f'rsb{i}')))

                    def prep_item(sl, hh, w3, b):
                        h = hp * 2 + hh
                        P_ = PS_[sl]
                        raw, acc, sqb, rsb = P_['raw'], P_['acc'], P_['sqb'], P_['rsb']
                        K_ = lambda nm: (nm, sl)
                        bkA, kyA = k.banks[2 * sl], ('ps', 2 * sl)
                        bkB, kyB = k.banks[2 * sl + 1], ('ps', 2 * sl + 1)
                        bs = slice(b * 512, (b + 1) * 512)
                        for c in range(8):
                            k.mm(bkA[:, :], Wp[:, c, w3 * 128:(w3 + 1) * 128], hT_[:, c, bs], c == 0, c == 7,
                                 ['W', ('hT', b)], [kyA])
                        yield
                        if w3 == 3:
                            k.act(catT[:, 8 + h, bs], bkA[:], AF.Silu, [kyA], [('catT', 8 + h)])
                            yield
                            return
                        ct = w3 * 4 + h
                        k.cp(raw[:], bkA[:], [kyA], [K_('raw')], e='act')
                        yield
                        nseg, Ls = (2, 256) if b == 0 else (8, 64)
                        xv = raw[:].rearrange("p (s l) -> p s l", s=nseg)
                        av = acc[:].rearrange("p (s l) -> p s l", s=nseg)
                        k.ts(acc[:], raw[:], convw[:, l, 1, ct:ct + 1], None, ALU.mult, None,
                             [K_('raw'), 'convw'], [K_('acc')])
                        k.stt(av[:, :, 1:Ls], xv[:, :, 0:Ls - 1], convw[:, l, 0, ct:ct + 1], av[:, :, 1:Ls],
                              ALU.mult, ALU.add, [K_('raw'), 'convw', K_('acc')], [K_('acc')])
                        k.stt(av[:, :, 0:Ls - 1], xv[:, :, 1:Ls], convw[:, l, 2, ct:ct + 1],
                              av[:, :, 0:Ls - 1], ALU.mult, ALU.add, [K_('raw'), 'convw', K_('acc')],
                              [K_('acc')])
                        yield
                        if w3 == 2:
                            k.act(qkvT[:, hh, 2, bs], acc[:], AF.Silu, [K_('acc')], ['qkvT'])
                            yield
                            return
                        k.act(acc[:], acc[:], AF.Silu, [K_('acc')], [K_('acc')])
                        k.act(sqb[:], acc[:], AF.Square, [K_('acc')], [K_('sqb')])
                        yield
                        k.mm(bkB[:], onesb[:], sqb[:], True, True, [K_('sqb'), 'onesb'], [kyB])
                        yield
                        k.act(rsb[:], bkB[:], AF.Ln, [kyB], [K_('rsb')], bias=EPS)
                        k.act(rsb[:], rsb[:], AF.Exp, [K_('rsb')], [K_('rsb')], scale=-0.5)
                        yield
                        sc = 128 ** -0.5 if w3 == 0 else 1.0
                        k.stt(qkvT[:, hh, w3, bs], acc[:], sc, rsb[:], ALU.mult, ALU.mult,
                              [K_('acc'), K_('rsb')], ['qkvT'])
                        yield

                    for hh in range(2):
                        h = hp * 2 + hh
                        L['load_w'](QKV0 + h * 128, 128, 0)
                        L['load_w'](QKV0 + 512 + h * 128, 128, 128)
                        L['load_w'](QKV0 + 1024 + h * 128, 128, 256)
                        L['load_w'](DG0 + h * 128, 128, 384)
                        items = [(w3, b) for w3 in range(4) for b in range(3)]
                        free = list(range(NPS))
                        running = []
                        while items or running:
                            while items and free:
                                sl = free.pop(0)
                                w3, b = items.pop(0)
                                running.append((sl, prep_item(sl, hh, w3, b)))
                            for (sl, g_) in list(running):
                                try:
                                    next(g_)
                                except StopIteration:
                                    running.remove((sl, g_))
                                    free.append(sl)
                        k.barrier()
                with contextlib.ExitStack() as e2:
                    kq = ['qkvT']
                    NSLOT = 6
                    SL = []
                    for sl in range(NSLOT):
                        bf = lambda nm, n=128: k.sb(e2, [128, n], BF16, f'{nm}{sl}')
                        f3 = lambda nm, n=128: k.sb(e2, [128, n], F32, f'{nm}{sl}')
                        dgA = f3('dgA', 256)
                        decA = f3('decA', 128)
                        rowsA = f3('rowsA', 256)
                        rowsB = f3('rowsB', 256)
                        dgb = dgA[:].bitcast(BF16)
                        decb = decA[:].bitcast(BF16)
                        SL.append(dict(
                            kv_tm=bf('kv_tm', 256), vb=bf('vb'), kbg=bf('kbg'), kd=bf('kd'),
                            dg=dgA, erow=f3('erow'), decT=decA, BS=f3('BS'),
                            t1=f3('t1'), APt=bf('APt', 256), TP=bf('TP', 384),
                            PT=[bf('PTa'), bf('PTb')], PT0=bf('PT0f'),
                            TiT=dgb[:, 0:128], BTm=dgb[:, 128:256], Yb=dgb[:, 256:384],
                            wTn=dgb[:, 384:512], vnew=decb[:, 0:128], qg=decb[:, 128:256],
                            St=f3('St'), Sb=bf('Sbg'), rows=[rowsA, rowsB]))
                    k.memset(oT[:].rearrange("p a b -> p (a b)"), 0.0, ['oTgdn'])

                    def load_rows(sl, t, d, hh, par):
                        dh = d * 4 + hp * 2 + hh
                        rb = SL[sl]['rows'][par]
                        k.dma('pool', rb[:, 0:128], G['gsc'][l, 0, t * 8 + dh, :].partition_broadcast(128),
                              ['gsc'], [('rows', sl, par)])
                        k.dma('pool', rb[:, 128:256], G['gsc'][l, 1, t * 8 + dh, :].partition_broadcast(128),
                              ['gsc'], [('rows', sl, par)])

                    def gdn_tile(t, d, sl, hh, par, nxt):
                        h = hp * 2 + hh
                        B = SL[sl]
                        K_ = lambda nm: (nm, sl)
                        kv_tm, vb, kbg, kd, dg = B['kv_tm'], B['vb'], B['kbg'], B['kd'], B['dg']
                        erow, decT, BS, t1 = B['erow'], B['decT'], B['BS'], B['t1']
                        APt = B['APt']
                        AT, P0 = APt[:, 0:128], APt[:, 128:256]
                        TP = B['TP']
                        Ti = TP[:, 128:256]
                        P = [TP[:, 0:128], TP[:, 256:384]]
                        PT, PT0, TiT, BTm, Yb = B['PT'], B['PT0'], B['TiT'], B['BTm'], B['Yb']
                        wTn, vnew, qg, St, Sb = B['wTn'], B['vnew'], B['qg'], B['St'], B['Sb']
                        kDG, kDE = K_('dg'), K_('decT')
                        ts_ = slice(t * 128, (t + 1) * 128)
                        dh = d * 4 + h
                        NEG = C('NEGf') if d == 0 else C('NEGb')
                        strict = C('SU') if d == 0 else C('SL')
                        bk, ky = k.banks[sl], ('ps', sl)
                        qT_, kT_, vT_ = qkvT[:, hh, 0, ts_], qkvT[:, hh, 1, ts_], qkvT[:, hh, 2, ts_]
                        pbk = bk[:, 0:128].bitcast(BF16)
                        k.tr(pbk[:, 0:128], kT_, identb[:], kq + ['identb'], [ky], inc=False)
                        k.tr(pbk[:, 128:256], vT_, identb[:], kq + ['identb'], [ky], inc=False)
                        k.mm(bk[:, 128:384], kT_, qkvT[:, hh, 0:2, ts_], True, True, kq, [ky])
                        grow, brow = B['rows'][par][:, 0:128], B['rows'][par][:, 128:256]
                        kRW = ('rows', sl, par)
                        if nxt is not None:
                            load_rows(sl, nxt[0], nxt[1], hh, 1 - par)
                        yield
                        k.cp(kv_tm[:], pbk[:, 0:256], [ky], [K_('kv_tm')], e='act')
                        k.act(erow[:], grow, AF.Exp, [kRW], [K_('erow')])
                        k.tt(t1[:], grow, NEG, ALU.add, [kRW, 'C'], [K_('t1')])
                        k.stt(BS[:], brow, -1.0, strict, ALU.mult, ALU.mult, [kRW, 'C'], [K_('BS')])
                        yield
                        k.act(decT[:], t1[:], AF.Exp, [K_('t1'), 'ngc'], [kDE], bias=ngc[:, t, dh:dh + 1])
                        k.act(vb[:], kv_tm[:, 128:256], AF.Identity, [K_('kv_tm'), 'ball'], [K_('vb')],
                              scale=ball[:, t, dh:dh + 1])
                        k.act(kbg[:], kv_tm[:, 0:128], AF.Identity, [K_('kv_tm'), 'cbs'], [K_('kbg')],
                              scale=cbs[:, t, dh:dh + 1])
                        k.act(kd[:], kv_tm[:, 0:128], AF.Identity, [K_('kv_tm'), 'kdcs'], [K_('kd')],
                              scale=kdcs[:, t, dh:dh + 1])
                        yield
                        k.tt(APt[:].rearrange("p (a b) -> p a b", a=2),
                             bk[:, 128:384].rearrange("p (a b) -> p a b", a=2),
                             decT[:].unsqueeze(1).broadcast_to([128, 2, 128]), ALU.mult, [ky, kDE],
                             [K_('AT'), K_('P0')])
                        k.tt(P0, P0, BS[:], ALU.mult, [K_('P0'), K_('BS')], [K_('P0')])
                        yield
                        ptb = bk[:, 384:448].bitcast(BF16)
                        k.tr(ptb[:, 0:128], P0[:], identb[:], [K_('P0'), 'identb'], [ky])
                        k.tt(P[0][:], P0[:], MK[:, 0, :], ALU.mult, [K_('P0'), 'MK'], [K_('Pa')])
                        k.tt(Ti[:], P[0][:], identb[:], ALU.add, [K_('Pa'), 'identb'], [K_('Ti')])
                        yield
                        k.cp(PT0[:], ptb[:, 0:128], [ky], [K_('PT0')], e='act')
                        yield
                        k.tt(PT[0][:], PT0[:], MK[:, 0, :], ALU.mult, [K_('PT0'), 'MK'], [K_('PTa')])
                        yield
                        pk = ['Pa', 'Pb']
                        ptk = ['PTa', 'PTb']
                        kTP = [K_('Pa'), K_('Ti'), K_('Pb')]
                        for s_ in range(4):
                            c_, n_ = s_ % 2, (s_ + 1) % 2
                            cTi = (256, 384) if c_ == 0 else (128, 256)
                            cPP = (128, 256) if c_ == 0 else (256, 384)
                            if 1 <= s_ < 3:
                                rhs2 = TP[:, 0:256] if c_ == 0 else TP[:, 128:384]
                                k.mm(bk[:, 128:384], PT[c_][:], rhs2, True, True, [K_(ptk[c_])] + kTP, [ky],
                                     inc=False)
                            elif s_ == 0:
                                k.mm(bk[:, cPP[0]:cPP[1]], PT[c_][:], P[c_], True, True,
                                     [K_(ptk[c_]), K_(pk[c_])], [ky], inc=False)
                            else:
                                k.mm(bk[:, cTi[0]:cTi[1]], PT[c_][:], Ti, True, True,
                                     [K_(ptk[c_]), K_('Ti')], [ky])
                            if s_ < 3:
                                k.mm(bk[:, 0:128], P[c_], PT[c_][:], True, True,
                                     [K_(ptk[c_]), K_(pk[c_])], [ky])
                            yield
                            if s_ < 3:
                                k.cp(P[n_], bk[:, cPP[0]:cPP[1]], [ky], [K_(pk[n_])], e='act')
                            if s_ >= 1:
                                k.tt(Ti, Ti, bk[:, cTi[0]:cTi[1]], ALU.add, [K_('Ti'), ky], [K_('Ti')])
                            if s_ < 3:
                                k.cp(PT[n_][:], bk[:, 0:128], [ky], [K_(ptk[n_])], e='act')
                            yield
                        for lv in range(3):
                            ptb2 = bk[:, 0:64].bitcast(BF16)
                            k.tr(ptb2[:, 0:128], Ti[:], identb[:], [K_('Ti'), 'identb'], [ky])
                            k.tt(BTm, PT0[:], MK[:, 1 + lv, :], ALU.mult, [K_('PT0'), 'MK'], [kDG])
                            k.mm(bk[:, 128:256], BTm, Ti[:], True, True, [kDG, K_('Ti')], [ky])
                            yield
                            k.cp(TiT, ptb2[:, 0:128], [ky], [kDG], e='act')
                            k.cp(Yb, bk[:, 128:256], [ky], [kDG], e='act')
                            yield
                            k.mm(bk[:, 256:384], TiT, Yb, True, True, [kDG], [ky])
                            yield
                            k.tt(Ti[:], Ti[:], bk[:, 256:384], ALU.add, [K_('Ti'), ky], [K_('Ti')])
                            yield
                        k.mm(bk[:, 0:128], kbg[:], Ti[:], True, True, [K_('kbg'), K_('Ti')], [ky])
                        k.mm(bk[:, 128:256], Ti[:], vb[:], True, False, [K_('Ti'), K_('vb')], [ky], inc=True)
                        k.tt(qg, qT_, erow[:], ALU.mult, kq + [K_('erow')], [kDE])
                        yield
                        k.act(wTn, bk[:, 0:128], AF.Identity, [ky], [kDG], scale=-1.0)
                        yield
                        k.mm(bk[:, 128:256], wTn, Sb[:], False, True, [kDG, K_('Sbg')], [ky])
                        k.mm(bk[:, 256:384], Sb[:], qg, True, False, [K_('Sbg'), kDE], [ky], inc=True)
                        yield
                        k.cp(vnew, bk[:, 128:256], [ky], [kDE])
                        yield
                        k.mm(bk[:, 256:384], vnew, AT[:], False, True, [kDE, K_('AT')], [ky])
                        k.mm(bk[:, 384:512], kd[:], vnew, True, True, [K_('kd'), kDE], [ky])
                        yield
                        k.tt(oT[:, hh, ts_], oT[:, hh, ts_], bk[:, 256:384], ALU.add, [ky, 'oTgdn'],
                             ['oTgdn'])
                        dlc = 127 if d == 0 else 0
                        k.stt(St[:], St[:], erow[:, dlc:dlc + 1], bk[:, 384:512], ALU.mult, ALU.add,
                              [K_('St'), K_('erow'), ky], [K_('St')])
                        k.cp(Sb[:], St[:], [K_('St')], [K_('Sbg')], e='act')
                        yield

                    def chain(sl, hh, items):
                        h = hp * 2 + hh
                        St, Sb = SL[sl]['St'], SL[sl]['Sb']
                        steps = []
                        for (si, d) in items:
                            t0, ln = SEQS[si]
                            tiles = list(range(t0 // 128, (t0 + ln) // 128))
                            for t in (tiles if d == 0 else tiles[::-1]):
                                steps.append((t, d))
                        load_rows(sl, steps[0][0], steps[0][1], hh, 0)
                        idx = 0
                        for (si, d) in items:
                            t0, ln = SEQS[si]
                            ntl = ln // 128
                            if si < 2:
                                k.memset(St[:], 0.0, [('St', sl)])
                            else:
                                k.dma('sp', St[:], G['st_gdn'][l, d, h], (), [('St', sl)])
                            k.cp(Sb[:], St[:], [('St', sl)], [('Sbg', sl)], e='act')
                            yield
                            for _ in range(ntl):
                                t, d_ = steps[idx]
                                nxt = steps[idx + 1] if idx + 1 < len(steps) else None
                                yield from gdn_tile(t, d_, sl, hh, idx % 2, nxt)
                                idx += 1
                            if si < 2:
                                k.dma('sp', G['ns_gdn'][si, l, d, h], St[:], [('St', sl)],
                                      [('nsgdn', si, l, d, h)])

                    plans = [[(2, 0)], [(2, 1)], [(0, 0), (0, 1), (1, 0), (1, 1)]]
                    gens = []
                    for hh in range(2):
                        for pi, pl in enumerate(plans):
                            gens.append(chain(hh * 3 + pi, hh, pl))
                    alive = list(gens)
                    while alive:
                        for g_ in list(alive):
                            try:
                                next(g_)
                            except StopIteration:
                                alive.remove(g_)
                    k.barrier()
                if hp == 0:
                    dump('oT_gdn', oT[:, 0, 0:128], ['oTgdn'])
                with contextlib.ExitStack() as e4:
                    NS_ = 3
                    HS = [dict(sqh=k.sb(e4, [128, 512], BF16, f'sqh{i}'), rs=k.sb(e4, [128, 512], F32, f'rsh{i}'))
                          for i in range(NS_)]

                    def hn_item(sl, hh, b):
                        h = hp * 2 + hh
                        sqh, rs = HS[sl]['sqh'], HS[sl]['rs']
                        K_ = lambda nm: (nm, sl)
                        bkA, kyA = k.banks[sl], ('ps', sl)
                        bs = slice(b * 512, (b + 1) * 512)
                        k.act(sqh[:], oT[:, hh, bs], AF.Square, ['oTgdn'], [K_('sqh')])
                        yield
                        k.mm(bkA[:], onesb[:], sqh[:], True, True, [K_('sqh'), 'onesb'], [kyA])
                        yield
                        k.act(rs[:], bkA[:], AF.Ln, [kyA], [K_('rsh')], scale=1.0 / 128, bias=EPS)
                        k.act(rs[:], rs[:], AF.Exp, [K_('rsh')], [K_('rsh')], scale=-0.5)
                        yield
                        k.stt(rs[:], rs[:], G['gdnn'][:, l:l + 1], catT[:, 8 + h, bs], ALU.mult, ALU.mult,
                              [K_('rsh'), 'gdnn', ('catT', 8 + h)], [K_('rsh')])
                        k.tt(catT[:, 8 + h, bs], oT[:, hh, bs], rs[:], ALU.mult, ['oTgdn', K_('rsh')],
                             [('catT', 8 + h)])
                        yield

                    items = [(hh, b) for hh in range(2) for b in range(3)]
                    free = list(range(NS_))
                    running = []
                    while items or running:
                        while items and free:
                            sl = free.pop(0)
                            hh, b = items.pop(0)
                            running.append((sl, hn_item(sl, hh, b)))
                        for (sl, g_) in list(running):
                            try:
                                next(g_)
                            except StopIteration:
                                running.remove((sl, g_))
                                free.append(sl)
                    k.barrier()


def s5_part1(k, l, G, L):
    hT = L['hT']
    proj_fm = L['proj_fm']
    with contextlib.ExitStack() as e1:
        L['alloc_w'](e1, 1024)
        L['load_w'](1568, 1024)
        tmpb = k.sb(e1, [128, 8, 512], BF16, 'tmpb')
        for b in range(3):
            bs = slice(b * 512, (b + 1) * 512)
            for i in range(8):
                pb, kp = proj_fm(i * 128, 128, b)
                if i < 4:
                    k.cp(tmpb[:, i, :], pb[:], [kp], ['tmpb'])
                else:
                    k.act(tmpb[:, i, :], pb[:], AF.Silu, [kp], ['tmpb'])
            k.cp(hT[:, :, bs], tmpb[:], ['tmpb'], [('hT', b)])
        k.barrier()


def s5_part2(k, l, G, L):
    C = G['C']
    catT = G['catT']
    hT = L['hT']
    dump = L['dump']
    identb = G['identb']
    stg = G['stg']
    PI = math.pi
    with contextlib.ExitStack() as e0:
        SELWb = k.sb(e0, [128, 8, 240], BF16, 'SELWb')
        with contextlib.ExitStack() as et:
            selt = k.sb(et, [128, 1920], F32, 'selt')
            o_, w_ = _CL['SELW']
            k.dma('sp', selt[:], G['consts'][:, o_:o_ + w_], (), ['selt'])
            k.cp(SELWb[:].rearrange("p a b -> p (a b)"), selt[:], ['selt'], ['SELWb'])
            k.barrier()
        AKre = k.sb(e0, [128, 4, 32, 8], F32, 'AKre')
        AKim = k.sb(e0, [128, 4, 32, 8], F32, 'AKim')
        bbre = k.sb(e0, [128, 32, 16], F32, 'bbre')
        bbim = k.sb(e0, [128, 32, 16], F32, 'bbim')
        Cre = k.sb(e0, [128, 32, 16], F32, 'Cre')
        Cim = k.sb(e0, [128, 32, 16], F32, 'Cim')
        LPA = k.sb(e0, [128, 4, 32, 2], F32, 'LPA')
        LPB = k.sb(e0, [128, 4, 32, 2], F32, 'LPB')
        h0 = k.sb(e0, [128, 32, 2], F32, 'h0')
        Fin = k.sb(e0, [128, 2, 2, 32], F32, 'Fin')
        hm0 = C('U')[:, 63:64]
        hm1 = C('SL')[:, 63:64]

        def load_T(dst, src_g_dn, key):
            k.dma('sp', stg[0:32, :].rearrange("g (d n) -> g d n", d=2),
                  src_g_dn.rearrange("d g n -> g d n"), (), ['stg'])
            pb, kp = k.ps()
            k.tr(pb[:, 0:32], stg[0:32, :], C('ident')[0:32, 0:32], ['stg', 'C'], [kp])
            k.cp(dst, pb[:, 0:32], [kp], [key])

        with contextlib.ExitStack() as e1:
            sm = lambda nm: k.sb(e1, [128, 32], F32, nm)
            lamre, lamim, dtt, mre, ang = sm('lamre'), sm('lamim'), sm('dtt'), sm('mre'), sm('ang')
            a1re, a1im, den, qre, qim, t_a, t_b = (sm('a1re'), sm('a1im'), sm('den'), sm('qre'),
                                                   sm('qim'), sm('t_a'), sm('t_b'))
            load_T(lamre[:], G['s5_lam_re'][l], 'lamre')
            load_T(lamim[:], G['s5_lam_im'][l], 'lamim')
            load_T(h0[:, :, 0], G['st_s5re'][l], 'h0')
            load_T(h0[:, :, 1], G['st_s5im'][l], 'h0')
            for d in range(2):
                k.dma('sp', dtt[d * 64:(d + 1) * 64, :], G['s5_log_dt'][l, d].partition_broadcast(64),
                      (), ['dtt'])
            k.act(dtt[:], dtt[:], AF.Exp, ['dtt'], ['dtt'])
            k.tt(mre[:], lamre[:], dtt[:], ALU.mult, ['lamre', 'dtt'], ['mre'])
            k.tt(ang[:], lamim[:], dtt[:], ALU.mult, ['lamim', 'dtt'], ['ang'])
            big = lambda nm, dt=F32: k.sb(e1, [128, 4, 32, 8], dt, nm)
            EA, EM, RR, NI = big('EA'), big('EM'), big('RR'), big('NI', I32)
            o_, w_ = _CL['KTAB']
            KT = G['CS'][:, o_:o_ + w_].rearrange("p (q s) -> p q s", q=4)
            ktb = KT.unsqueeze(2).broadcast_to([128, 4, 32, 8])
            for q in range(4):
                k.tt(EA[:, q], KT[:, q].unsqueeze(1).broadcast_to([128, 32, 8]),
                     ang[:].unsqueeze(2).broadcast_to([128, 32, 8]), ALU.mult, ['C', 'ang'], ['EA'])
                k.tt(EM[:, q], KT[:, q].unsqueeze(1).broadcast_to([128, 32, 8]),
                     mre[:].unsqueeze(2).broadcast_to([128, 32, 8]), ALU.mult, ['C', 'mre'], ['EM'])
            fl = lambda t: t[:].rearrange("p a b c -> p (a b c)")
            k.ts(fl(RR), fl(EA), 1.0 / (2 * PI), None, ALU.mult, None, ['EA'], ['RR'])
            k.cp(fl(NI), fl(RR), ['RR'], ['NI'])
            k.cp(fl(RR), fl(NI), ['NI'], ['RR'])
            k.stt(fl(EA), fl(RR), -2 * PI, fl(EA), ALU.mult, ALU.add, ['RR', 'EA'], ['EA'])
            k.ts(fl(EA), fl(EA), 3.14159, -3.14159, ALU.min, ALU.max, ['EA'], ['EA'])
            k.act(fl(EM), fl(EM), AF.Exp, ['EM'], ['EM'])
            k.act(fl(RR), fl(EA), AF.Sin, ['EA'], ['RR'])
            k.tt(fl(AKim), fl(RR), fl(EM), ALU.mult, ['RR', 'EM'], ['AK'])
            k.act(fl(EA), fl(EA), AF.Abs, ['EA'], ['EA'])
            k.act(fl(RR), fl(EA), AF.Sin, ['EA'], ['RR'], scale=-1.0, bias=PI / 2)
            k.tt(fl(AKre), fl(RR), fl(EM), ALU.mult, ['RR', 'EM'], ['AK'])
            for (r0, s1, s8) in ((0, 0, 7), (64, 7, 0)):
                rs_ = slice(r0, r0 + 64)
                k.cp(a1re[rs_, :], AKre[rs_, 1, :, s1], ['AK'], ['a1re'])
                k.cp(a1im[rs_, :], AKim[rs_, 1, :, s1], ['AK'], ['a1im'])
                k.cp(LPA[rs_, 0, :, 0], AKre[rs_, 1, :, s8], ['AK'], ['LPA'])
                k.cp(LPB[rs_, 0, :, 1], AKim[rs_, 1, :, s8], ['AK'], ['LPB'])
            for q_ in range(4):
                if q_ > 0:
                    k.tt(t_a[:], LPA[:, q_ - 1, :, 0], LPA[:, q_ - 1, :, 0], ALU.mult, ['LPA'], ['t_a'])
                    k.tt(t_b[:], LPB[:, q_ - 1, :, 1], LPB[:, q_ - 1, :, 1], ALU.mult, ['LPB'], ['t_b'])
                    k.tt(LPA[:, q_, :, 0], t_a[:], t_b[:], ALU.subtract, ['t_a', 't_b'], ['LPA'])
                    k.tt(t_a[:], LPA[:, q_ - 1, :, 0], LPB[:, q_ - 1, :, 1], ALU.mult, ['LPA', 'LPB'], ['t_a'])
                    k.ts(LPB[:, q_, :, 1], t_a[:], 2.0, None, ALU.mult, None, ['t_a'], ['LPB'])
                k.cp(LPA[:, q_, :, 1], LPA[:, q_, :, 0], ['LPA'], ['LPA'])
                k.ts(LPB[:, q_, :, 0], LPB[:, q_, :, 1], -1.0, None, ALU.mult, None, ['LPB'], ['LPB'])
            k.ts(a1re[:], a1re[:], -1.0, None, ALU.add, None, ['a1re'], ['a1re'])
            k.tt(den[:], lamre[:], lamre[:], ALU.mult, ['lamre'], ['den'])
            k.tt(t_a[:], lamim[:], lamim[:], ALU.mult, ['lamim'], ['t_a'])
            k.tt(den[:], den[:], t_a[:], ALU.add, ['den', 't_a'], ['den'])
            k.recip(den[:], den[:], ['den'], ['den'])
            k.tt(t_a[:], a1re[:], lamre[:], ALU.mult, ['a1re', 'lamre'], ['t_a'])
            k.tt(t_b[:], a1im[:], lamim[:], ALU.mult, ['a1im', 'lamim'], ['t_b'])
            k.tt(t_a[:], t_a[:], t_b[:], ALU.add, ['t_a', 't_b'], ['t_a'])
            k.tt(qre[:], t_a[:], den[:], ALU.mult, ['t_a', 'den'], ['qre'])
            k.tt(t_a[:], a1im[:], lamre[:], ALU.mult, ['a1im', 'lamre'], ['t_a'])
            k.tt(t_b[:], a1re[:], lamim[:], ALU.mult, ['a1re', 'lamim'], ['t_b'])
            k.tt(t_a[:], t_a[:], t_b[:], ALU.subtract, ['t_a', 't_b'], ['t_a'])
            k.tt(qim[:], t_a[:], den[:], ALU.mult, ['t_a', 'den'], ['qim'])
            Bre = k.sb(e1, [128, 32, 16], F32, 'Bre')
            Bim = k.sb(e1, [128, 32, 16], F32, 'Bim')
            tb1 = k.sb(e1, [128, 32, 16], F32, 'tb1')
            for d in range(2):
                k.dma('sp', Bre[d * 64:(d + 1) * 64], G['s5_b_re'][l].rearrange("g n h -> n g h"), (),
                      ['Bre'])
                k.dma('sp', Bim[d * 64:(d + 1) * 64], G['s5_b_im'][l].rearrange("g n h -> n g h"), (),
                      ['Bim'])
            qrb = qre[:].unsqueeze(2).broadcast_to([128, 32, 16])
            qib = qim[:].unsqueeze(2).broadcast_to([128, 32, 16])
            k.tt(bbre[:], Bre[:], qrb, ALU.mult, ['Bre', 'qre'], ['bbre'])
            k.tt(tb1[:], Bim[:], qib, ALU.mult, ['Bim', 'qim'], ['tb1'])
            k.tt(bbre[:], bbre[:], tb1[:], ALU.subtract, ['bbre', 'tb1'], ['bbre'])
            k.tt(bbim[:], Bim[:], qrb, ALU.mult, ['Bim', 'qre'], ['bbim'])
            k.tt(tb1[:], Bre[:], qib, ALU.mult, ['Bre', 'qim'], ['tb1'])
            k.tt(bbim[:], bbim[:], tb1[:], ALU.add, ['bbim', 'tb1'], ['bbim'])
            cst = k.sb(e1, [128, 128], F32, 'cst')
            for (src, dst, key) in ((G['s5_c_re'], Cre, 'Cre'), (G['s5_c_im'], Cim, 'Cim')):
                for cc in range(4):
                    k.dma('sp', cst[:].rearrange("a (d n) -> a d n", d=2),
                          src[l].rearrange("d g h n -> (g h) d n")[cc * 128:(cc + 1) * 128], (),
                          ['cst'])
                    pb, kp = k.ps()
                    k.tr(pb[:, 0:128], cst[:], C('ident'), ['cst', 'C'], [kp])
                    k.cp(dst[:, cc * 8:(cc + 1) * 8, :].rearrange("p g h -> p (g h)"), pb[:, 0:128],
                         [kp], [key])
            k.barrier()

        agen = None
        if l + 1 < DEPTH:
            wabp = [k.sb(e0, [128, 8, 128], F32, f'wabp{i}') for i in range(2)]
            agen = G['ada_gen'](l + 1, wabp)

        def tick():
            nonlocal agen
            if agen is not None:
                try:
                    next(agen)
                except StopIteration:
                    agen = None

        NSL = 198
        QB = [0, 34, 68]
        for ct in range(4):
            g0 = ct * 8
            gs_ = slice(g0, g0 + 8)
            with contextlib.ExitStack() as e1:
                wt = lambda nm: k.sb(e1, [128, 8, 128], BF16, nm)
                W1re, W1im, Wcre, Wcim, Wtp = wt('W1re'), wt('W1im'), wt('Wcre'), wt('Wcim'), wt('Wtp')
                with contextlib.ExitStack() as e2:
                    V = [k.sb(e2, [128, 8, 8, 16], F32, f'V{i}') for i in range(6)]
                    f2 = lambda t: t[:].rearrange("p a b c -> p a (b c)")

                    def cmul(outre, outim, q, X_re, X_im, xdim, neg_im=False, tmp=None):
                        are = AKre[:, q, gs_, :].unsqueeze(3).broadcast_to([128, 8, 8, 16])
                        aim = AKim[:, q, gs_, :].unsqueeze(3).broadcast_to([128, 8, 8, 16])
                        xre = X_re[:, gs_, :].unsqueeze(2).broadcast_to([128, 8, 8, 16])
                        xim = X_im[:, gs_, :].unsqueeze(2).broadcast_to([128, 8, 8, 16])
                        rk = ['AK', 'bbre', 'bbim', 'Cre', 'Cim']
                        k.tt(outre, are, xre, ALU.mult, rk, ['Vo'])
                        k.tt(tmp[:], aim, xim, ALU.mult, rk, ['Vt'])
                        k.tt(outre, outre, tmp[:], ALU.subtract, ['Vo', 'Vt'], ['Vo'])
                        k.tt(outim, are, xim, ALU.mult, rk, ['Vo2'])
                        k.tt(tmp[:], aim, xre, ALU.mult, rk, ['Vt'])
                        if neg_im:
                            k.stt(outim, outim, -1.0, tmp[:], ALU.mult, ALU.subtract,
                                  ['Vo2', 'Vt'], ['Vo2'])
                        else:
                            k.tt(outim, outim, tmp[:], ALU.add, ['Vo2', 'Vt'], ['Vo2'])

                    cmul(V[0][:], V[1][:], 0, bbre, bbim, 16, tmp=V[5])
                    for ri, (Vs, Wd) in enumerate(((V[0], W1re), (V[1], W1im))):
                        for half in range(2):
                            pb, kp = k.ps()
                            for gi in range(4):
                                g = half * 4 + gi
                                k.tr(pb[:, gi * 128:(gi + 1) * 128],
                                     Vs[:, g].rearrange("p s h -> p (s h)"), C('ident'),
                                     ['Vo', 'Vo2', 'C'], [kp])
                            k.cp(Wd[:, half * 4:half * 4 + 4, :].rearrange("p g m -> p (g m)"), pb[:],
                                 [kp], ['W1'], e='act' if half else 'dve')
                    k.barrier()
                    cmul(Wcre[:].rearrange("p g (s h) -> p g s h", s=8),
                         Wcim[:].rearrange("p g (s h) -> p g s h", s=8), 1, Cre, Cim, 16,
                         neg_im=True, tmp=V[5])
                    k.barrier()
                    cmul(V[0][:], V[1][:], 2, bbre, bbim, 16, tmp=V[5])
                    k.barrier()
                    cmul(V[2][:], V[3][:], 3, Cre, Cim, 16, neg_im=True, tmp=V[5])
                    k.barrier()
                    tF = k.sb(e2, [128, 512], F32, 'tF')
                    tB = k.sb(e2, [128, 512], F32, 'tB')
                    for half in range(2):
                        gsl = slice(half * 4, half * 4 + 4)
                        pF, kF = k.ps()
                        pB, kB = k.ps()
                        for (hm, pp, kk_) in ((hm0, pF, kF), (hm1, pB, kB)):
                            k.ts(f2(V[4]), f2(V[0]), hm, None, ALU.mult, None, ['Vo', 'Vo2', 'C'], ['V4'])
                            k.ts(f2(V[5]), f2(V[1]), hm, None, ALU.mult, None, ['Vo', 'Vo2', 'C'], ['V5'])
                            for gi in range(4):
                                g = half * 4 + gi
                                k.mm(pp[:, gi * 128:(gi + 1) * 128],
                                     V[4][:, g].rearrange("p s h -> p (s h)"),
                                     V[2][:, g].rearrange("p s h -> p (s h)"), True, False,
                                     ['V4', 'Vo', 'Vo2'], [kk_])
                                k.mm(pp[:, gi * 128:(gi + 1) * 128],
                                     V[5][:, g].rearrange("p s h -> p (s h)"),
                                     V[3][:, g].rearrange("p s h -> p (s h)"), False, True,
                                     ['V5', 'Vo', 'Vo2'], [kk_])
                        k.tt(tF[:].rearrange("p (g m) -> p g m", g=4),
                             pF[:].rearrange("p (g m) -> p g m", g=4),
                             C('TMF').unsqueeze(1).broadcast_to([128, 4, 128]), ALU.mult, [kF, 'C'],
                             ['tF'])
                        k.tt(tB[:].rearrange("p (g m) -> p g m", g=4),
                             pB[:].rearrange("p (g m) -> p g m", g=4),
                             C('TMB').unsqueeze(1).broadcast_to([128, 4, 128]), ALU.mult, [kB, 'C'],
                             ['tB'])
                        k.tt(Wtp[:, gsl, :].rearrange("p g m -> p (g m)"), tF[:], tB[:], ALU.add,
                             ['tF', 'tB'], ['Wtp'])
                    k.barrier()

                with contextlib.ExitStack() as e3:
                    Ublk = k.sb(e3, [128, 8, 192], BF16, 'Ublk')
                    ZP = k.sb(e3, [128, 16, 41, 2], F32, 'ZP')
                    ZS = k.sb(e3, [128, 8, 137, 2], F32, 'ZS')
                    T1 = k.sb(e3, [128, 16, 32, 2], F32, 'T1')
                    T2 = k.sb(e3, [128, 16, 32, 2], F32, 'T2')
                    LAP = k.sb(e3, [128, 4, 16, 2], F32, 'LAP')
                    LBP = k.sb(e3, [128, 4, 16, 2], F32, 'LBP')
                    Xbre = k.sb(e3, [128, 8, 192], BF16, 'Xbre')
                    Xbim = k.sb(e3, [128, 8, 192], BF16, 'Xbim')
                    Ysb = k.sb(e3, [128, 8, 192], BF16, 'Ysb')
                    ytmp = k.sb(e3, [128, T], F32, 'ytmp')
                    ga = k.sb(e3, [128, 512], F32, 'ga')
                    gb_ = k.sb(e3, [128, 512], F32, 'gb_')
                    suP = hT[:, ct, 0:512].rearrange("p (b s) -> p b s", s=8)
                    suS = hT[:, ct, 512:T].rearrange("p (rh s c) -> p c rh s", rh=2, s=8, c=64)
                    for gp_ in range(4):
                        pb, kp = k.ps()
                        for g2_ in range(2):
                            gi = gp_ * 2 + g2_
                            o_ = g2_ * 192
                            for s_ in range(8):
                                k.mm(pb[:, o_:o_ + 64], SELWb[:, gi, 112 - 16 * s_:240 - 16 * s_],
                                     suP[:, :, s_], s_ == 0, s_ == 7, ['SELWb', ('hT', 0)], [kp])
                            for s_ in range(8):
                                k.mm(pb[:, o_ + 64:o_ + 192], SELWb[:, gi, 112 - 16 * s_:240 - 16 * s_],
                                     suS[:, :, :, s_], s_ == 0, s_ == 7,
                                     ['SELWb', ('hT', 1), ('hT', 2)], [kp])
                        k.cp(Ublk[:, gp_ * 2:gp_ * 2 + 2, :],
                             pb[:, 0:384].rearrange("p (g c) -> p g c", g=2), [kp], ['Ublk'], e='act')
                    k.memset(ZP[:].rearrange("p a b c -> p (a b c)"), 0.0, ['Zb'])
                    k.memset(ZS[:].rearrange("p a b c -> p (a b c)"), 0.0, ['Zb'])
                    for kk_ in range(4):
                        for sq_ in range(2):
                            k.cp(LAP[:, kk_, sq_ * 8:(sq_ + 1) * 8, :], LPA[:, kk_, gs_, :], ['LPA'], ['LAP'])
                            k.cp(LBP[:, kk_, sq_ * 8:(sq_ + 1) * 8, :], LPB[:, kk_, gs_, :], ['LPB'], ['LAP'])
                    for gp in range(4):
                        pR, kR = k.ps()
                        pI, kI = k.ps()
                        for gi in range(2):
                            g = gp * 2 + gi
                            k.mm(pR[:, gi * 192:(gi + 1) * 192], W1re[:, g, :], Ublk[:, g, :], True, True,
                                 ['W1', 'Ublk'], [kR])
                            k.mm(pI[:, gi * 192:(gi + 1) * 192], W1im[:, g, :], Ublk[:, g, :], True, True,
                                 ['W1', 'Ublk'], [kI])
                        for ri, (pp, kk_) in enumerate(((pR, kR), (pI, kI))):
                            src = pp[:, 0:384].rearrange("p (g c) -> p g c", g=2)
                            eng = 'act'
                            g2 = slice(gp * 2, gp * 2 + 2)
                            ZPg = ZP[:].rearrange("p (s g) q r -> p g s q r", s=2)
                            k.cp(ZPg[0:64, g2, :, 9:41, ri],
                                 src[0:64, :, 0:64].rearrange("p g (s b) -> p g s b", s=2), [kk_], ['Zb'],
                                 e=eng)
                            k.cp(ZS[0:64, g2, 9:137, ri], src[0:64, :, 64:192], [kk_], ['Zb'], e=eng)
                            k.cp(ZPg[64:128, g2, :, 9:41, ri],
                                 src[64:128, :, 0:64].rearrange("p g (s b) -> p g s b", s=2)[:, :, :, ::-1],
                                 [kk_], ['Zb'], e=eng)
                            k.cp(ZS[64:128, g2, 9:137, ri], src[64:128, :, 64:192][:, :, ::-1], [kk_],
                                 ['Zb'], e=eng)
                    k.cp(ZS[:, :, 8, :], h0[:, gs_, :], ['h0', 'Zb'], ['Zb'])
                    CH = 32

                    def cadd(Zdst, Zsrc, la, lb, nsg, n):
                        tick()
                        t1 = T1[:].rearrange("p a b c -> p (a b c)")[:, 0:nsg * n * 2].rearrange(
                            "p (a b c) -> p a b c", a=nsg, b=n)
                        t2 = T2[:].rearrange("p a b c -> p (a b c)")[:, 0:nsg * n * 2].rearrange(
                            "p (a b c) -> p a b c", a=nsg, b=n)
                        k.tt(t1, Zsrc, la.unsqueeze(2).broadcast_to([128, nsg, n, 2]), ALU.mult,
                             ['Zb', 'LAP', 'LPA'], ['T1'])
                        k.tt(t2, Zsrc[:, :, :, ::-1], lb.unsqueeze(2).broadcast_to([128, nsg, n, 2]),
                             ALU.mult, ['Zb', 'LAP', 'LPB'], ['T2'])
                        k.tt(t1, t1, t2, ALU.add, ['T1', 'T2'], ['T1'])
                        k.tt(Zdst, Zdst, t1, ALU.add, ['Zb', 'T1'], ['Zb'])

                    for (Zr, nsg, Ltot, nel, LAx, LBx, CHx) in (
                            (ZP, 16, 41, 33, lambda q: LAP[:, q], lambda q: LBP[:, q], 32),
                            (ZS, 8, 137, 129, lambda q: LPA[:, q, gs_, :], lambda q: LPB[:, q, gs_, :], 64)):
                        for q_, dsh in enumerate((1, 2, 4)):
                            starts = list(range(dsh, Ltot, CHx))
                            for lo in reversed(starts):
                                hi = min(lo + CHx, Ltot)
                                cadd(Zr[:, :, lo:hi, :], Zr[:, :, lo - dsh:hi - dsh, :], LAx(q_), LBx(q_),
                                     nsg, hi - lo)
                        k.memset(Zr[:, :, 0:8, :], 0.0, ['Zb'])
                        nst = (nel + 7) // 8 if DEBUG.get('s5_steps', 1) else 0
                        for j in range(nst):
                            s0 = 8 + 8 * j
                            n = min(8, 8 + nel - s0)
                            cadd(Zr[:, :, s0:s0 + n, :], Zr[:, :, s0 - 8:s0 - 8 + n, :], LAx(3), LBx(3),
                                 nsg, n)
                    for si in range(2):
                        for ri in range(2):
                            k.cp(Fin[:, si, ri, gs_], ZP[:, si * 8:(si + 1) * 8, 40, ri], ['Zb'], ['Fin'])
                    ZPv = ZP[:].rearrange("p (s g) q r -> p g s q r", s=2)
                    for ri, Xd in enumerate((Xbre, Xbim)):
                        eng = 'act'
                        k.cp(Xd[0:64, :, 0:64].rearrange("p g (s b) -> p g s b", s=2),
                             ZPv[0:64, :, :, 8:40, ri], ['Zb'], ['Xb'], e=eng)
                        k.cp(Xd[0:64, :, 64:192], ZS[0:64, :, 8:136, ri], ['Zb'], ['Xb'], e=eng)
                        k.cp(Xd[64:128, :, 0:64].rearrange("p g (s b) -> p g s b", s=2),
                             ZPv[64:128, :, :, 8:40, ri][:, :, :, ::-1], ['Zb'], ['Xb'], e=eng)
                        k.cp(Xd[64:128, :, 64:192], ZS[64:128, :, 8:136, ri][:, :, ::-1], ['Zb'], ['Xb'],
                             e=eng)
                    for gp_ in range(4):
                        pb, kp = k.ps()
                        for g2_ in range(2):
                            gi = gp_ * 2 + g2_
                            po_ = pb[:, g2_ * 192:(g2_ + 1) * 192]
                            k.mm(po_, Wtp[:, gi, :], Ublk[:, gi, :], True, False, ['Wtp', 'Ublk'], [kp])
                            k.mm(po_, Wcre[:, gi, :], Xbre[:, gi, :], False, False, ['Vo', 'Xb'], [kp])
                            k.mm(po_, Wcim[:, gi, :], Xbim[:, gi, :], False, True, ['Vo2', 'Xb'], [kp])
                        k.cp(Ysb[:, gp_ * 2:gp_ * 2 + 2, :],
                             pb[:, 0:384].rearrange("p (g c) -> p g c", g=2), [kp], ['Ysb'], e='act')
                    yP = ytmp[:, 0:512].rearrange("p (b s) -> p b s", s=8)
                    yS = ytmp[:, 512:T].rearrange("p (rh s c) -> p c rh s", rh=2, s=8, c=64)
                    dcol = G['s5dT'][:, l, ct:ct + 1]
                    for tau in range(8):
                        pb, kp = k.ps()
                        for gi in range(8):
                            k.mm(pb[:, 0:192], SELWb[:, tau, 112 - 16 * gi:240 - 16 * gi], Ysb[:, gi, :],
                                 gi == 0, gi == 7, ['SELWb', 'Ysb'], [kp])
                        k.stt(yP[:, :, tau], suP[:, :, tau], dcol, pb[:, 0:64], ALU.mult, ALU.add,
                              [('hT', 0), 's5dT', kp], ['ytmp'])
                        k.stt(yS[:, :, :, tau], suS[:, :, :, tau], dcol,
                              pb[:, 64:192].rearrange("p (c rh) -> p c rh", rh=2), ALU.mult, ALU.add,
                              [('hT', 1), ('hT', 2), 's5dT', kp], ['ytmp'])
                    if ct == 0:
                        dump('s5_y', ytmp[:, 0:128], ['ytmp'])
                        dump('s5_ys', ytmp[:, 512:640], ['ytmp'])
                    for b in range(3):
                        bs = slice(b * 512, (b + 1) * 512)
                        k.act(ga[:], ytmp[:, bs], AF.Square, ['ytmp'], ['ga'])
                        k.act(ga[:], ga[:], AF.Identity, ['ga'], ['ga'], scale=0.044715, bias=1.0)
                        k.tt(ga[:], ga[:], ytmp[:, bs], ALU.mult, ['ga', 'ytmp'], ['ga'])
                        k.act(gb_[:], ga[:], AF.Sigmoid, ['ga'], ['gb_'], scale=1.5957691216057308)
                        k.tt(catT[:, 4 + ct, bs], gb_[:], ytmp[:, bs], ALU.mult, ['gb_', 'ytmp'],
                             [('catT', 4 + ct)])
                    k.barrier()

        while agen is not None:
            tick()
        for si in range(2):
            for ri, dst in enumerate((G['ns_s5re'], G['ns_s5im'])):
                pb, kp = k.ps()
                k.tr(pb[0:32, 0:128], Fin[:, si, ri, :], C('ident'), ['Fin', 'C'], [kp])
                k.cp(stg[0:32, :], pb[0:32, 0:128], [kp], ['stg'])
                k.dma('sp', dst[si, l].rearrange("d g n -> g d n"),
                      stg[0:32, :].rearrange("g (d n) -> g d n", d=2), ['stg'], [('nss5', si, ri)])
        with contextlib.ExitStack() as e1:
            gw = k.sb(e1, [128, 4, 512], BF16, 'gluw')
            gws = k.sb(e1, [128, 4, 512], F32, 'gluws')
            k.dma('sp', gws[:], G['s5_glu_w'][l].rearrange("(c p) f -> p c f", p=128), (), ['gluws'])
            k.cp(gw[:], gws[:], ['gluws'], ['gluw'])
            sig = k.sb(e1, [128, 4, 512], F32, 'sig')
            kc = [('catT', 4 + i) for i in range(4)]
            for b in range(3):
                bs = slice(b * 512, (b + 1) * 512)
                for ft in range(4):
                    pb, kp = k.ps()
                    for c4 in range(4):
                        k.mm(pb[:], gw[:, c4, ft * 128:(ft + 1) * 128], catT[:, 4 + c4, bs], c4 == 0,
                             c4 == 3, ['gluw'] + kc, [kp])
                    k.act(sig[:, ft, :], pb[:], AF.Sigmoid, [kp, 'glub'], ['sig'],
                          bias=G['glub'][:, l, ft:ft + 1])
                k.tt(sig[:], sig[:], hT[:, 4:8, bs], ALU.mult, ['sig', ('hT', b)], ['sig'])
                k.tt(catT[:, 4:8, bs], catT[:, 4:8, bs], sig[:], ALU.mult, ['sig'] + kc, kc)
            k.barrier()


_NC_CACHE = {}


def _get_nc(dbg=None):
    key = tuple(sorted((dbg or {}).items(), key=lambda kv: kv[0])) if dbg else ()
    key = str(key) + str(sorted(DEBUG.items()))
    if key not in _NC_CACHE:
        _NC_CACHE[key] = build(dbg)
    return _NC_CACHE[key]


def kernel(x_prompt, x_sample, c, state_gla, state_s5_re, state_s5_im, state_gdn, c_ctx,
           norm_pre, norm_post, w_ada, b_ada, w_in, gla_gate_w, gla_gate_b, gla_norm,
           s5_lam_re, s5_lam_im, s5_log_dt, s5_b_re, s5_b_im, s5_c_re, s5_c_im, s5_d,
           s5_glu_w, s5_glu_b, gdn_conv, gdn_a_log, gdn_dt_bias, gdn_norm, w_out, _dbg=None):
    f = lambda a: np.ascontiguousarray(np.asarray(a, dtype=np.float32))
    x_prompt, x_sample, c, c_ctx = f(x_prompt), f(x_sample), f(c), f(c_ctx)
    shared = dict(norm_pre=f(norm_pre), norm_post=f(norm_post), w_ada=f(w_ada), b_ada=f(b_ada),
                  w_in=f(w_in), gla_gate_w=f(gla_gate_w), gla_gate_b=f(gla_gate_b),
                  gla_norm=f(gla_norm), s5_lam_re=f(s5_lam_re), s5_lam_im=f(s5_lam_im),
                  s5_log_dt=f(s5_log_dt), s5_b_re=f(s5_b_re), s5_b_im=f(s5_b_im),
                  s5_c_re=f(s5_c_re), s5_c_im=f(s5_c_im), s5_d=f(s5_d), s5_glu_w=f(s5_glu_w),
                  s5_glu_b=f(s5_glu_b), gdn_conv=f(gdn_conv), gdn_a_log=f(gdn_a_log),
                  gdn_dt_bias=f(gdn_dt_bias), gdn_norm=f(gdn_norm), w_out=f(w_out),
                  consts=CONSTS)
    state_gla, state_s5_re, state_s5_im, state_gdn = f(state_gla), f(state_s5_re), f(state_s5_im), f(state_gdn)
    in_maps = []
    for i in range(8):
        m = dict(shared)
        m['xin'] = np.ascontiguousarray(np.concatenate(
            [x_prompt[2 * i], x_prompt[2 * i + 1], x_sample[i]], axis=0))
        m['cond'] = np.ascontiguousarray(np.stack([c_ctx, c[i]], axis=0))
        m['st_gla'] = state_gla[i]
        m['st_s5re'] = state_s5_re[i]
        m['st_s5im'] = state_s5_im[i]
        m['st_gdn'] = state_gdn[i]
        in_maps.append(m)
    nc = _get_nc(_dbg)
    res = run_bass_kernel_spmd(nc, in_maps, core_ids=list(range(8)))
    R = res.results
    y_prompt = np.stack([R[i // 2]['y'][(i % 2) * 256:(i % 2 + 1) * 256] for i in range(16)], 0)
    y_sample = np.stack([R[i]['y'][512:] for i in range(8)], 0)
    cat = lambda n: np.concatenate([R[i][n] for i in range(8)], axis=0)
    outs = (y_prompt.astype(np.float32), y_sample.astype(np.float32), cat('ns_gla'),
            cat('ns_s5re'), cat('ns_s5im'), cat('ns_gdn'))
    if _dbg:
        return outs, {n: [R[i]['dbg_' + n] for i in range(8)] for n in _dbg}
    return outs
```
